# Optimizing a Trainium2 kernel written in Bass

```python
import math
import jax
import jax.numpy as jnp
from jax import lax
import numpy as np

D_MODEL = 2048
BATCH = 2
SEQ = 4096
DEPTH = 4

GRID_W = 64
CTX_LEN = 256
N_MOD = 9
FFN_DIM = 5632
SSD_HEADS = 32
SSD_HEAD_DIM = 64
SSD_WIDTH = SSD_HEADS * SSD_HEAD_DIM
SSD_GROUPS = 8
SSD_STATE = 128
SSD_CONV = 3
SSD_CHUNK = 128
SSD_XBC = SSD_WIDTH + 2 * SSD_GROUPS * SSD_STATE
CM_WIDTH = 2048
CM_CONV = 31
NA_HEADS = 16
NA_HEAD_DIM = 128
NA_WIDTH = NA_HEADS * NA_HEAD_DIM
NA_ROWS = 8
NA_COLS = 16
SC_WIDTH = 2048
SC_CONV = 3
EVEN_IN = SSD_WIDTH + SSD_XBC + 2 * SSD_HEADS + 2 * CM_WIDTH
EVEN_OUT = SSD_WIDTH + CM_WIDTH
ODD_IN = 3 * NA_WIDTH + 3 * SC_WIDTH
ODD_OUT = NA_WIDTH + SC_WIDTH
EPS = 1e-6

kernel_name = 'hybrid_ssd_conformer_natten_shortconv_dit'


def _offsets(sizes):
    out, acc = [], 0
    for s in sizes[:-1]:
        acc += s
        out.append(acc)
    return out


def _rms(x):
    x32 = x.astype(jnp.float32)
    return (x32 * lax.rsqrt(jnp.mean(x32 * x32, axis=-1, keepdims=True) + EPS)).astype(x.dtype)


def rmsnorm(x, g):
    return _rms(x) * g.astype(x.dtype)


def layernorm(x, g, b):
    x32 = x.astype(jnp.float32)
    mu = jnp.mean(x32, axis=-1, keepdims=True)
    xc = x32 - mu
    y = xc * lax.rsqrt(jnp.mean(xc * xc, axis=-1, keepdims=True) + EPS)
    return (y * g.astype(jnp.float32) + b.astype(jnp.float32)).astype(x.dtype)


def adaln(h, g, shift, scale):
    return rmsnorm(h, g) * (1 + scale) + shift


def dwconv(x, w, b=None):
    k, ch = w.shape
    y = lax.conv_general_dilated(x, w[:, None, :].astype(x.dtype), window_strides=(1,),
                                 padding=[((k - 1) // 2, k // 2)],
                                 dimension_numbers=('NWC', 'WIO', 'NWC'), feature_group_count=ch)
    return y if b is None else y + b.astype(y.dtype)


def swiglu(h, w_in, w_out):
    a, g = jnp.split(h @ w_in, 2, axis=-1)
    return (jax.nn.silu(a) * g) @ w_out


def ffn_half(h, m, g, w_in, w_out, j):
    y = swiglu(adaln(h, g, m[:, :, 3 * j], m[:, :, 3 * j + 1]), w_in, w_out)
    return h + 0.5 * m[:, :, 3 * j + 2] * y


def ssd_scan(xs, dt, a_neg, bm, cm, h0, with_y=True):
    bsz, seq, nh, hp = xs.shape
    ng, ns = bm.shape[2], bm.shape[3]
    nr = nh // ng
    nc = seq // SSD_CHUNK
    xs_c = xs.reshape(bsz, nc, SSD_CHUNK, ng, nr, hp)
    dt_c = dt.reshape(bsz, nc, SSD_CHUNK, ng, nr)
    b_c = bm.reshape(bsz, nc, SSD_CHUNK, ng, ns)
    c_c = cm.reshape(bsz, nc, SSD_CHUNK, ng, ns)
    a_cum = jnp.cumsum(dt_c * a_neg.reshape(ng, nr), axis=2)
    a_last = a_cum[:, :, -1]
    w_state = jnp.exp(a_last[:, :, None] - a_cum) * dt_c
    states = jnp.einsum('bcqgn,bcqgr,bcqgrp->bcgrpn', b_c, w_state, xs_c)

    def step(h, inp):
        s, dec = inp
        return h * dec[..., None, None] + s, h

    h_last, h_prev = lax.scan(step, h0, (jnp.moveaxis(states, 1, 0), jnp.moveaxis(jnp.exp(a_last), 1, 0)))
    if not with_y:
        return None, h_last
    h_prev = jnp.moveaxis(h_prev, 0, 1)
    causal = jnp.tril(jnp.ones((SSD_CHUNK, SSD_CHUNK), dtype=bool))[:, :, None, None]
    seg = a_cum[:, :, :, None] - a_cum[:, :, None]
    decay = jnp.exp(jnp.where(causal, seg, -jnp.inf))
    cb = jnp.einsum('bcqgn,bckgn->bcqkg', c_c, b_c).astype(jnp.float32)
    mix = cb[..., None] * decay * dt_c[:, :, None]
    y_diag = jnp.einsum('bcqkgr,bckgrp->bcqgrp', mix, xs_c)
    y_off = jnp.einsum('bcqgn,bcgrpn->bcqgrp', c_c, h_prev) * jnp.exp(a_cum)[..., None]
    y = (y_diag + y_off).reshape(bsz, seq, nh, hp).astype(xs.dtype)
    return y, h_last


def _even_prep(h, w_in, conv_w, conv_b, dt_bias):
    bsz, seq, _ = h.shape
    z, xbc, dtf, dtb, ga, gg = jnp.split(
        h @ w_in, _offsets([SSD_WIDTH, SSD_XBC, SSD_HEADS, SSD_HEADS, CM_WIDTH, CM_WIDTH]), axis=-1)
    xbc = jax.nn.silu(dwconv(xbc, conv_w, conv_b))
    xs, bm, cm = jnp.split(xbc, _offsets([SSD_WIDTH, SSD_GROUPS * SSD_STATE, SSD_GROUPS * SSD_STATE]), axis=-1)
    xs = xs.reshape(bsz, seq, SSD_HEADS, SSD_HEAD_DIM)
    bm = bm.reshape(bsz, seq, SSD_GROUPS, SSD_STATE)
    cm = cm.reshape(bsz, seq, SSD_GROUPS, SSD_STATE)
    dtf = jax.nn.softplus(dtf.astype(jnp.float32) + dt_bias[0].astype(jnp.float32))
    dtb = jax.nn.softplus(dtb.astype(jnp.float32) + dt_bias[1].astype(jnp.float32))
    u = ga * jax.nn.sigmoid(gg)
    return z, xs, bm, cm, dtf, dtb, u


def _even_out(z, y, xs, u, d_skip, norm_g, cm_w, cm_b, cm_g, cm_beta, w_out):
    bsz, seq = z.shape[:2]
    y = y + d_skip[:, None].astype(y.dtype) * xs
    y = y.reshape(bsz, seq, SSD_WIDTH) * jax.nn.silu(z)
    y = _rms(y.reshape(bsz, seq, SSD_GROUPS, SSD_WIDTH // SSD_GROUPS)).reshape(bsz, seq, SSD_WIDTH) * norm_g
    v = jax.nn.silu(layernorm(dwconv(u, cm_w, cm_b), cm_g, cm_beta))
    return jnp.concatenate([y, v.astype(y.dtype)], axis=-1) @ w_out


def mixer_even(hx, hc, w_in, w_out, conv_w, conv_b, dt_bias, a_log, d_skip, norm_g,
               cm_w, cm_b, cm_g, cm_beta, need_ctx):
    zx, xsx, bx, cmx, dfx, dbx, ux = _even_prep(hx, w_in, conv_w, conv_b, dt_bias)
    zc, xsc, bc, cmc, dfc, dbc, uc = _even_prep(hc, w_in, conv_w, conv_b, dt_bias)
    a_neg = -jnp.exp(a_log.astype(jnp.float32))
    h0 = jnp.zeros((hx.shape[0], SSD_GROUPS, SSD_HEADS // SSD_GROUPS, SSD_HEAD_DIM, SSD_STATE), jnp.float32)
    rev = lambda t: t[:, ::-1]
    yfc, hf = ssd_scan(xsc, dfc, a_neg[0], bc, cmc, h0, need_ctx)
    yfx, _ = ssd_scan(xsx, dfx, a_neg[0], bx, cmx, hf)
    ybc, hb = ssd_scan(rev(xsc), rev(dbc), a_neg[1], rev(bc), rev(cmc), h0, need_ctx)
    ybx, _ = ssd_scan(rev(xsx), rev(dbx), a_neg[1], rev(bx), rev(cmx), hb)
    out_x = _even_out(zx, yfx + rev(ybx), xsx, ux, d_skip, norm_g, cm_w, cm_b, cm_g, cm_beta, w_out)
    if not need_ctx:
        return out_x, None
    out_c = _even_out(zc, yfc + rev(ybc), xsc, uc, d_skip, norm_g, cm_w, cm_b, cm_g, cm_beta, w_out)
    return out_x, out_c


def neighbourhood_attention(q, k, v, k_ctx, v_ctx, rpb):
    bsz, seq, nh, hd = q.shape
    rows = seq // GRID_W
    kr = min(NA_ROWS, rows)
    nk = kr * GRID_W
    col = jnp.arange(GRID_W)
    c0 = jnp.clip(col - NA_COLS // 2, 0, GRID_W - NA_COLS)
    col_ok = (col[None, :] >= c0[:, None]) & (col[None, :] < c0[:, None] + NA_COLS)
    mask = jnp.tile(col_ok, (1, kr))
    col_idx = jnp.clip(col[None, :] - col[:, None] + NA_COLS - 1, 0, 2 * NA_COLS - 2)
    col_bias = rpb[:, :, col_idx].astype(jnp.float32)
    q_rows = jnp.moveaxis(q.reshape(bsz, rows, GRID_W, nh, hd), 1, 0)

    def one_row(args):
        r, q_r = args
        r0 = jnp.clip(r - kr // 2, 0, rows - kr)
        k_w = lax.dynamic_slice_in_dim(k, r0 * GRID_W, nk, axis=1)
        v_w = lax.dynamic_slice_in_dim(v, r0 * GRID_W, nk, axis=1)
        bias = col_bias[:, r0 + jnp.arange(kr) - r + NA_ROWS - 1]
        bias = jnp.transpose(bias, (0, 2, 1, 3)).reshape(nh, GRID_W, nk)
        s_lat = jnp.einsum('bqhd,bkhd->bhqk', q_r, k_w).astype(jnp.float32) + bias
        s_lat = jnp.where(mask, s_lat, -jnp.inf)
        s_ctx = jnp.einsum('bqhd,bkhd->bhqk', q_r, k_ctx).astype(jnp.float32)
        p = jax.nn.softmax(jnp.concatenate([s_lat, s_ctx], axis=-1), axis=-1).astype(v.dtype)
        return (jnp.einsum('bhqk,bkhd->bqhd', p[..., :nk], v_w)
                + jnp.einsum('bhqk,bkhd->bqhd', p[..., nk:], v_ctx))

    o = lax.map(one_row, (jnp.arange(rows), q_rows))
    return jnp.moveaxis(o, 0, 1).reshape(bsz, seq, nh, hd)


def context_attention(q, k, v):
    s = jnp.einsum('bqhd,bkhd->bhqk', q, k).astype(jnp.float32)
    p = jax.nn.softmax(s, axis=-1).astype(v.dtype)
    return jnp.einsum('bhqk,bkhd->bqhd', p, v)


def _odd_prep(h, w_in, q_g, k_g):
    bsz, seq, _ = h.shape
    q, k, v, gb, gc, hs = jnp.split(h @ w_in, _offsets([NA_WIDTH] * 3 + [SC_WIDTH] * 3), axis=-1)
    heads = (bsz, seq, NA_HEADS, NA_HEAD_DIM)
    q = rmsnorm(q.reshape(heads), q_g) * (NA_HEAD_DIM ** -0.5)
    k = rmsnorm(k.reshape(heads), k_g)
    return q, k, v.reshape(heads), gb, gc, hs


def mixer_odd(hx, hc, w_in, w_out, q_g, k_g, rpb, sc_w, need_ctx):
    qx, kx, vx, bx, gx, sx = _odd_prep(hx, w_in, q_g, k_g)
    qc, kc, vc, bc, gc, sc = _odd_prep(hc, w_in, q_g, k_g)
    bsz, seq = hx.shape[:2]
    ox = neighbourhood_attention(qx, kx, vx, kc, vc, rpb).reshape(bsz, seq, NA_WIDTH)
    yx = bx * dwconv(gx * sx, sc_w)
    out_x = jnp.concatenate([ox, yx], axis=-1) @ w_out
    if not need_ctx:
        return out_x, None
    oc = context_attention(qc, kc, vc).reshape(bsz, hc.shape[1], NA_WIDTH)
    yc = bc * dwconv(gc * sc, sc_w)
    out_c = jnp.concatenate([oc, yc], axis=-1) @ w_out
    return out_x, out_c


def setup_inputs(seed: int = 0) -> dict:
    key = jax.random.key(seed)
    ks = iter(jax.random.split(key, 40))
    nrm = lambda shape, s: jax.random.normal(next(ks), shape, jnp.float32) * s
    n_even = (DEPTH + 1) // 2
    n_odd = DEPTH // 2
    dt0 = jnp.exp(jax.random.uniform(next(ks), (n_even, 2, SSD_HEADS), jnp.float32,
                                     minval=math.log(1e-3), maxval=math.log(1e-1)))
    dt_bias = dt0 + jnp.log(-jnp.expm1(-dt0))
    a_log = jnp.log(jax.random.uniform(next(ks), (n_even, 2, SSD_HEADS), jnp.float32, minval=1.0, maxval=16.0))
    return {
        'x': nrm((BATCH, SEQ, D_MODEL), 1.0),
        'c': nrm((BATCH, D_MODEL), 1.0),
        'ctx': nrm((BATCH, CTX_LEN, D_MODEL), 1.0),
        'c_ctx': nrm((D_MODEL,), 1.0),
        'w_mod': nrm((DEPTH, D_MODEL, N_MOD * D_MODEL), 0.5 * D_MODEL ** -0.5),
        'b_mod': nrm((DEPTH, N_MOD * D_MODEL), 0.02),
        'norm_g': 1.0 + nrm((DEPTH, 3, D_MODEL), 0.02),
        'ffn_w_in': nrm((DEPTH, 2, D_MODEL, 2 * FFN_DIM), D_MODEL ** -0.5),
        'ffn_w_out': nrm((DEPTH, 2, FFN_DIM, D_MODEL), FFN_DIM ** -0.5),
        'ev_w_in': nrm((n_even, D_MODEL, EVEN_IN), D_MODEL ** -0.5),
        'ev_w_out': nrm((n_even, EVEN_OUT, D_MODEL), EVEN_OUT ** -0.5),
        'ssd_conv_w': nrm((n_even, SSD_CONV, SSD_XBC), SSD_CONV ** -0.5),
        'ssd_conv_b': nrm((n_even, SSD_XBC), 0.02),
        'ssd_dt_bias': dt_bias,
        'ssd_a_log': a_log,
        'ssd_d': 1.0 + nrm((n_even, SSD_HEADS), 0.1),
        'ssd_norm_g': 1.0 + nrm((n_even, SSD_WIDTH), 0.02),
        'cm_conv_w': nrm((n_even, CM_CONV, CM_WIDTH), CM_CONV ** -0.5),
        'cm_conv_b': nrm((n_even, CM_WIDTH), 0.02),
        'cm_ln_g': 1.0 + nrm((n_even, CM_WIDTH), 0.02),
        'cm_ln_b': nrm((n_even, CM_WIDTH), 0.02),
        'od_w_in': nrm((n_odd, D_MODEL, ODD_IN), D_MODEL ** -0.5),
        'od_w_out': nrm((n_odd, ODD_OUT, D_MODEL), ODD_OUT ** -0.5),
        'na_q_g': 1.0 + nrm((n_odd, NA_HEAD_DIM), 0.02),
        'na_k_g': 1.0 + nrm((n_odd, NA_HEAD_DIM), 0.02),
        'na_rpb': nrm((n_odd, NA_HEADS, 2 * NA_ROWS - 1, 2 * NA_COLS - 1), 0.05),
        'sc_conv_w': nrm((n_odd, SC_CONV, SC_WIDTH), SC_CONV ** -0.5),
    }


def reference(x, c, ctx, c_ctx, w_mod, b_mod, norm_g, ffn_w_in, ffn_w_out, ev_w_in, ev_w_out,
              ssd_conv_w, ssd_conv_b, ssd_dt_bias, ssd_a_log, ssd_d, ssd_norm_g,
              cm_conv_w, cm_conv_b, cm_ln_g, cm_ln_b, od_w_in, od_w_out, na_q_g, na_k_g, na_rpb, sc_conv_w):
    bsz = x.shape[0]
    cx = ctx
    for i in range(DEPTH):
        last = i == DEPTH - 1
        mx = (jax.nn.silu(c) @ w_mod[i] + b_mod[i]).reshape(bsz, 1, N_MOD, D_MODEL)
        mc = (jax.nn.silu(c_ctx) @ w_mod[i] + b_mod[i]).reshape(1, 1, N_MOD, D_MODEL)
        x = ffn_half(x, mx, norm_g[i, 0], ffn_w_in[i, 0], ffn_w_out[i, 0], 0)
        cx = ffn_half(cx, mc, norm_g[i, 0], ffn_w_in[i, 0], ffn_w_out[i, 0], 0)
        hx = adaln(x, norm_g[i, 1], mx[:, :, 3], mx[:, :, 4])
        hc = adaln(cx, norm_g[i, 1], mc[:, :, 3], mc[:, :, 4])
        j = i // 2
        if i % 2 == 0:
            ox, oc = mixer_even(hx, hc, ev_w_in[j], ev_w_out[j], ssd_conv_w[j], ssd_conv_b[j], ssd_dt_bias[j],
                                ssd_a_log[j], ssd_d[j], ssd_norm_g[j], cm_conv_w[j], cm_conv_b[j],
                                cm_ln_g[j], cm_ln_b[j], not last)
        else:
            ox, oc = mixer_odd(hx, hc, od_w_in[j], od_w_out[j], na_q_g[j], na_k_g[j], na_rpb[j],
                               sc_conv_w[j], not last)
        x = x + mx[:, :, 5] * ox
        x = ffn_half(x, mx, norm_g[i, 2], ffn_w_in[i, 1], ffn_w_out[i, 1], 2)
        if not last:
            cx = cx + mc[:, :, 5] * oc
            cx = ffn_half(cx, mc, norm_g[i, 2], ffn_w_in[i, 1], ffn_w_out[i, 1], 2)
    return x
```

```python
import numpy as np
from contextlib import ExitStack
import concourse.bass as bass
import concourse.mybir as mybir
from concourse.bass_utils import run_bass_kernel_spmd

F32, BF16 = mybir.dt.float32, mybir.dt.bfloat16
AF = mybir.ActivationFunctionType
ALU = mybir.AluOpType
ENG = ('pe', 'act', 'dve', 'pool', 'sp')


def _flat(w):
    out = []
    for x in w:
        if x is None:
            continue
        if isinstance(x, list):
            out.extend(_flat(x))
        else:
            out.append(x)
    return out


class Prog:
    def __init__(s, nc):
        s.nc = nc
        s.q = {e: [] for e in ENG}
        s.cnt = {e: 0 for e in ENG}
        s.sem = {}
        s.waited = {e: {} for e in ENG}
        s.stack = ExitStack()
        for e in ENG:
            s.sem[e] = s.stack.enter_context(nc.semaphore("es_" + e))
        s.nds = 0
        s.ps = s.stack.enter_context(nc.psum_tensor("ps", [128, 8, 512], F32))
        s.bank_free = [[] for _ in range(8)]
        s.bank_i = 0

    def sb(s, name, shape, dt):
        return s.stack.enter_context(s.nc.sbuf_tensor(name, shape, dt))

    def dsem(s, name=None):
        s.nds += 1
        key = "ds%d" % s.nds
        s.sem[key] = s.stack.enter_context(s.nc.semaphore(key))
        s.cnt[key] = 0
        return key

    def _w(s, eng, waits):
        mx = {}
        for (k, v) in _flat(list(waits)):
            if v > mx.get(k, 0):
                mx[k] = v
        res = []
        for k, v in mx.items():
            if s.waited[eng].get(k, 0) >= v:
                continue
            s.waited[eng][k] = v
            res.append((k, v))
        return res

    def op(s, eng, fn, waits=(), sig=True):
        w = s._w(eng, waits)
        ev = None
        if sig:
            s.cnt[eng] += 1
            ev = (eng, s.cnt[eng])
        s.q[eng].append((fn, w, ev, 1))
        return ev

    def dma(s, eng, out, in_, ds, waits=()):
        w = s._w(eng, waits)
        s.cnt[ds] += 16
        ev = (ds, s.cnt[ds])
        s.q[eng].append((lambda e: e.dma_start(out=out, in_=in_), w, ev, 16))
        return ev

    def bank(s):
        i = s.bank_i
        s.bank_i = (i + 1) % 8
        return i, s.bank_free[i]

    def bank_release(s, i, evs):
        s.bank_free[i] = _flat([evs])

    def run(s):
        nc = s.nc
        with nc.Block() as block:
            def mk(name):
                def f(e):
                    for (fn, w, ev, inc) in s.q[name]:
                        for (k, v) in w:
                            e.wait_ge(s.sem[k], v)
                        ins = fn(e)
                        if ev is not None:
                            ins.then_inc(s.sem[ev[0]], inc)
                return f
            block.tensor(mk('pe'))
            block.scalar(mk('act'))
            block.vector(mk('dve'))
            block.gpsimd(mk('pool'))
            block.sync(mk('sp'))


class Ring:
    def __init__(s, n):
        s.n = n
        s.i = 0
        s.free = [[] for _ in range(n)]

    def get(s):
        i = s.i
        s.i = (i + 1) % s.n
        return i, s.free[i]

    def rel(s, i, evs):
        s.free[i] = _flat([evs])


def subs_of(n, step=512):
    out = []
    t = 0
    while t < n:
        m = min(step, n - t)
        out.append((t, m))
        t += m
    return out


class Cfg:
    def __init__(s, D=2048, FF=5632, S=4096, CL=256, DEPTH=4, NB=4, GW=64):
        s.D, s.FF, s.S, s.CL, s.DEPTH, s.NB, s.GW = D, FF, S, CL, DEPTH, NB, GW
        s.DC = D // 128
        s.T = S + CL
        s.NM = 9
        s.XB = S // NB
        s.CB = CL // NB
        s.TBT = s.XB + s.CB
        s.FG = 256
        s.WSLOT = 12288 * (16 // s.DC) if s.DC < 16 else 12288
        s.WSLOT = 12288
        s.EPS = 1e-6

    def blk_subs(s):
        out = [(t, n, 0) for (t, n) in subs_of(s.XB)]
        out += [(s.XB + t, n, 1) for (t, n) in subs_of(s.CB)]
        return out


class Model:
    def __init__(M, cfg):
        M.c = c = cfg
        M.nc = nc = bass.Bass("TRN2", target_bir_lowering=False)
        M.P = P = Prog(nc)
        D, DC, T = c.D, c.DC, c.T
        M.din = {}
        M.wbuf = P.sb("wbuf", [128, 2, c.WSLOT], BF16)
        M.wring = Ring(2)
        M.wds = [P.dsem(), P.dsem()]
        M.AW = max(DC * c.TBT, 17408)
        M.arena = P.sb("arena", [128, M.AW], F32)
        M.xres = M.arena[:, 0:DC * c.TBT].rearrange("p (dc t) -> p dc t", dc=DC)
        M.hbuf = P.sb("hbuf", [128, DC, max(c.TBT + 64, 768)], BF16)
        M.actb = P.sb("actb", [128, 2, 2, max(c.TBT, 1024)], BF16)
        M.actring = Ring(2)
        M.tmp = P.sb("tmpf", [128, 6, 512], F32)
        M.tmpring = Ring(6)
        M.sil = M.tmp
        M.silring = M.tmpring
        M.sq = P.sb("sq", [128, 4, 512], BF16)
        M.sqring = Ring(4)
        M.rstd = P.sb("rstd", [128, max(c.TBT, 1024)], F32)
        M.onesD = P.sb("onesD", [128, 128], BF16)
        M.epsb = P.sb("epsb", [128, 1], F32)
        M.MT = P.sb("MT", [128, c.DEPTH, c.NM * DC, 2], F32)
        M.gsb = M.arena[:, 0:c.DEPTH * 3 * DC].rearrange("p (l f) -> p l f", l=c.DEPTH)
        M.bmsb = M.arena[:, 1024:1024 + c.DEPTH * c.NM * DC].rearrange("p (l f) -> p l f", l=c.DEPTH)
        M.csb = P.sb("csb", [128, DC, 2], F32)
        M.scb = P.sb("scb", [128, DC, 2], BF16)
        M.ld = P.dsem()
        M.xld = P.dsem()
        M.xst = P.dsem()
        M.x_ev = {}
        M.h_rd = []
        M.const_ev = []
        M.rstd_rd = {}
        M.ln_rd = []
        M.debug_scratch = False

    def inp(M, name, shape, dt=F32):
        t = M.nc.dram_tensor(name, list(shape), dt, kind="ExternalInput").ap()
        M.din[name] = t
        return t

    def outp(M, name, shape, dt=F32):
        return M.nc.dram_tensor(name, list(shape), dt, kind="ExternalOutput").ap()

    def scratch(M, name, shape, dt):
        if M.debug_scratch:
            return M.nc.dram_tensor(name, list(shape), dt, kind="ExternalOutput").ap()
        return M.nc.dram_tensor(name, list(shape), dt).ap()

    def wslot(M):
        i, fr = M.wring.get()
        return i, fr

    def load_w(M, dst, src, slot, waits):
        return M.P.dma('pool', dst, src.rearrange("(kc p) n -> p kc n", p=128), M.wds[slot], waits)

    def mod_phase(M, cT, wmod, bmodL, gL):
        c, P = M.c, M.P
        DC = c.DC
        e1 = P.dma('sp', M.csb[:], cT, M.ld)
        e2 = P.dma('sp', M.gsb[:], gL.rearrange("l p f -> p l f"), M.ld)
        e3 = P.dma('sp', M.bmsb[:], bmodL.rearrange("l p f -> p l f"), M.ld)
        ld_ev = [e1, e2, e3]
        ev_ones = P.op('pool', lambda e: e.memset(M.onesD[:], 1.0 / c.D))
        M.const_ev.append(ev_ones)
        M.const_ev.append(P.op('pool', lambda e: e.memset(M.epsb[:], c.EPS)))
        ev_sc = P.op('act', lambda e: e.activation(out=M.scb[:], in_=M.csb[:], func=AF.Silu), waits=ld_ev)
        NF = c.NM * DC
        FCS = c.WSLOT // DC // 128
        mt_evs = []
        for l in range(c.DEPTH):
            for f0 in range(0, NF, FCS):
                nf = min(FCS, NF - f0)
                slot, fr = M.wslot()
                wt = M.wbuf[:, slot, 0:DC * nf * 128].rearrange("p (kc n) -> p kc n", kc=DC)
                wev = M.load_w(wt, wmod[l, :, f0 * 128:(f0 + nf) * 128], slot, fr)
                bk, bfree = P.bank()
                last = None
                for fi in range(nf):
                    for kc in range(DC):
                        last = P.op('pe', (lambda e, fi=fi, kc=kc, wt=wt, bk=bk: e.matmul(
                            P.ps[:, bk, 2 * fi:2 * fi + 2], lhsT=wt[:, kc, fi * 128:(fi + 1) * 128],
                            rhs=M.scb[:, kc, :], start=(kc == 0), stop=(kc == DC - 1))),
                            waits=[wev, ev_sc, bfree], sig=(fi == nf - 1 and kc == DC - 1))
                M.wring.rel(slot, last)
                evs = []
                for s_ in range(2):
                    ev = P.op('dve', (lambda e, bk=bk, s_=s_, l=l, f0=f0, nf=nf: e.tensor_tensor(
                        out=M.MT[:, l, f0:f0 + nf, s_],
                        in0=P.ps[:, bk, 0:2 * nf].rearrange("p (f s) -> p f s", s=2)[:, :, s_],
                        in1=M.bmsb[:, l, f0:f0 + nf], op=ALU.add)), waits=[last, ld_ev])
                    evs.append(ev)
                P.bank_release(bk, evs)
                mt_evs += evs
            for j in range(3):
                for s_ in range(2):
                    ev = P.op('dve', (lambda e, l=l, j=j, s_=s_: e.scalar_tensor_tensor(
                        out=M.MT[:, l, (3 * j + 1) * DC:(3 * j + 2) * DC, s_],
                        in0=M.MT[:, l, (3 * j + 1) * DC:(3 * j + 2) * DC, s_], scalar=1.0,
                        in1=M.gsb[:, l, j * DC:(j + 1) * DC], op0=ALU.add, op1=ALU.mult)), waits=mt_evs)
                    mt_evs.append(ev)
                    if j != 1:
                        ev = P.op('dve', (lambda e, l=l, j=j, s_=s_: e.tensor_scalar(
                            out=M.MT[:, l, (3 * j + 2) * DC:(3 * j + 3) * DC, s_],
                            in0=M.MT[:, l, (3 * j + 2) * DC:(3 * j + 3) * DC, s_], scalar1=0.5, scalar2=None,
                            op0=ALU.mult)), waits=mt_evs)
                        mt_evs.append(ev)
        M.mt_ev = mt_evs[-8:] + [mt_evs[-1]]
        M.mt_ev = [mt_evs[-1]]

    def mvec(M, l, m, dc, s_):
        DC = M.c.DC
        return M.MT[:, l, m * DC + dc, s_:s_ + 1]

    def load_x(M, src, b, waits=()):
        c, P = M.c, M.P
        fr = _flat([list(M.x_ev.values()), list(waits), M.mt_ev])
        sv = src.rearrange("(dc p) t -> p dc t", p=128)
        e1 = P.dma('sp', M.xres[:, :, 0:c.XB], sv[:, :, b * c.XB:(b + 1) * c.XB], M.xld, fr)
        e2 = P.dma('sp', M.xres[:, :, c.XB:c.TBT], sv[:, :, c.S + b * c.CB:c.S + (b + 1) * c.CB], M.xld, fr)
        for si, _ in enumerate(c.blk_subs()):
            for dc in range(c.DC):
                M.x_ev[(dc, si)] = [e1, e2]

    def store_x(M, dst, b, with_ctx=True, xs_only_cols=None):
        c, P = M.c, M.P
        evs = _flat([list(M.x_ev.values())])
        dv = dst.rearrange("(dc p) t -> p dc t", p=128)
        out = [P.dma('sp', dv[:, :, b * c.XB:(b + 1) * c.XB], M.xres[:, :, 0:c.XB], M.xst, evs)]
        if with_ctx:
            out.append(P.dma('sp', dv[:, :, c.S + b * c.CB:c.S + (b + 1) * c.CB], M.xres[:, :, c.XB:c.TBT], M.xst, evs))
        for k in M.x_ev:
            M.x_ev[k] = _flat([M.x_ev[k], out])
        return out

    def adaln(M, l, j):
        c, P = M.c, M.P
        DC = c.DC
        h_ev = {}
        for si, (t0, n, s_) in enumerate(c.blk_subs()):
            bk, bfree = P.bank()
            last = None
            for dc in range(DC):
                qi, qfree = M.sqring.get()
                ev = P.op('act', (lambda e, qi=qi, dc=dc, t0=t0, n=n: e.activation(
                    out=M.sq[:, qi, 0:n], in_=M.xres[:, dc, t0:t0 + n], func=AF.Square)),
                    waits=[qfree, M.x_ev[(dc, si)]])
                last = P.op('pe', (lambda e, qi=qi, dc=dc, n=n, bk=bk: e.matmul(
                    P.ps[:, bk, 0:n], lhsT=M.onesD[:], rhs=M.sq[:, qi, 0:n], start=(dc == 0), stop=(dc == DC - 1))),
                    waits=[ev, bfree, M.const_ev])
                M.sqring.rel(qi, last)
            ev_q = P.op('act', (lambda e, bk=bk, t0=t0, n=n: e.activation(
                out=M.rstd[:, t0:t0 + n], in_=P.ps[:, bk, 0:n], func=AF.Sqrt, bias=M.epsb[:, 0:1], scale=1.0)),
                waits=[last, M.rstd_rd.get(si), M.const_ev])
            P.bank_release(bk, ev_q)
            ev_r = P.op('dve', (lambda e, t0=t0, n=n: e.reciprocal(
                out=M.rstd[:, t0:t0 + n], in_=M.rstd[:, t0:t0 + n])), waits=[ev_q])
            evs = []
            for dc in range(DC):
                ti, tfree = M.silring.get()
                ev = P.op('dve', (lambda e, ti=ti, dc=dc, t0=t0, n=n, s_=s_: e.scalar_tensor_tensor(
                    out=M.sil[:, ti, 0:n], in0=M.xres[:, dc, t0:t0 + n], scalar=M.mvec(l, 3 * j + 1, dc, s_),
                    in1=M.rstd[:, t0:t0 + n], op0=ALU.mult, op1=ALU.mult)),
                    waits=[tfree, ev_r, M.x_ev[(dc, si)], M.mt_ev])
                ev2 = P.op('act', (lambda e, ti=ti, dc=dc, t0=t0, n=n, s_=s_: e.activation(
                    out=M.hbuf[:, dc, t0:t0 + n], in_=M.sil[:, ti, 0:n], func=AF.Identity,
                    bias=M.mvec(l, 3 * j, dc, s_), scale=1.0)), waits=[ev, M.h_rd])
                M.silring.rel(ti, ev2)
                evs.append(ev2)
            M.rstd_rd[si] = evs[-1:]
            M.rstd_rd[si] = [ev]
            h_ev[si] = evs
        return h_ev

    def ffn(M, l, j, w_in, w_out, h_ev):
        c, P = M.c, M.P
        DC, FF, FG = c.DC, c.FF, c.FG
        NG = FF // FG
        FJ = FG // 128
        subs = c.blk_subs()
        rd = []
        for g in range(NG):
            slot, fr = M.wslot()
            win = M.wbuf[:, slot, 0:DC * 2 * FG].rearrange("p (kc n) -> p kc n", kc=DC)
            wo = M.wbuf[:, slot, DC * 2 * FG:DC * 2 * FG + FJ * c.D].rearrange("p (kc n) -> p kc n", kc=FJ)
            wev = [M.load_w(win[:, :, 0:FG], w_in[:, g * FG:(g + 1) * FG], slot, fr),
                   M.load_w(win[:, :, FG:2 * FG], w_in[:, FF + g * FG:FF + (g + 1) * FG], slot, fr),
                   M.load_w(wo, w_out[g * FG:(g + 1) * FG, :], slot, fr)]
            ai, afree = M.actring.get()
            act_ev = {}
            last_pe = None
            for si, (t0, n, s_) in enumerate(subs):
                for jf in range(FJ):
                    ba, fa = P.bank()
                    for kc in range(DC):
                        la = P.op('pe', (lambda e, ba=ba, kc=kc, jf=jf, t0=t0, n=n, win=win: e.matmul(
                            P.ps[:, ba, 0:n], lhsT=win[:, kc, jf * 128:(jf + 1) * 128], rhs=M.hbuf[:, kc, t0:t0 + n],
                            start=(kc == 0), stop=(kc == DC - 1))), waits=[wev, fa, h_ev[si]], sig=(kc == DC - 1))
                    bg, fg_ = P.bank()
                    for kc in range(DC):
                        lg = P.op('pe', (lambda e, bg=bg, kc=kc, jf=jf, t0=t0, n=n, win=win: e.matmul(
                            P.ps[:, bg, 0:n], lhsT=win[:, kc, FG + jf * 128:FG + (jf + 1) * 128],
                            rhs=M.hbuf[:, kc, t0:t0 + n], start=(kc == 0), stop=(kc == DC - 1))),
                            waits=[fg_], sig=(kc == DC - 1))
                    ti, tfree = M.silring.get()
                    e1 = P.op('act', (lambda e, ba=ba, ti=ti, n=n: e.activation(
                        out=M.sil[:, ti, 0:n], in_=P.ps[:, ba, 0:n], func=AF.Silu)), waits=[la, tfree])
                    P.bank_release(ba, e1)
                    e2 = P.op('dve', (lambda e, bg=bg, ti=ti, n=n, ai=ai, jf=jf, t0=t0: e.tensor_tensor(
                        out=M.actb[:, ai, jf, t0:t0 + n], in0=M.sil[:, ti, 0:n], in1=P.ps[:, bg, 0:n], op=ALU.mult)),
                        waits=[lg, e1, afree])
                    P.bank_release(bg, e2)
                    M.silring.rel(ti, e2)
                    act_ev[(si, jf)] = e2
                    last_pe = lg
            rd.append(last_pe)
            arel = []
            for si, (t0, n, s_) in enumerate(subs):
                for dc in range(DC):
                    bo, fo = P.bank()
                    for jf in range(FJ):
                        lo = P.op('pe', (lambda e, bo=bo, jf=jf, dc=dc, t0=t0, n=n, wo=wo, ai=ai: e.matmul(
                            P.ps[:, bo, 0:n], lhsT=wo[:, jf, dc * 128:(dc + 1) * 128], rhs=M.actb[:, ai, jf, t0:t0 + n],
                            start=(jf == 0), stop=(jf == FJ - 1))), waits=[fo, act_ev[(si, jf)]], sig=(jf == FJ - 1))
                    ex = P.op('dve', (lambda e, bo=bo, dc=dc, t0=t0, n=n, s_=s_: e.scalar_tensor_tensor(
                        out=M.xres[:, dc, t0:t0 + n], in0=P.ps[:, bo, 0:n], scalar=M.mvec(l, 3 * j + 2, dc, s_),
                        in1=M.xres[:, dc, t0:t0 + n], op0=ALU.mult, op1=ALU.add)),
                        waits=[lo, M.x_ev[(dc, si)], M.mt_ev])
                    P.bank_release(bo, ex)
                    M.x_ev[(dc, si)] = [ex]
                    arel = lo
            M.wring.rel(slot, arel)
            M.actring.rel(ai, arel)
        M.h_rd = _flat([rd])

    def outproj(M, l, YT, w_out, b, y_wr):
        c, P = M.c, M.P
        DC = c.DC
        subs = c.blk_subs()
        yv = YT.rearrange("(kc p) t -> p kc t", p=128)
        rd = []
        for kh in range(2):
            fr = _flat([M.h_rd, rd])
            e1 = P.dma('sp', M.hbuf[:, :, 0:c.XB], yv[:, kh * DC:(kh + 1) * DC, b * c.XB:(b + 1) * c.XB], M.xld, [fr, y_wr])
            e2 = P.dma('sp', M.hbuf[:, :, c.XB:c.TBT], yv[:, kh * DC:(kh + 1) * DC, c.S + b * c.CB:c.S + (b + 1) * c.CB], M.xld, [fr, y_wr])
            yev = [e1, e2]
            NS = c.D // 512 if c.D >= 512 else 1
            CW = min(512, c.D)
            for ds in range(NS):
                slot, wfr = M.wslot()
                wt = M.wbuf[:, slot, 0:DC * CW].rearrange("p (kc n) -> p kc n", kc=DC)
                wev = M.load_w(wt, w_out[kh * c.D:(kh + 1) * c.D, ds * CW:(ds + 1) * CW], slot, wfr)
                lo = None
                for si, (t0, n, s_) in enumerate(subs):
                    for dcl in range(CW // 128):
                        dc = ds * (CW // 128) + dcl
                        bo, fo = P.bank()
                        for kc in range(DC):
                            lo = P.op('pe', (lambda e, bo=bo, kc=kc, dcl=dcl, t0=t0, n=n, wt=wt: e.matmul(
                                P.ps[:, bo, 0:n], lhsT=wt[:, kc, dcl * 128:(dcl + 1) * 128], rhs=M.hbuf[:, kc, t0:t0 + n],
                                start=(kc == 0), stop=(kc == DC - 1))), waits=[fo, wev, yev], sig=(kc == DC - 1))
                        ex = P.op('dve', (lambda e, bo=bo, dc=dc, t0=t0, n=n, s_=s_: e.scalar_tensor_tensor(
                            out=M.xres[:, dc, t0:t0 + n], in0=P.ps[:, bo, 0:n], scalar=M.mvec(l, 5, dc, s_),
                            in1=M.xres[:, dc, t0:t0 + n], op0=ALU.mult, op1=ALU.add)),
                            waits=[lo, M.x_ev[(dc, si)], M.mt_ev])
                        P.bank_release(bo, ex)
                        M.x_ev[(dc, si)] = [ex]
                M.wring.rel(slot, lo)
                rd = [lo]
        M.h_rd = rd

    def store_h(M, HT, b, h_ev, waits):
        c, P = M.c, M.P
        evs = _flat([list(h_ev.values())])
        hv = HT.rearrange("(dc p) t -> p dc t", p=128)
        o1 = P.dma('sp', hv[:, :, b * c.XB:(b + 1) * c.XB], M.hbuf[:, :, 0:c.XB], M.xst, [evs, waits])
        o2 = P.dma('sp', hv[:, :, c.S + b * c.CB:c.S + (b + 1) * c.CB], M.hbuf[:, :, c.XB:c.TBT], M.xst, [evs, waits])
        M.h_rd = _flat([M.h_rd, o1, o2])
        return [o1, o2]

    def finish(M, evs):
        M.P.q['sp'].append((None, M.P._w('sp', evs), None, 0))

    def run(M):
        P = M.P
        nc = M.nc
        with nc.Block() as block:
            def mkf(name):
                def f(e):
                    for (fn, w, ev, inc) in P.q[name]:
                        for (k, v) in w:
                            e.wait_ge(P.sem[k], v)
                        if fn is None:
                            continue
                        ins = fn(e)
                        if ev is not None:
                            ins.then_inc(P.sem[ev[0]], inc)
                return f
            block.tensor(mkf('pe'))
            block.scalar(mkf('act'))
            block.vector(mkf('dve'))
            block.gpsimd(mkf('pool'))
            block.sync(mkf('sp'))

    def mixer_setup(M):
        c, P = M.c, M.P
        M.stg = P.sb("stg", [128, 4, 512], BF16)
        M.stgring = Ring(4)
        M.stgds = [P.dsem() for _ in range(4)]
        M.onesH = P.sb("onesH", [128, 128], BF16)
        M.const_ev.append(P.op('pool', lambda e: e.memset(M.onesH[:], 1.0 / 128)))
        M.ones1 = P.sb("ones1", [128, 128], BF16)
        M.const_ev.append(P.op('pool', lambda e: e.memset(M.ones1[:], 1.0)))
        M.hmds = P.dsem()

    def stage(M):
        i, fr = M.stgring.get()
        return i, fr

    def stage_dma(M, i, dst, src_ap, ev, extra=()):
        o = M.P.dma('sp', dst, src_ap, M.stgds[i], [ev, list(extra)])
        M.stgring.rel(i, o)
        return o

    def load_hm(M, HT, b, HL, waits):
        c, P = M.c, M.P
        fr = _flat([M.h_rd, list(waits)])
        hv = HT.rearrange("(dc p) t -> p dc t", p=128)
        xo = 0
        co = c.XB + 2 * HL
        W = c.TBT + 4 * HL
        ez = P.op('pool', lambda e: e.memset(M.hbuf[:, :, 0:W], 0.0), waits=fr)
        a0 = max(0, b * c.XB - HL)
        a1 = min(c.S, (b + 1) * c.XB + HL)
        e1 = P.dma('sp', M.hbuf[:, :, xo + HL - (b * c.XB - a0): xo + HL - (b * c.XB - a0) + (a1 - a0)], hv[:, :, a0:a1], M.hmds, [ez])
        g0 = max(0, b * c.CB - HL)
        g1 = min(c.CL, (b + 1) * c.CB + HL)
        e2 = P.dma('sp', M.hbuf[:, :, co + HL - (b * c.CB - g0): co + HL - (b * c.CB - g0) + (g1 - g0)], hv[:, :, c.S + g0:c.S + g1], M.hmds, [ez])
        return [e1, e2], xo + HL, co + HL

    def odd_consts(M, qgL, kgL, scwL, n_odd):
        c, P = M.c, M.P
        M.qk_g = P.sb("qk_g", [128, 2, n_odd], F32)
        M.scw = P.sb("scw", [128, n_odd, c.DC * 3], F32)
        ds_ = P.dsem()
        e1 = P.dma('sp', M.qk_g[:, 0, :], qgL, ds_)
        e2 = P.dma('sp', M.qk_g[:, 1, :], kgL, ds_)
        e3 = P.dma('sp', M.scw[:], scwL.rearrange("j p f -> p j f"), ds_)
        ev = P.op('dve', lambda e: e.tensor_scalar(out=M.qk_g[:, 0, :], in0=M.qk_g[:, 0, :], scalar1=128 ** -0.5,
                                                   scalar2=None, op0=ALU.mult), waits=[e1, e2, e3])
        M.odd_ev = [ev, e1, e2, e3]

    def odd_in(M, jl, HT, w_in, QT, KT, VT, YT, h_wr, dst_free):
        c, P = M.c, M.P
        D, DC = c.D, c.DC
        HL = 1
        wr = []
        rd_all = []
        for b in range(c.NB):
            hev, xc0, cc0 = M.load_hm(HT, b, HL, h_wr)
            rd_all += hev
            subsA = [(xc0 + t, n, b * c.XB + t) for (t, n) in subs_of(c.XB, 510)] + \
                    [(cc0 + t, n, c.S + b * c.CB + t) for (t, n) in subs_of(c.CB, 510)]
            last_rd = None
            for fam, dst in enumerate([QT, KT]):
                CW = min(512, D)
                for sl in range(D // CW):
                    slot, wfr = M.wslot()
                    wt = M.wbuf[:, slot, 0:DC * CW].rearrange("p (kc n) -> p kc n", kc=DC)
                    wev = M.load_w(wt, w_in[:, fam * D + sl * CW: fam * D + (sl + 1) * CW], slot, wfr)
                    for hh in range(CW // 128):
                        head = sl * (CW // 128) + hh
                        for (col0, n, tok0) in subsA:
                            bq, fq = P.bank()
                            for kc in range(DC):
                                lq = P.op('pe', (lambda e, bq=bq, kc=kc, hh=hh, col0=col0, n=n, wt=wt: e.matmul(
                                    P.ps[:, bq, 0:n], lhsT=wt[:, kc, hh * 128:(hh + 1) * 128], rhs=M.hbuf[:, kc, col0:col0 + n],
                                    start=(kc == 0), stop=(kc == DC - 1))), waits=[fq, wev, hev], sig=(kc == DC - 1))
                            qi, qfree = M.sqring.get()
                            es = P.op('act', (lambda e, bq=bq, qi=qi, n=n: e.activation(
                                out=M.sq[:, qi, 0:n], in_=P.ps[:, bq, 0:n], func=AF.Square)), waits=[lq, qfree])
                            br, frr = P.bank()
                            lr = P.op('pe', (lambda e, br=br, qi=qi, n=n: e.matmul(
                                P.ps[:, br, 0:n], lhsT=M.onesH[:], rhs=M.sq[:, qi, 0:n], start=True, stop=True)),
                                waits=[es, frr, M.const_ev])
                            M.sqring.rel(qi, lr)
                            ti, tfree = M.tmpring.get()
                            e1 = P.op('act', (lambda e, br=br, ti=ti, n=n: e.activation(
                                out=M.tmp[:, ti, 0:n], in_=P.ps[:, br, 0:n], func=AF.Sqrt, bias=M.epsb[:, 0:1], scale=1.0)),
                                waits=[lr, tfree])
                            P.bank_release(br, e1)
                            e2 = P.op('dve', (lambda e, ti=ti, n=n: e.reciprocal(out=M.tmp[:, ti, 0:n], in_=M.tmp[:, ti, 0:n])),
                                      waits=[e1])
                            gi, gfree = M.stage()
                            e3 = P.op('dve', (lambda e, bq=bq, ti=ti, gi=gi, n=n, fam=fam: e.scalar_tensor_tensor(
                                out=M.stg[:, gi, 0:n], in0=P.ps[:, bq, 0:n], scalar=M.qk_g[:, fam, jl:jl + 1],
                                in1=M.tmp[:, ti, 0:n], op0=ALU.mult, op1=ALU.mult)), waits=[e2, gfree, M.odd_ev])
                            P.bank_release(bq, e3)
                            M.tmpring.rel(ti, e3)
                            wr.append(M.stage_dma(gi, dst[head * 128:(head + 1) * 128, tok0:tok0 + n], M.stg[:, gi, 0:n], e3, dst_free))
                            last_rd = lq
                    M.wring.rel(slot, last_rd)
            tiles = [(xc0 + t, n, b * c.XB + t) for (t, n) in subs_of(c.XB, 128)] + \
                    [(cc0 + t, n, c.S + b * c.CB + t) for (t, n) in subs_of(c.CB, 128)]
            CW = min(512, D)
            for sl in range(D // CW):
                slot, wfr = M.wslot()
                wt = M.wbuf[:, slot, 0:DC * CW].rearrange("p (kc n) -> p kc n", kc=DC)
                wev = M.load_w(wt, w_in[:, 2 * D + sl * CW: 2 * D + (sl + 1) * CW], slot, wfr)
                for (col0, m, tok0) in tiles:
                    bv, fv = P.bank()
                    for kc in range(DC):
                        lv = P.op('pe', (lambda e, bv=bv, kc=kc, col0=col0, m=m, wt=wt, CW=CW: e.matmul(
                            P.ps[0:m, bv, 0:CW], lhsT=M.hbuf[:, kc, col0:col0 + m], rhs=wt[:, kc, 0:CW],
                            start=(kc == 0), stop=(kc == DC - 1))), waits=[fv, wev, hev], sig=(kc == DC - 1))
                    gi, gfree = M.stage()
                    e3 = P.op('act', (lambda e, bv=bv, gi=gi, m=m, CW=CW: e.activation(
                        out=M.stg[0:m, gi, 0:CW], in_=P.ps[0:m, bv, 0:CW], func=AF.Copy)), waits=[lv, gfree])
                    P.bank_release(bv, e3)
                    wr.append(M.stage_dma(gi, VT[tok0:tok0 + m, sl * CW:(sl + 1) * CW], M.stg[0:m, gi, 0:CW], e3, dst_free))
                    last_rd = lv
                M.wring.rel(slot, last_rd)
            GW_ = 256 if D >= 256 else D
            for gq in range(D // GW_):
                slot, wfr = M.wslot()
                wt = M.wbuf[:, slot, 0:DC * 3 * GW_].rearrange("p (kc n) -> p kc n", kc=DC)
                wev = [M.load_w(wt[:, :, 0:GW_], w_in[:, 4 * D + gq * GW_:4 * D + (gq + 1) * GW_], slot, wfr),
                       M.load_w(wt[:, :, GW_:2 * GW_], w_in[:, 5 * D + gq * GW_:5 * D + (gq + 1) * GW_], slot, wfr),
                       M.load_w(wt[:, :, 2 * GW_:3 * GW_], w_in[:, 3 * D + gq * GW_:3 * D + (gq + 1) * GW_], slot, wfr)]
                for cc in range(GW_ // 128):
                    ch = gq * (GW_ // 128) + cc
                    for (col0, n, tok0) in subsA:
                        n2 = n + 2
                        banks = []
                        lasts = []
                        for fi in range(3):
                            bb, fb = P.bank()
                            lo_, nn = (col0 - 1, n2) if fi < 2 else (col0, n)
                            for kc in range(DC):
                                lx = P.op('pe', (lambda e, bb=bb, kc=kc, fi=fi, cc=cc, lo_=lo_, nn=nn, wt=wt: e.matmul(
                                    P.ps[:, bb, 0:nn], lhsT=wt[:, kc, fi * GW_ + cc * 128: fi * GW_ + (cc + 1) * 128],
                                    rhs=M.hbuf[:, kc, lo_:lo_ + nn], start=(kc == 0), stop=(kc == DC - 1))),
                                    waits=[fb, wev, hev], sig=(kc == DC - 1))
                            banks.append(bb)
                            lasts.append(lx)
                        t1, f1 = M.tmpring.get()
                        ea = P.op('act', (lambda e, b0=banks[0], t1=t1, n2=n2: e.activation(
                            out=M.tmp[:, t1, 0:n2], in_=P.ps[:, b0, 0:n2], func=AF.Copy)), waits=[lasts[0], f1])
                        P.bank_release(banks[0], ea)
                        t2, f2 = M.tmpring.get()
                        eb = P.op('dve', (lambda e, b1=banks[1], t1=t1, t2=t2, n2=n2: e.tensor_tensor(
                            out=M.tmp[:, t2, 0:n2], in0=M.tmp[:, t1, 0:n2], in1=P.ps[:, b1, 0:n2], op=ALU.mult)),
                            waits=[ea, lasts[1], f2])
                        P.bank_release(banks[1], eb)
                        M.tmpring.rel(t1, eb)
                        t3, f3 = M.tmpring.get()
                        w_ = lambda k, ch=ch: M.scw[:, jl, ch * 3 + k: ch * 3 + k + 1]
                        ec = P.op('dve', (lambda e, t2=t2, t3=t3, n=n, w_=w_: e.tensor_scalar(
                            out=M.tmp[:, t3, 0:n], in0=M.tmp[:, t2, 0:n], scalar1=w_(0), scalar2=None, op0=ALU.mult)),
                            waits=[eb, f3, M.odd_ev])
                        for k in (1, 2):
                            ec = P.op('dve', (lambda e, t2=t2, t3=t3, n=n, k=k, w_=w_: e.scalar_tensor_tensor(
                                out=M.tmp[:, t3, 0:n], in0=M.tmp[:, t2, k:k + n], scalar=w_(k), in1=M.tmp[:, t3, 0:n],
                                op0=ALU.mult, op1=ALU.add)), waits=[ec])
                        M.tmpring.rel(t2, ec)
                        gi, gfree = M.stage()
                        ed = P.op('dve', (lambda e, b2=banks[2], t3=t3, gi=gi, n=n: e.tensor_tensor(
                            out=M.stg[:, gi, 0:n], in0=M.tmp[:, t3, 0:n], in1=P.ps[:, b2, 0:n], op=ALU.mult)),
                            waits=[ec, lasts[2], gfree])
                        P.bank_release(banks[2], ed)
                        M.tmpring.rel(t3, ed)
                        wr.append(M.stage_dma(gi, YT[D + ch * 128:D + (ch + 1) * 128, tok0:tok0 + n], M.stg[:, gi, 0:n], ed, dst_free))
                        last_rd = lasts[2]
                M.wring.rel(slot, last_rd)
            M.h_rd = [last_rd]
        return wr

    def odd_att(M, jl, QT, KT, VT, YT, nabias, namask, wr_ev, y_free):
        c, P = M.c, M.P
        D, S, CL, T = c.D, c.S, c.CL, c.T
        NH = D // 128
        NT, NC = S // 128, CL // 128
        NQ = NT + NC
        BW = 5 * 5 * 128
        W2 = T // 2
        off = [0]

        def carve(words):
            a = off[0]
            off[0] += words
            return M.arena[:, a:a + words]
        ktv = [carve(W2).bitcast(BF16) for _ in range(2)]
        qtv = [carve(W2).bitcast(BF16) for _ in range(2)]
        vv = [carve(NQ * 64).bitcast(BF16).rearrange("p (t c) -> p t c", c=128) for _ in range(2)]
        otv = carve(W2).bitcast(BF16)
        sv = [carve(640).rearrange("p (i q) -> p i q", q=128) for _ in range(2)]
        ptv = [carve(64 * (5 + NC)).bitcast(BF16).rearrange("p (i q) -> p i q", q=128) for _ in range(2)]
        assert off[0] <= M.AW, off[0]
        maskc = M.wbuf[:, 0, 0:2 * BW].bitcast(F32)
        biasb = M.wbuf[:, 1, 0:2 * BW].bitcast(F32).rearrange("p (a i q) -> p a i q", a=5, i=5)
        ads_m, ads_b, ads_o = P.dsem(), P.dsem(), P.dsem()
        ads_h = [P.dsem(), P.dsem()]
        arena_free = _flat([list(M.x_ev.values())])
        wfree = _flat([M.wring.free[0], M.wring.free[1]])
        em = P.dma('sp', maskc, namask, ads_m, wfree)
        slot_rd = [[], []]
        bias_rd = []
        ot_rd = []
        sring, pring = Ring(2), Ring(2)
        last_pe = None
        outs = []
        for h in range(NH):
            hs = h % 2
            fr = _flat([slot_rd[hs], arena_free, wr_ev])
            e1 = P.dma('sp', ktv[hs], KT[h * 128:(h + 1) * 128, :], ads_h[hs], fr)
            e2 = P.dma('sp', qtv[hs], QT[h * 128:(h + 1) * 128, :], ads_h[hs], fr)
            e3 = P.dma('sp', vv[hs], VT[:, h * 128:(h + 1) * 128].rearrange("(t p) c -> p t c", p=128), ads_h[hs], fr)
            e4 = P.dma('sp', biasb.rearrange("p a i q -> p (a i q)"), nabias[jl, h], ads_b, [bias_rd, wfree])
            hev = [e1, e2, e3]
            eb = P.op('dve', lambda e: e.tensor_tensor(out=biasb.rearrange("p a i q -> p (a i q)"),
                                                        in0=biasb.rearrange("p a i q -> p (a i q)"), in1=maskc, op=ALU.add),
                      waits=[e4, em])
            kt, qt, v = ktv[hs], qtv[hs], vv[hs]
            o_evs = []
            for j in range(NQ):
                isx = j < NT
                if isx:
                    a0 = min(max(j - 2, 0), NT - 5)
                    pat = 0 if j == 0 else 1 if j == 1 else 3 if j == NT - 2 else 4 if j == NT - 1 else 2
                    ktiles = [a0 + i for i in range(5)] + [NT + i for i in range(NC)]
                else:
                    ktiles = [NT + i for i in range(NC)]
                nk = len(ktiles)
                bA, fA = P.bank()
                bB, fB = P.bank()
                lA = lB = None
                for i, kt_i in enumerate(ktiles):
                    bk_, cc_ = (bA, i) if i < 4 else (bB, i - 4)
                    ev = P.op('pe', (lambda e, bk_=bk_, cc_=cc_, kt_i=kt_i, j=j, kt=kt, qt=qt: e.matmul(
                        P.ps[:, bk_, cc_ * 128:(cc_ + 1) * 128], lhsT=kt[:, kt_i * 128:(kt_i + 1) * 128],
                        rhs=qt[:, j * 128:(j + 1) * 128], start=True, stop=True)), waits=[fA, fB, hev],
                        sig=(i == min(3, nk - 1) or i == nk - 1))
                    if i < 4:
                        lA = ev
                    else:
                        lB = ev
                pi, pfree = pring.get()
                pt = ptv[pi]
                if isx:
                    si_, sfree = sring.get()
                    sb_ = sv[si_]
                    d1 = P.op('dve', (lambda e, bA=bA, sb_=sb_, pat=pat: e.tensor_tensor(
                        out=sb_[:, 0:4, :], in0=P.ps[:, bA, 0:512].rearrange("p (i q) -> p i q", q=128),
                        in1=biasb[:, pat, 0:4, :], op=ALU.add)), waits=[lA, sfree, eb])
                    d2 = P.op('dve', (lambda e, bB=bB, sb_=sb_, pat=pat: e.tensor_tensor(
                        out=sb_[:, 4, :], in0=P.ps[:, bB, 0:128], in1=biasb[:, pat, 4, :], op=ALU.add)), waits=[lB])
                    P.bank_release(bA, d1)
                    a1 = P.op('act', (lambda e, sb_=sb_, pt=pt: e.activation(out=pt[:, 0:5, :], in_=sb_[:, :, :], func=AF.Exp)),
                              waits=[d1, d2, pfree])
                    sring.rel(si_, a1)
                    a2 = P.op('act', (lambda e, bB=bB, pt=pt: e.activation(
                        out=pt[:, 5:5 + NC, :], in_=P.ps[:, bB, 128:128 + NC * 128].rearrange("p (i q) -> p i q", q=128),
                        func=AF.Exp)), waits=[lB])
                    P.bank_release(bB, [d2, a2])
                    pev = [a1, a2]
                else:
                    a1 = P.op('act', (lambda e, bA=bA, pt=pt, nk=nk: e.activation(
                        out=pt[:, 0:nk, :], in_=P.ps[:, bA, 0:nk * 128].rearrange("p (i q) -> p i q", q=128), func=AF.Exp)),
                        waits=[lA, pfree])
                    P.bank_release(bA, a1)
                    P.bank_release(bB, [])
                    pev = [a1]
                bC, fC = P.bank()
                for part in range(2):
                    for i, kt_i in enumerate(ktiles):
                        lc = P.op('pe', (lambda e, bC=bC, i=i, kt_i=kt_i, part=part, pt=pt, v=v, nk=nk: e.matmul(
                            P.ps[:, bC, part * 128:(part + 1) * 128], lhsT=(v[:, kt_i, :] if part == 0 else M.ones1[:]),
                            rhs=pt[:, i, :], start=(i == 0), stop=(i == nk - 1))), waits=[fC, pev, M.const_ev],
                            sig=(part == 1 and i == nk - 1))
                pring.rel(pi, lc)
                ti, tfree = M.tmpring.get()
                r1 = P.op('dve', (lambda e, bC=bC, ti=ti: e.reciprocal(out=M.tmp[:, ti, 0:128], in_=P.ps[:, bC, 128:256])),
                          waits=[lc, tfree])
                r2 = P.op('dve', (lambda e, bC=bC, ti=ti, j=j: e.tensor_tensor(
                    out=otv[:, j * 128:(j + 1) * 128], in0=P.ps[:, bC, 0:128], in1=M.tmp[:, ti, 0:128], op=ALU.mult)),
                    waits=[r1, ot_rd])
                P.bank_release(bC, r2)
                M.tmpring.rel(ti, r2)
                o_evs = [r2]
                last_pe = lc
            slot_rd[hs] = [last_pe]
            bias_rd = [o_evs[-1]]
            od = P.dma('sp', YT[h * 128:(h + 1) * 128, :], otv, ads_o, [o_evs, y_free])
            ot_rd = [od]
            outs.append(od)
        for k in M.x_ev:
            M.x_ev[k] = _flat([M.x_ev[k], outs, last_pe])
        M.wring.rel(0, [o_evs[-1]])
        M.wring.rel(1, [o_evs[-1]])
        return outs

    def even_consts(M, n_even, ins):
        c, P = M.c, M.P
        DC = c.DC
        XC = (c.D + 2 * 1024) // 128
        M.e_cw = P.sb("e_cw", [128, n_even, XC * 3], F32)
        M.e_cb = P.sb("e_cb", [128, n_even, XC], F32)
        M.e_dtb = P.sb("e_dtb", [128, n_even, 64], F32)
        M.e_aneg = P.sb("e_aneg", [128, n_even, 64], F32)
        M.e_dsk = P.sb("e_dsk", [128, n_even, 32], F32)
        M.e_cmw = P.sb("e_cmw", [128, n_even, DC * 31], F32)
        M.e_cmv = P.sb("e_cmv", [128, n_even, 3, DC], F32)
        M.e_tri = P.sb("e_tri", [128, 2, 128], F32)
        M.e_mask = P.sb("e_mask", [128, 2, 128], F32)
        M.e_id = P.sb("e_id", [128, 128], F32)
        M.e_onesf = P.sb("e_onesf", [128, 128], F32)
        M.e_oh = P.sb("e_oh", [4, 4 * 128], F32)
        M.e_one = P.sb("e_one", [128, 1], F32)
        ds_ = P.dsem()
        evs = []
        for dst, src in [(M.e_cw[:], ins['cw'].rearrange("j p f -> p j f")), (M.e_cb[:], ins['cb'].rearrange("j p f -> p j f")),
                         (M.e_dtb[:], ins['dtb'].rearrange("j p f -> p j f")), (M.e_aneg[:], ins['alog'].rearrange("j p f -> p j f")),
                         (M.e_dsk[:], ins['dsk'].rearrange("j p f -> p j f")), (M.e_cmw[:], ins['cmw'].rearrange("j p f -> p j f")),
                         (M.e_cmv[:, :, 0, :], ins['cmb'].rearrange("j p f -> p j f")), (M.e_cmv[:, :, 1, :], ins['cmg'].rearrange("j p f -> p j f")),
                         (M.e_cmv[:, :, 2, :], ins['cmbeta'].rearrange("j p f -> p j f")),
                         (M.e_tri[:], ins['tri'].rearrange("d p f -> p d f")), (M.e_mask[:], ins['maskfb'].rearrange("d p f -> p d f")),
                         (M.e_id[:], ins['ident']), (M.e_oh[:], ins['onehot'])]:
            evs.append(P.dma('sp', dst, src, ds_))
        e1 = P.op('act', lambda e: e.activation(out=M.e_aneg[:], in_=M.e_aneg[:], func=AF.Exp), waits=evs)
        e2 = P.op('dve', lambda e: e.tensor_scalar(out=M.e_aneg[:], in0=M.e_aneg[:], scalar1=-1.0, scalar2=None, op0=ALU.mult), waits=[e1])
        e4 = P.op('pool', lambda e: e.memset(M.e_onesf[:], 1.0))
        e5 = P.op('pool', lambda e: e.memset(M.e_one[:], 1.0))
        M.even_ev = _flat([evs, e2, e4, e5])
        M.ng_dram = ins['ng']

    def even_in(M, jl, HT, w_in, ZS, XS, BM, BT, CT, DT, YT, h_wr, dst_free):
        c, P = M.c, M.P
        D, DC = c.D, c.DC
        HL = 15
        NXC = D // 128
        NBC = 1024 // 128
        o_xbc = D
        o_dt = D + D + 2048
        o_ga = o_dt + 64
        o_gg = o_ga + D
        wr = []
        dtst = M.actb[:, 0, 0, 0:256].bitcast(F32).rearrange("p (a f) -> p a f", a=2)
        dtring = Ring(2)
        dtds = [P.dsem(), P.dsem()]
        a_ev = _flat([list(M.x_ev.values())])
        xcb = M.arena[:, 0:4 * 512].rearrange("p (c t) -> p c t", c=4)
        cv = M.arena[:, 0:DC * c.TBT].rearrange("p (dc t) -> p dc t", dc=DC)
        for b in range(c.NB):
            hev, xc0, cc0 = M.load_hm(HT, b, HL, h_wr)
            last_rd = None
            tiles = [(xc0 + t, n, b * c.XB + t) for (t, n) in subs_of(c.XB, 128)] + \
                    [(cc0 + t, n, c.S + b * c.CB + t) for (t, n) in subs_of(c.CB, 128)]
            CW = min(512, D)
            for sl in range(D // CW):
                slot, wfr = M.wslot()
                wt = M.wbuf[:, slot, 0:DC * CW].rearrange("p (kc n) -> p kc n", kc=DC)
                wev = M.load_w(wt, w_in[:, sl * CW:(sl + 1) * CW], slot, wfr)
                for (col0, m, tok0) in tiles:
                    bv, fv = P.bank()
                    for kc in range(DC):
                        lv = P.op('pe', (lambda e, bv=bv, kc=kc, col0=col0, m=m, wt=wt, CW=CW: e.matmul(
                            P.ps[0:m, bv, 0:CW], lhsT=M.hbuf[:, kc, col0:col0 + m], rhs=wt[:, kc, 0:CW],
                            start=(kc == 0), stop=(kc == DC - 1))), waits=[fv, wev, hev], sig=(kc == DC - 1))
                    gi, gfree = M.stage()
                    e3 = P.op('act', (lambda e, bv=bv, gi=gi, m=m, CW=CW: e.activation(
                        out=M.stg[0:m, gi, 0:CW], in_=P.ps[0:m, bv, 0:CW], func=AF.Silu)), waits=[lv, gfree])
                    P.bank_release(bv, e3)
                    wr.append(M.stage_dma(gi, ZS[tok0:tok0 + m, sl * CW:(sl + 1) * CW], M.stg[0:m, gi, 0:CW], e3, dst_free))
                    last_rd = lv
                M.wring.rel(slot, last_rd)
            slot, wfr = M.wslot()
            wt = M.wbuf[:, slot, 0:DC * 64].rearrange("p (kc n) -> p kc n", kc=DC)
            wev = M.load_w(wt, w_in[:, o_dt:o_dt + 64], slot, wfr)
            for (col0, m, tok0) in tiles:
                bv, fv = P.bank()
                for kc in range(DC):
                    lv = P.op('pe', (lambda e, bv=bv, kc=kc, col0=col0, m=m, wt=wt: e.matmul(
                        P.ps[0:m, bv, 0:64], lhsT=M.hbuf[:, kc, col0:col0 + m], rhs=wt[:, kc, 0:64],
                        start=(kc == 0), stop=(kc == DC - 1))), waits=[fv, wev, hev], sig=(kc == DC - 1))
                ti, tfree = M.tmpring.get()
                d1 = P.op('dve', (lambda e, bv=bv, ti=ti, m=m: e.tensor_tensor(
                    out=M.tmp[0:m, ti, 0:64], in0=P.ps[0:m, bv, 0:64], in1=M.e_dtb[0:m, jl, :], op=ALU.add)),
                    waits=[lv, tfree, M.even_ev])
                P.bank_release(bv, d1)
                d2 = P.op('act', (lambda e, ti=ti, m=m: e.activation(out=M.tmp[0:m, ti, 0:64], in_=M.tmp[0:m, ti, 0:64], func=AF.Exp)),
                          waits=[d1])
                di, dfree = dtring.get()
                d3 = P.op('act', (lambda e, ti=ti, di=di, m=m: e.activation(
                    out=dtst[0:m, di, :], in_=M.tmp[0:m, ti, 0:64], func=AF.Ln, bias=M.e_one[0:m, 0:1], scale=1.0)),
                    waits=[d2, dfree, M.actring.free[0], M.actring.free[1]])
                M.tmpring.rel(ti, d3)
                o = P.dma('sp', DT[tok0:tok0 + m, :], dtst[0:m, di, :], dtds[di], [d3, dst_free])
                dtring.rel(di, o)
                wr.append(o)
                M.actring.free[0] = _flat([M.actring.free[0], o])
                M.actring.free[1] = _flat([M.actring.free[1], o])
                last_rd = lv
            M.wring.rel(slot, last_rd)
            subsA = [(xc0 + t, n, b * c.XB + t) for (t, n) in subs_of(c.XB, 510)] + \
                    [(cc0 + t, n, c.S + b * c.CB + t) for (t, n) in subs_of(c.CB, 510)]
            NG4 = (NXC + 2 * NBC) // 4
            for g4 in range(NG4):
                slot, wfr = M.wslot()
                wt = M.wbuf[:, slot, 0:DC * 512].rearrange("p (kc n) -> p kc n", kc=DC)
                wev = M.load_w(wt, w_in[:, o_xbc + g4 * 512:o_xbc + (g4 + 1) * 512], slot, wfr)
                for (col0, n, tok0) in subsA:
                    xc_ev = []
                    for cc in range(4):
                        ch = g4 * 4 + cc
                        bb, fb = P.bank()
                        for kc in range(DC):
                            lx = P.op('pe', (lambda e, bb=bb, kc=kc, cc=cc, col0=col0, n=n, wt=wt: e.matmul(
                                P.ps[:, bb, 0:n + 2], lhsT=wt[:, kc, cc * 128:(cc + 1) * 128],
                                rhs=M.hbuf[:, kc, col0 - 1:col0 + n + 1], start=(kc == 0), stop=(kc == DC - 1))),
                                waits=[fb, wev, hev], sig=(kc == DC - 1))
                        last_rd = lx
                        t1, f1 = M.tmpring.get()
                        ea = P.op('act', (lambda e, bb=bb, t1=t1, n=n: e.activation(
                            out=M.tmp[:, t1, 0:n + 2], in_=P.ps[:, bb, 0:n + 2], func=AF.Copy)), waits=[lx, f1])
                        P.bank_release(bb, ea)
                        t2, f2 = M.tmpring.get()
                        w_ = lambda k, ch=ch: M.e_cw[:, jl, ch * 3 + k:ch * 3 + k + 1]
                        ec = P.op('dve', (lambda e, t1=t1, t2=t2, n=n, w_=w_, ch=ch: e.tensor_scalar(
                            out=M.tmp[:, t2, 0:n], in0=M.tmp[:, t1, 0:n], scalar1=w_(0), scalar2=M.e_cb[:, jl, ch:ch + 1],
                            op0=ALU.mult, op1=ALU.add)), waits=[ea, f2, M.even_ev])
                        for k in (1, 2):
                            ec = P.op('dve', (lambda e, t1=t1, t2=t2, n=n, k=k, w_=w_: e.scalar_tensor_tensor(
                                out=M.tmp[:, t2, 0:n], in0=M.tmp[:, t1, k:k + n], scalar=w_(k), in1=M.tmp[:, t2, 0:n],
                                op0=ALU.mult, op1=ALU.add)), waits=[ec])
                        M.tmpring.rel(t1, ec)
                        es = P.op('act', (lambda e, t2=t2, cc=cc, n=n: e.activation(
                            out=xcb[:, cc, 0:n], in_=M.tmp[:, t2, 0:n], func=AF.Silu)), waits=[ec, a_ev])
                        M.tmpring.rel(t2, es)
                        xc_ev.append(es)
                        if ch >= NXC:
                            gi, gfree = M.stage()
                            ef = P.op('dve', (lambda e, gi=gi, cc=cc, n=n: e.tensor_copy(out=M.stg[:, gi, 0:n], in_=xcb[:, cc, 0:n])),
                                      waits=[es, gfree])
                            dstT = BT if ch < NXC + NBC else CT
                            r0 = (ch - NXC) * 128 if ch < NXC + NBC else (ch - NXC - NBC) * 128
                            wr.append(M.stage_dma(gi, dstT[r0:r0 + 128, tok0:tok0 + n], M.stg[:, gi, 0:n], ef, dst_free))
                            xc_ev.append(ef)
                    if g4 * 4 < NXC + NBC:
                        last_t = None
                        for (tt, m) in subs_of(n, 128):
                            bt_, ft = P.bank()
                            for cc in range(4):
                                last_t = P.op('pe', (lambda e, bt_=bt_, cc=cc, tt=tt, m=m: e.transpose(
                                    P.ps[0:m, bt_, cc * 128:(cc + 1) * 128], xcb[:, cc, tt:tt + m], M.e_id[:])),
                                    waits=[ft, xc_ev, M.even_ev], sig=(cc == 3))
                            gi, gfree = M.stage()
                            ecp = P.op('act', (lambda e, bt_=bt_, gi=gi, m=m: e.activation(
                                out=M.stg[0:m, gi, 0:512], in_=P.ps[0:m, bt_, 0:512], func=AF.Copy)), waits=[last_t, gfree])
                            P.bank_release(bt_, ecp)
                            if g4 * 4 < NXC:
                                dd = XS[tok0 + tt:tok0 + tt + m, g4 * 512:(g4 + 1) * 512]
                            else:
                                dd = BM[tok0 + tt:tok0 + tt + m, (g4 * 4 - NXC) * 128:(g4 * 4 - NXC) * 128 + 512]
                            wr.append(M.stage_dma(gi, dd, M.stg[0:m, gi, 0:512], ecp, dst_free))
                        a_ev = _flat([xc_ev, last_t])
                    else:
                        a_ev = _flat([xc_ev])
                M.wring.rel(slot, last_rd)
            subsC = [(xc0 + t, n, b * c.XB + t, t) for (t, n) in subs_of(c.XB, 482)] + \
                    [(cc0 + t, n, c.S + b * c.CB + t, c.XB + t) for (t, n) in subs_of(c.CB, 482)]
            GC = 256
            cv_ev = {}
            for gq in range(D // GC):
                slot, wfr = M.wslot()
                wt = M.wbuf[:, slot, 0:DC * 2 * GC].rearrange("p (kc n) -> p kc n", kc=DC)
                wev = [M.load_w(wt[:, :, 0:GC], w_in[:, o_ga + gq * GC:o_ga + (gq + 1) * GC], slot, wfr),
                       M.load_w(wt[:, :, GC:2 * GC], w_in[:, o_gg + gq * GC:o_gg + (gq + 1) * GC], slot, wfr)]
                for cc in range(GC // 128):
                    ch = gq * (GC // 128) + cc
                    for (col0, n, tok0, bo) in subsC:
                        nn = n + 30
                        bks, ls = [], []
                        for fi in range(2):
                            bb, fb = P.bank()
                            for kc in range(DC):
                                lx = P.op('pe', (lambda e, bb=bb, kc=kc, fi=fi, cc=cc, col0=col0, nn=nn, wt=wt: e.matmul(
                                    P.ps[:, bb, 0:nn], lhsT=wt[:, kc, fi * GC + cc * 128:fi * GC + (cc + 1) * 128],
                                    rhs=M.hbuf[:, kc, col0 - 15:col0 - 15 + nn], start=(kc == 0), stop=(kc == DC - 1))),
                                    waits=[fb, wev, hev], sig=(kc == DC - 1))
                            bks.append(bb)
                            ls.append(lx)
                        last_rd = lx
                        t1, f1 = M.tmpring.get()
                        ea = P.op('act', (lambda e, b1=bks[1], t1=t1, nn=nn: e.activation(
                            out=M.tmp[:, t1, 0:nn], in_=P.ps[:, b1, 0:nn], func=AF.Sigmoid)), waits=[ls[1], f1])
                        P.bank_release(bks[1], ea)
                        eb = P.op('dve', (lambda e, b0=bks[0], t1=t1, nn=nn: e.tensor_tensor(
                            out=M.tmp[:, t1, 0:nn], in0=M.tmp[:, t1, 0:nn], in1=P.ps[:, b0, 0:nn], op=ALU.mult)),
                            waits=[ea, ls[0]])
                        P.bank_release(bks[0], eb)
                        w_ = lambda k, ch=ch: M.e_cmw[:, jl, ch * 31 + k:ch * 31 + k + 1]
                        t4, f4 = M.tmpring.get()
                        ec = P.op('dve', (lambda e, t1=t1, n=n, w_=w_, ch=ch, bo=bo: e.tensor_scalar(
                            out=cv[:, ch, bo:bo + n], in0=M.tmp[:, t1, 0:n], scalar1=w_(0), scalar2=M.e_cmv[:, jl, 0, ch:ch + 1],
                            op0=ALU.mult, op1=ALU.add)), waits=[eb, a_ev, M.even_ev])
                        ec = P.op('dve', (lambda e, t1=t1, t4=t4, n=n, w_=w_: e.tensor_scalar(
                            out=M.tmp[:, t4, 0:n], in0=M.tmp[:, t1, 1:1 + n], scalar1=w_(1), scalar2=None, op0=ALU.mult)),
                            waits=[f4])
                        for k in range(2, 31):
                            acc = (lambda ch=ch, bo=bo, n=n: cv[:, ch, bo:bo + n]) if k % 2 == 0 else (lambda t4=t4, n=n: M.tmp[:, t4, 0:n])
                            ec = P.op('dve', (lambda e, t1=t1, n=n, k=k, w_=w_, acc=acc: e.scalar_tensor_tensor(
                                out=acc(), in0=M.tmp[:, t1, k:k + n], scalar=w_(k), in1=acc(),
                                op0=ALU.mult, op1=ALU.add)), sig=(k == 30))
                        ec = P.op('dve', (lambda e, t4=t4, n=n, ch=ch, bo=bo: e.tensor_tensor(
                            out=cv[:, ch, bo:bo + n], in0=cv[:, ch, bo:bo + n], in1=M.tmp[:, t4, 0:n], op=ALU.add)), waits=[ec])
                        M.tmpring.rel(t1, ec)
                        M.tmpring.rel(t4, ec)
                        cv_ev[(ch, bo)] = ec
                M.wring.rel(slot, last_rd)
            M.h_rd = [last_rd]
            fin = []
            for (col0, n, tok0, bo) in subsC:
                b1, f1_ = P.bank()
                b2, f2_ = P.bank()
                l1 = l2 = None
                for ch in range(DC):
                    qi, qfree = M.sqring.get()
                    t1, f1 = M.tmpring.get()
                    e0 = P.op('act', (lambda e, t1=t1, ch=ch, bo=bo, n=n: e.activation(
                        out=M.tmp[:, t1, 0:n], in_=cv[:, ch, bo:bo + n], func=AF.Square)), waits=[cv_ev[(ch, bo)], f1])
                    l1 = P.op('pe', (lambda e, b1=b1, ch=ch, bo=bo, n=n: e.matmul(
                        P.ps[:, b1, 0:n], lhsT=M.e_onesf[:], rhs=cv[:, ch, bo:bo + n], start=(ch == 0), stop=(ch == DC - 1))),
                        waits=[f1_, cv_ev[(ch, bo)], M.even_ev], sig=(ch == DC - 1))
                    l2 = P.op('pe', (lambda e, b2=b2, t1=t1, ch=ch, n=n: e.matmul(
                        P.ps[:, b2, 0:n], lhsT=M.e_onesf[:], rhs=M.tmp[:, t1, 0:n], start=(ch == 0), stop=(ch == DC - 1))),
                        waits=[f2_, e0])
                    M.tmpring.rel(t1, l2)
                    M.sqring.rel(qi, [])
                tm, fm = 'm', M.ln_rd
                tv, fv_ = 'v', M.ln_rd
                m1 = P.op('act', (lambda e, b1=b1, tm=tm, n=n: e.activation(
                    out=M.rstd[:, 0:n], in_=P.ps[:, b1, 0:n], func=AF.Copy, scale=1.0 / D)), waits=[l1, fm])
                P.bank_release(b1, m1)
                m2 = P.op('dve', (lambda e, tm=tm, tv=tv, n=n: e.tensor_tensor(
                    out=M.rstd[:, 512:512 + n], in0=M.rstd[:, 0:n], in1=M.rstd[:, 0:n], op=ALU.mult)), waits=[m1, fv_])
                m3 = P.op('dve', (lambda e, b2=b2, tv=tv, n=n: e.scalar_tensor_tensor(
                    out=M.rstd[:, 512:512 + n], in0=P.ps[:, b2, 0:n], scalar=1.0 / D, in1=M.rstd[:, 512:512 + n],
                    op0=ALU.mult, op1=ALU.subtract)), waits=[m2, l2])
                P.bank_release(b2, m3)
                m4 = P.op('act', (lambda e, tv=tv, n=n: e.activation(
                    out=M.rstd[:, 512:512 + n], in_=M.rstd[:, 512:512 + n], func=AF.Sqrt, bias=M.epsb[:, 0:1], scale=1.0)), waits=[m3])
                m5 = P.op('dve', (lambda e, tv=tv, n=n: e.reciprocal(out=M.rstd[:, 512:512 + n], in_=M.rstd[:, 512:512 + n])), waits=[m4])
                lastv = None
                for ch in range(DC):
                    t3, f3 = M.tmpring.get()
                    v1 = P.op('dve', (lambda e, t3=t3, tm=tm, ch=ch, bo=bo, n=n: e.tensor_tensor(
                        out=M.tmp[:, t3, 0:n], in0=cv[:, ch, bo:bo + n], in1=M.rstd[:, 0:n], op=ALU.subtract)),
                        waits=[m5, f3])
                    v2 = P.op('dve', (lambda e, t3=t3, tv=tv, ch=ch, n=n: e.scalar_tensor_tensor(
                        out=M.tmp[:, t3, 0:n], in0=M.tmp[:, t3, 0:n], scalar=M.e_cmv[:, jl, 1, ch:ch + 1], in1=M.rstd[:, 512:512 + n],
                        op0=ALU.mult, op1=ALU.mult)), waits=[v1])
                    gi, gfree = M.stage()
                    v3 = P.op('act', (lambda e, t3=t3, gi=gi, ch=ch, n=n: e.activation(
                        out=M.stg[:, gi, 0:n], in_=M.tmp[:, t3, 0:n], func=AF.Silu, bias=M.e_cmv[:, jl, 2, ch:ch + 1], scale=1.0)),
                        waits=[v2, gfree])
                    M.tmpring.rel(t3, v3)
                    wr.append(M.stage_dma(gi, YT[D + ch * 128:D + (ch + 1) * 128, tok0:tok0 + n], M.stg[:, gi, 0:n], v3, dst_free))
                    lastv = v1
                M.ln_rd = [lastv, v2]
                fin += [lastv, l1]
            a_ev = _flat([fin])
        for k in M.x_ev:
            M.x_ev[k] = _flat([a_ev])
        return wr

    def even_ssd(M, jl, ZS, XS, BM, BT, CT, DT, YF, YT, in_wr, y_free):
        c, P = M.c, M.P
        D, S, CL = c.D, c.S, c.CL
        op = P.op
        NCX, NCC = S // 128, CL // 128
        NH, HP, NG = 32, 64, 8
        off = [0]

        def carve(words):
            a = off[0]
            off[0] += words
            return M.arena[:, a:a + words]
        xsv = [carve(1024).bitcast(BF16) for _ in range(2)]
        bmv = [carve(512).bitcast(BF16) for _ in range(2)]
        btv = [carve(512).bitcast(BF16).rearrange("p (g t) -> p g t", g=NG) for _ in range(2)]
        ctv = [carve(512).bitcast(BF16).rearrange("p (g t) -> p g t", g=NG) for _ in range(2)]
        dtv = [carve(64) for _ in range(2)]
        hT = carve(2048)
        hTb = carve(1024).bitcast(BF16)
        xdt = carve(1024).bitcast(BF16)
        xw = carve(1024).bitcast(BF16)
        ydir = carve(2048)
        yw = carve(2048)
        small = carve(32 * 8).rearrange("p (a f) -> p a f", a=8)
        acumT = M.rstd[:, 0:1024]
        nacumT = M.sq[:, :, :].rearrange('p a b -> p (a b)').bitcast(F32)
        LTv = [carve(512) for _ in range(2)]
        MTv = [carve(256).bitcast(BF16).rearrange("p (h q) -> p h q", h=4) for _ in range(2)]
        ctm = [carve(256) for _ in range(2)]
        assert off[0] <= M.AW, off[0]
        hflat = M.hbuf[:, :, :].rearrange("p a b -> p (a b)")
        zsv = [hflat[:, i * 2048:(i + 1) * 2048] for i in range(2)]
        yfv = [hflat[:, 4096 + i * 4096:4096 + (i + 1) * 4096].bitcast(F32) for i in range(2)]
        normg = M.actb[:, :, :, :].rearrange("p a b t -> p (a b t)")[:, 0:4096].bitcast(F32)
        lds = [P.dsem(), P.dsem()]
        lds2 = [P.dsem(), P.dsem()]
        ods = P.dsem()
        ngds = P.dsem()
        a_free = _flat([list(M.x_ev.values())])
        e_ng = P.dma('sp', normg, M.ng_dram[jl], ngds, [M.actring.free[0], M.actring.free[1]])
        a_sb, acum_sb, Eq, dec, wst, ss, rstd = [small[:, i, :] for i in range(7)]
        lring, mring, cring = Ring(2), Ring(2), Ring(2)
        slot_rd = [[], []]
        slot2_rd = [_flat([M.h_rd]), _flat([M.h_rd])]
        outs = []
        n = 0
        prev = {'a_rd': [], 'small_rd': [], 'acT_rd': [], 'xdt_rd': [], 'xw_rd': [], 'ydir_rd': [], 'hTb_rd': [], 'yw_rd': []}
        hT_ev = [[] for _ in range(NG)]
        hTb_ev = [[] for _ in range(NG)]
        last_all = []
        for d in range(2):
            order = [('c', i) for i in range(NCC)] + [('x', i) for i in range(NCX)]
            if d == 1:
                order = [('c', i) for i in reversed(range(NCC))] + [('x', i) for i in reversed(range(NCX))]
            for idx, (kind, ci) in enumerate(order):
                if getattr(M, 'dbg_ssd', None) and (d > 0 or idx >= M.dbg_ssd[0]):
                    continue
                first = idx == 0
                t0 = (S if kind == 'c' else 0) + ci * 128
                sl = n % 2
                n += 1
                fr = _flat([slot_rd[sl], a_free, in_wr])
                xs, bm, bt, ct, dt = xsv[sl], bmv[sl], btv[sl], ctv[sl], dtv[sl]
                lev = [P.dma('sp', xs, XS[t0:t0 + 128, :], lds[sl], fr),
                       P.dma('sp', bm, BM[t0:t0 + 128, :], lds[sl], fr),
                       P.dma('sp', bt, BT[:, t0:t0 + 128].rearrange("(g n) t -> n g t", g=NG), lds[sl], fr),
                       P.dma('sp', ct, CT[:, t0:t0 + 128].rearrange("(g n) t -> n g t", g=NG), lds[sl], fr),
                       P.dma('sp', dt, DT[t0:t0 + 128, :], lds[sl], fr)]
                if d == 1:
                    fr2 = _flat([slot2_rd[sl], outs])
                    lev2 = [P.dma('sp', zsv[sl], ZS[t0:t0 + 128, :], lds2[sl], fr2),
                            P.dma('sp', yfv[sl], YF[t0:t0 + 128, :], lds2[sl], fr2)]
                dtd = dt[:, d * 32:(d + 1) * 32]
                rds = []
                eA = op('dve', lambda e, dtd=dtd, d=d: e.tensor_tensor(out=a_sb, in0=dtd, in1=M.e_aneg[:, jl, d * 32:(d + 1) * 32], op=ALU.mult),
                        waits=[lev, prev['a_rd'], M.even_ev])
                b1, f1 = P.bank()
                pB1 = op('pe', lambda e, b1=b1, d=d: e.matmul(P.ps[:, b1, 0:32], lhsT=M.e_tri[:, d, :], rhs=a_sb, start=True, stop=True),
                         waits=[eA, f1, M.even_ev])
                pB2 = op('pe', lambda e, b1=b1: e.matmul(P.ps[:, b1, 32:64], lhsT=M.e_onesf[:], rhs=a_sb, start=True, stop=True))
                prev['a_rd'] = [pB2]
                c1 = op('act', lambda e, b1=b1: e.activation(out=acum_sb, in_=P.ps[:, b1, 0:32], func=AF.Copy),
                        waits=[pB2, prev['small_rd'], prev['acT_rd']])
                c2 = op('act', lambda e, b1=b1: e.activation(out=Eq, in_=P.ps[:, b1, 0:32], func=AF.Exp))
                c3 = op('act', lambda e, b1=b1: e.activation(out=dec, in_=P.ps[:, b1, 32:64], func=AF.Exp))
                c4 = op('dve', lambda e, b1=b1: e.tensor_tensor(out=wst, in0=P.ps[:, b1, 32:64], in1=acum_sb, op=ALU.subtract),
                        waits=[c1, pB2, prev['small_rd']])
                c5 = op('act', lambda e: e.activation(out=wst, in_=wst, func=AF.Exp), waits=[c4])
                c6 = op('dve', lambda e, dtd=dtd: e.tensor_tensor(out=wst, in0=wst, in1=dtd, op=ALU.mult), waits=[c5])
                d1 = []
                for half in range(2):
                    b2, f2 = P.bank()
                    for gg in range(4):
                        g_ = half * 4 + gg
                        pD = op('pe', lambda e, b2=b2, gg=gg, g_=g_: e.transpose(P.ps[0:4, b2, gg * 128:(gg + 1) * 128], acum_sb[:, 4 * g_:4 * g_ + 4], M.e_id[:]),
                                waits=[c1, f2, M.even_ev], sig=(gg == 3))
                    dA = op('act', lambda e, b2=b2, half=half: e.activation(out=acumT[0:4, half * 512:(half + 1) * 512], in_=P.ps[0:4, b2, 0:512], func=AF.Copy),
                            waits=[pD, prev['acT_rd']])
                    dB = op('act', lambda e, b2=b2, half=half: e.activation(out=nacumT[0:4, half * 512:(half + 1) * 512], in_=P.ps[0:4, b2, 0:512], func=AF.Copy, scale=-1.0))
                    P.bank_release(b2, [dB])
                    d1 += [dA, dB]
                P.bank_release(b1, [c2, c3, c4])
                xs3 = xs.rearrange("p (h q) -> p h q", h=NH)
                e1 = op('dve', lambda e, xs3=xs3, dtd=dtd: e.tensor_tensor(
                    out=xdt.rearrange("p (h q) -> p h q", h=NH), in0=xs3, in1=dtd.unsqueeze(2).to_broadcast([128, NH, HP]), op=ALU.mult),
                    waits=[lev, prev['xdt_rd']])
                e2 = op('dve', lambda e, xs3=xs3: e.tensor_tensor(
                    out=xw.rearrange("p (h q) -> p h q", h=NH), in0=xs3, in1=wst.unsqueeze(2).to_broadcast([128, NH, HP]), op=ALU.mult),
                    waits=[c6, prev['xw_rd']])
                y_evs = []
                xdt_rd, xw_rd, small_rd, acT_rd, hTb_rd = [], [], [], [], []
                for g in range(NG if not getattr(M, 'dbg_ssd', None) else M.dbg_ssd[1]):
                    bc, fc = P.bank()
                    pc = op('pe', lambda e, bc=bc, g=g, bt=bt, ct=ct: e.matmul(P.ps[:, bc, 0:128], lhsT=bt[:, g, :], rhs=ct[:, g, :],
                                                                            start=True, stop=True), waits=[fc, lev])
                    bs, fs = P.bank()
                    for hh in range(4):
                        h = 4 * g + hh
                        op('pe', lambda e, bs=bs, hh=hh, g=g: e.matmul(P.ps[:, bs, hh * 128:(hh + 1) * 128], lhsT=M.e_oh[0:4, hh * 128:(hh + 1) * 128],
                                                                      rhs=acumT[0:4, g * 128:(g + 1) * 128], start=True, stop=False), waits=[fs, d1, M.even_ev], sig=False)
                        op('pe', lambda e, bs=bs, hh=hh, g=g: e.matmul(P.ps[:, bs, hh * 128:(hh + 1) * 128], lhsT=nacumT[0:4, g * 128:(g + 1) * 128],
                                                                      rhs=M.e_oh[0:4, hh * 128:(hh + 1) * 128], start=False, stop=False), sig=False)
                        psg = op('pe', lambda e, bs=bs, hh=hh, d=d: e.matmul(P.ps[:, bs, hh * 128:(hh + 1) * 128], lhsT=M.e_id[:],
                                                                       rhs=M.e_mask[:, d, :], start=False, stop=True), sig=(hh == 3))
                    li, lf = lring.get()
                    aL = op('act', lambda e, bs=bs, li=li: e.activation(out=LTv[li], in_=P.ps[:, bs, 0:512], func=AF.Exp), waits=[psg, lf])
                    P.bank_release(bs, aL)
                    mi, mf = mring.get()
                    dM = op('dve', lambda e, bc=bc, li=li, mi=mi: e.tensor_tensor(
                        out=MTv[mi], in0=LTv[li].rearrange("p (h q) -> p h q", h=4),
                        in1=P.ps[:, bc, 0:128].unsqueeze(1).to_broadcast([128, 4, 128]), op=ALU.mult), waits=[aL, pc, mf])
                    P.bank_release(bc, dM)
                    lring.rel(li, dM)
                    by, fy = P.bank()
                    for hh in range(4):
                        h = 4 * g + hh
                        py = op('pe', lambda e, by=by, hh=hh, h=h, mi=mi: e.matmul(
                            P.ps[:, by, hh * HP:(hh + 1) * HP], lhsT=MTv[mi][:, hh, :], rhs=xdt[:, h * HP:(h + 1) * HP],
                            start=True, stop=True), waits=[fy, dM, e1], sig=(hh == 3))
                    mring.rel(mi, py)
                    xdt_rd = [py]
                    if not first:
                        po = op('pe', lambda e, by=by, g=g, ct=ct: e.matmul(P.ps[:, by, 256:512], lhsT=ct[:, g, :], rhs=hTb[:, g * 256:(g + 1) * 256],
                                                                            start=True, stop=True), waits=[hTb_ev[g]])
                        hTb_rd = [po]
                        ki, kf = cring.get()
                        k1 = op('dve', lambda e, by=by, g=g, ki=ki: e.tensor_tensor(
                            out=ctm[ki].rearrange("p (h q) -> p h q", h=4), in0=P.ps[:, by, 256:512].rearrange("p (h q) -> p h q", h=4),
                            in1=Eq[:, 4 * g:4 * g + 4].unsqueeze(2).to_broadcast([128, 4, HP]), op=ALU.mult), waits=[po, c2, kf])
                        k2 = op('dve', lambda e, by=by, g=g, ki=ki: e.tensor_tensor(
                            out=ydir[:, g * 256:(g + 1) * 256], in0=P.ps[:, by, 0:256], in1=ctm[ki], op=ALU.add),
                            waits=[k1, py, prev['ydir_rd']])
                        cring.rel(ki, k2)
                    else:
                        po = py
                        k2 = op('act', lambda e, by=by, g=g: e.activation(out=ydir[:, g * 256:(g + 1) * 256], in_=P.ps[:, by, 0:256], func=AF.Copy),
                                waits=[py, prev['ydir_rd']])
                    P.bank_release(by, k2)
                    y_evs.append(k2)
                    bst, fst = P.bank()
                    pst = op('pe', lambda e, bst=bst, g=g, bm=bm: e.matmul(P.ps[:, bst, 0:256], lhsT=bm[:, g * 128:(g + 1) * 128],
                                                                          rhs=xw[:, g * 256:(g + 1) * 256], start=True, stop=True),
                             waits=[fst, e2, lev])
                    xw_rd = [pst]
                    hg = hT[:, g * 256:(g + 1) * 256]
                    if first:
                        s2 = op('dve', lambda e, bst=bst, hg=hg: e.tensor_copy(out=hg, in_=P.ps[:, bst, 0:256]), waits=[pst, hT_ev[g]])
                    else:
                        s1 = op('dve', lambda e, hg=hg, g=g: e.tensor_tensor(
                            out=hg.rearrange("p (h q) -> p h q", h=4), in0=hg.rearrange("p (h q) -> p h q", h=4),
                            in1=dec[:, 4 * g:4 * g + 4].unsqueeze(2).to_broadcast([128, 4, HP]), op=ALU.mult), waits=[c3, hT_ev[g]])
                        s2 = op('dve', lambda e, bst=bst, hg=hg: e.tensor_tensor(out=hg, in0=hg, in1=P.ps[:, bst, 0:256], op=ALU.add),
                                waits=[s1, pst])
                    P.bank_release(bst, s2)
                    s3 = op('act', lambda e, hg=hg, g=g: e.activation(out=hTb[:, g * 256:(g + 1) * 256], in_=hg, func=AF.Copy),
                            waits=[s2, po, prev['hTb_rd']])
                    hT_ev[g] = [s3]
                    hTb_ev[g] = [s3]
                    small_rd = [s2, k2]
                    acT_rd = [psg]
                prev['xdt_rd'], prev['xw_rd'], prev['small_rd'], prev['acT_rd'], prev['hTb_rd'] = xdt_rd, xw_rd, _flat([small_rd, c6, e1, e2]), acT_rd, hTb_rd
                slot_rd[sl] = _flat([xdt_rd, xw_rd, e1, e2, hTb_rd, pc])
                if d == 0:
                    o = P.dma('sp', YF[t0:t0 + 128, :], ydir, ods, [y_evs])
                    prev['ydir_rd'] = [o]
                    outs.append(o)
                else:
                    zs, yf = zsv[sl], yfv[sl]
                    fin_ev = []
                    sq_ev = []
                    for g in range(NG):
                        gs = slice(g * 256, (g + 1) * 256)
                        ki, kf = cring.get()
                        f1_ = op('dve', lambda e, g=g, gs=gs, ki=ki, xs=xs: e.tensor_tensor(
                            out=ctm[ki].rearrange("p (h q) -> p h q", h=4), in0=xs[:, gs].rearrange("p (h q) -> p h q", h=4),
                            in1=M.e_dsk[:, jl, 4 * g:4 * g + 4].unsqueeze(2).to_broadcast([128, 4, HP]), op=ALU.mult), waits=[kf, lev])
                        f2_ = op('dve', lambda e, gs=gs, yf=yf: e.tensor_tensor(out=yw[:, gs], in0=ydir[:, gs], in1=yf[:, gs], op=ALU.add),
                                 waits=[y_evs[g], lev2, prev['yw_rd']])
                        f3_ = op('dve', lambda e, gs=gs, ki=ki: e.tensor_tensor(out=yw[:, gs], in0=yw[:, gs], in1=ctm[ki], op=ALU.add), waits=[f1_, f2_])
                        cring.rel(ki, f3_)
                        f4_ = op('dve', lambda e, gs=gs, zs=zs: e.tensor_tensor(out=yw[:, gs], in0=yw[:, gs], in1=zs[:, gs], op=ALU.mult), waits=[f3_])
                        ti, tf = M.tmpring.get()
                        f5a = op('act', lambda e, gs=gs, g=g, ti=ti: e.activation(out=M.tmp[:, ti, 0:256], in_=yw[:, gs], func=AF.Square),
                                 waits=[f4_, tf])
                        f5_ = op('dve', lambda e, g=g, ti=ti: e.reduce_sum(out=ss[:, g:g + 1], in_=M.tmp[:, ti, 0:256], axis=mybir.AxisListType.X),
                                 waits=[f5a, prev['small_rd']])
                        M.tmpring.rel(ti, f5_)
                        sq_ev.append(f5_)
                    prev['ydir_rd'] = [f2_]
                    r1 = op('act', lambda e: e.activation(out=rstd[:, 0:8], in_=ss[:, 0:8], func=AF.Sqrt, bias=M.epsb[:, 0:1], scale=1.0 / 256),
                            waits=[sq_ev])
                    r2 = op('dve', lambda e: e.reciprocal(out=rstd[:, 0:8], in_=rstd[:, 0:8]), waits=[r1])
                    for g in range(NG):
                        gs = slice(g * 256, (g + 1) * 256)
                        f6_ = op('dve', lambda e, gs=gs, g=g: e.scalar_tensor_tensor(
                            out=yw[:, gs], in0=yw[:, gs], scalar=rstd[:, g:g + 1], in1=normg[:, gs], op0=ALU.mult, op1=ALU.mult),
                            waits=[r2, e_ng])
                        fin_ev.append(f6_)
                    prev['small_rd'] = _flat([prev['small_rd'], r2, fin_ev])
                    slot2_rd[sl] = [f4_]
                    slot_rd[sl] = _flat([slot_rd[sl], f1_])
                    tl = None
                    for g4 in range(D // 512):
                        bt_, ft = P.bank()
                        for cc in range(4):
                            ch = g4 * 4 + cc
                            tl = op('pe', lambda e, bt_=bt_, cc=cc, ch=ch: e.transpose(
                                P.ps[:, bt_, cc * 128:(cc + 1) * 128], yw[:, ch * 128:(ch + 1) * 128], M.e_id[:]),
                                waits=[ft, fin_ev[ch // 2]], sig=(cc == 3))
                        gi, gfree = M.stage()
                        ecp = op('act', lambda e, bt_=bt_, gi=gi: e.activation(out=M.stg[:, gi, 0:512], in_=P.ps[:, bt_, 0:512], func=AF.Copy),
                                 waits=[tl, gfree])
                        P.bank_release(bt_, ecp)
                        o = M.stage_dma(gi, YT[g4 * 512:(g4 + 1) * 512, t0:t0 + 128].rearrange("(c p) t -> p c t", p=128),
                                        M.stg[:, gi, 0:512].rearrange("p (c t) -> p c t", c=4), ecp, y_free)
                        outs.append(o)
                    prev['yw_rd'] = [tl]
                last_all = _flat([slot_rd[sl], y_evs])
        if getattr(M, 'dbg_ssd', None):
            dd_ = P.dsem()
            allev = [(e_, P.cnt[e_]) for e_ in ENG]
            for nm_, ap_, shp in [('d_small', small.rearrange("p a f -> p (a f)"), [128, 256]), ('d_acumT', acumT[0:4, :], [4, 1024]), ('d_nacumT', nacumT[0:4, :], [4, 1024]),
                                  ('d_LT', LTv[0], [128, 512]), ('d_ydir', ydir, [128, 2048]), ('d_hT', hT, [128, 2048])]:
                t_ = M.nc.dram_tensor(nm_, shp, F32, kind="ExternalOutput").ap()
                outs.append(P.dma('sp', t_, ap_, dd_, allev))
            for nm_, ap_, shp in [('d_MT', MTv[0].rearrange("p h q -> p (h q)"), [128, 512]), ('d_xdt', xdt, [128, 2048]), ('d_xw', xw, [128, 2048])]:
                t_ = M.nc.dram_tensor(nm_, shp, BF16, kind="ExternalOutput").ap()
                outs.append(P.dma('sp', t_, ap_, dd_, allev))
        for k in M.x_ev:
            M.x_ev[k] = _flat([last_all, outs[-8:], prev['yw_rd']])
        M.h_rd = _flat([M.h_rd, slot2_rd[0], slot2_rd[1]])
        M.actring.free[0] = _flat([M.actring.free[0], prev['small_rd']])
        M.actring.free[1] = _flat([M.actring.free[1], prev['small_rd']])
        return outs


EVEN_IN = 2048 + 4096 + 64 + 4096
ODD_IN = 6 * 2048
E_SHAPES = lambda n: {'cw': [n, 128, 96], 'cb': [n, 128, 32], 'dtb': [n, 128, 64], 'alog': [n, 128, 64], 'dsk': [n, 128, 32],
                      'ng': [n, 128, 2048], 'cmw': [n, 128, 16 * 31], 'cmb': [n, 128, 16], 'cmg': [n, 128, 16], 'cmbeta': [n, 128, 16],
                      'tri': [2, 128, 128], 'maskfb': [2, 128, 128], 'ident': [128, 128], 'onehot': [4, 512]}


def build_full(c, layer_types=None):
    M = Model(c)
    P = M.P
    D, DC, T, S, FF = c.D, c.DC, c.T, c.S, c.FF
    L = c.DEPTH
    lt = layer_types or ['e' if i % 2 == 0 else 'o' for i in range(L)]
    n_even = max(1, sum(1 for t in lt if t == 'e'))
    n_odd = max(1, sum(1 for t in lt if t == 'o'))
    xT = M.inp("xT", [D, T])
    cT = M.inp("cT", [128, DC, 2])
    wmod = M.inp("wmod", [L, D, 9 * D])
    bmodL = M.inp("bmodL", [L, 128, 9 * DC])
    gL = M.inp("gL", [L, 128, 3 * DC])
    fwi = M.inp("ffn_w_in", [L, 2, D, 2 * FF])
    fwo = M.inp("ffn_w_out", [L, 2, FF, D])
    ewi = M.inp("ev_w_in", [n_even, D, EVEN_IN])
    ewo = M.inp("ev_w_out", [n_even, 2 * D, D])
    owi = M.inp("od_w_in", [n_odd, D, ODD_IN])
    owo = M.inp("od_w_out", [n_odd, 2 * D, D])
    qgL = M.inp("qgL", [128, n_odd])
    kgL = M.inp("kgL", [128, n_odd])
    scwL = M.inp("scwL", [n_odd, 128, DC * 3])
    nabias = M.inp("nabias", [n_odd, D // 128, 128, 3200])
    namask = M.inp("namask", [128, 3200])
    eins = {k: M.inp("ei_" + k, v) for k, v in E_SHAPES(n_even).items()}
    yT = M.outp("yT", [D, S])
    XSc = M.scratch("XSc", [D, T], F32)
    HT = M.scratch("HT", [D, T], BF16)
    YT = M.scratch("YT", [2 * D, T], BF16)
    ZS = M.scratch("ZS", [T, D], BF16)
    XS_ = M.scratch("XS_", [T, D], BF16)
    BM = M.scratch("BM", [T, 1024], BF16)
    BT = M.scratch("BT", [1024, T], BF16)
    CT = M.scratch("CT", [1024, T], BF16)
    DT = M.scratch("DT", [T, 64], F32)
    YF = M.scratch("YF", [T, D], F32)
    QT = M.scratch("QT", [D, T], BF16)
    KT = M.scratch("KT", [D, T], BF16)
    VT = M.scratch("VT", [T, D], BF16)

    M.mod_phase(cT, wmod, bmodL, gL)
    M.mixer_setup()
    M.odd_consts(qgL, kgL, scwL, n_odd)
    M.even_consts(n_even, eins)

    x_wr, ht_wr, ht_rd, yt_wr, yt_rd = [], [], [], [], []
    st2_rd = {'e': [], 'o': []}
    je = jo = 0
    prev_wout = None
    final = []
    for l in range(L + 1):
        src = xT if l == 0 else XSc
        new_ht, new_xwr, new_ytrd = [], [], []
        for b in range(c.NB):
            M.load_x(src, b, waits=x_wr)
            if l > 0:
                M.outproj(l - 1, YT, prev_wout, b, yt_wr)
                new_ytrd += _flat([M.h_rd])
                h = M.adaln(l - 1, 2)
                M.ffn(l - 1, 2, fwi[l - 1, 1], fwo[l - 1, 1], h)
            if l < L:
                h = M.adaln(l, 0)
                M.ffn(l, 0, fwi[l, 0], fwo[l, 0], h)
                h2 = M.adaln(l, 1)
                new_ht += M.store_h(HT, b, h2, ht_rd)
                new_xwr += M.store_x(XSc, b)
            else:
                final += M.store_x(yT, b, with_ctx=False)
        if l == L:
            break
        x_wr = new_xwr
        ht_wr = new_ht
        yt_rd = new_ytrd
        if lt[l] == 'e':
            wr = M.even_in(je, HT, ewi[je], ZS, XS_, BM, BT, CT, DT, YT, ht_wr, [st2_rd['e'], yt_rd])
            outs = M.even_ssd(je, ZS, XS_, BM, BT, CT, DT, YF, YT, wr, yt_rd)
            prev_wout = ewo[je]
            je += 1
        else:
            wr = M.odd_in(jo, HT, owi[jo], QT, KT, VT, YT, ht_wr, [st2_rd['o'], yt_rd])
            outs = M.odd_att(jo, QT, KT, VT, YT, nabias, namask, wr, yt_rd)
            prev_wout = owo[jo]
            jo += 1
        st2_rd[lt[l]] = outs[-4:]
        yt_wr = _flat([wr, outs])
        ht_rd = _flat([M.h_rd])
    M.finish([final])
    M.run()
    return M


def na_tables(rpb, S, GW=64, NR=8, NCOL=16):
    rows = S // GW; NT = S // 128
    col = np.arange(GW)
    c0 = np.clip(col - NCOL // 2, 0, GW - NCOL)
    col_ok = (col[None, :] >= c0[:, None]) & (col[None, :] < c0[:, None] + NCOL)
    col_idx = np.clip(col[None, :] - col[:, None] + NCOL - 1, 0, 2 * NCOL - 2)
    reps = [0, 1, 2, NT - 2, NT - 1]
    n_odd, H = rpb.shape[:2]
    idx_rel = np.zeros((5, 2, 5, 2), np.int64); valid = np.zeros((5, 2, 5, 2), bool)
    for p, j in enumerate(reps):
        a0 = min(max(j - 2, 0), NT - 5)
        for i in range(5):
            for eps in range(2):
                kr = 2 * (a0 + i) + eps
                for dl in range(2):
                    r = 2 * j + dl
                    r0 = min(max(r - NR // 2, 0), rows - NR)
                    ok = (r0 <= kr <= r0 + NR - 1)
                    valid[p, eps, i, dl] = ok
                    idx_rel[p, eps, i, dl] = (kr - r + NR - 1) if ok else 0
    g = rpb[:, :, idx_rel]
    g = g[..., col_idx]
    m = valid[:, :, :, :, None, None] & col_ok[None, None, None, None]
    g = np.where(m[None, None], g, 0.0).astype(np.float32)
    g = g.transpose(0, 1, 3, 7, 2, 4, 5, 6)
    nabias = np.ascontiguousarray(g.reshape(n_odd, H, 128, 5 * 5 * 128))
    mm = np.where(m, 0.0, -30000.0).astype(np.float32).transpose(1, 5, 0, 2, 3, 4)
    namask = np.ascontiguousarray(mm.reshape(128, 5 * 5 * 128))
    return nabias, namask

def fm(v, nch):
    v = np.asarray(v)
    return np.ascontiguousarray(np.moveaxis(v.reshape(v.shape[:-1] + (nch, 128)), -1, -2))

def rep(v):
    v = np.asarray(v, np.float32).reshape(-1)
    return np.ascontiguousarray(np.broadcast_to(v[None, :], (128, v.size)))

def even_tables(ssd_conv_w, ssd_conv_b, dt_bias, a_log, ssd_d, ssd_norm_g, cm_conv_w, cm_conv_b, cm_ln_g, cm_ln_b):
    n = ssd_conv_w.shape[0]
    out = {}
    out['cw'] = np.stack([np.ascontiguousarray(fm(ssd_conv_w[j], 32).transpose(1, 2, 0).reshape(128, 96)) for j in range(n)])
    out['cb'] = np.stack([fm(ssd_conv_b[j], 32) for j in range(n)])
    out['dtb'] = np.stack([rep(dt_bias[j]) for j in range(n)])
    out['alog'] = np.stack([rep(a_log[j]) for j in range(n)])
    out['dsk'] = np.stack([rep(ssd_d[j]) for j in range(n)])
    out['ng'] = np.stack([rep(ssd_norm_g[j]) for j in range(n)])
    out['cmw'] = np.stack([np.ascontiguousarray(fm(cm_conv_w[j], 16).transpose(1, 2, 0).reshape(128, 16 * 31)) for j in range(n)])
    out['cmb'] = np.stack([fm(cm_conv_b[j], 16) for j in range(n)])
    out['cmg'] = np.stack([fm(cm_ln_g[j], 16) for j in range(n)])
    out['cmbeta'] = np.stack([fm(cm_ln_b[j], 16) for j in range(n)])
    i = np.arange(128)
    tri = np.stack([(i[:, None] <= i[None, :]), (i[:, None] >= i[None, :])]).astype(np.float32)
    out['tri'] = tri
    out['maskfb'] = np.stack([np.where(i[None, :] >= i[:, None], 0.0, -30000.0), np.where(i[None, :] <= i[:, None], 0.0, -30000.0)]).astype(np.float32)
    out['ident'] = np.eye(128, dtype=np.float32)
    oh = np.zeros((4, 4, 128), np.float32)
    for h in range(4):
        oh[h, h, :] = 1.0
    out['onehot'] = oh.reshape(4, 4 * 128)
    return {k: np.ascontiguousarray(v, dtype=np.float32) for k, v in out.items()}

def prep_inputs(inp, S, CL, DEPTH, bidx):
    D = 2048; DC = 16
    f32 = np.float32
    x = np.asarray(inp['x'], f32)[bidx]; ctx = np.asarray(inp['ctx'], f32)[bidx]
    m = {}
    m['xT'] = np.ascontiguousarray(np.concatenate([x, ctx], 0).T)
    cv = np.stack([np.asarray(inp['c'], f32)[bidx], np.asarray(inp['c_ctx'], f32)], 0)
    m['cT'] = np.ascontiguousarray(cv.T.reshape(DC, 128, 2).transpose(1, 0, 2))
    m['wmod'] = np.asarray(inp['w_mod'], f32)
    m['bmodL'] = np.ascontiguousarray(np.asarray(inp['b_mod'], f32).reshape(DEPTH, 9 * DC, 128).transpose(0, 2, 1))
    m['gL'] = np.ascontiguousarray(np.asarray(inp['norm_g'], f32).reshape(DEPTH, 3 * DC, 128).transpose(0, 2, 1))
    m['ffn_w_in'] = np.asarray(inp['ffn_w_in'], f32); m['ffn_w_out'] = np.asarray(inp['ffn_w_out'], f32)
    m['ev_w_in'] = np.asarray(inp['ev_w_in'], f32); m['ev_w_out'] = np.asarray(inp['ev_w_out'], f32)
    m['od_w_in'] = np.asarray(inp['od_w_in'], f32); m['od_w_out'] = np.asarray(inp['od_w_out'], f32)
    m['qgL'] = np.ascontiguousarray(np.asarray(inp['na_q_g'], f32).T)
    m['kgL'] = np.ascontiguousarray(np.asarray(inp['na_k_g'], f32).T)
    scw = np.asarray(inp['sc_conv_w'], f32)
    m['scwL'] = np.ascontiguousarray(scw.transpose(0, 2, 1).reshape(-1, DC, 128, 3).transpose(0, 2, 1, 3).reshape(-1, 128, DC * 3))
    nb, nm = na_tables(np.asarray(inp['na_rpb'], f32), S)
    m['nabias'] = nb; m['namask'] = nm
    tabs = even_tables(*[np.asarray(inp[k], f32) for k in ['ssd_conv_w', 'ssd_conv_b', 'ssd_dt_bias', 'ssd_a_log', 'ssd_d', 'ssd_norm_g',
                                                            'cm_conv_w', 'cm_conv_b', 'cm_ln_g', 'cm_ln_b']])
    for k, v in tabs.items():
        m['ei_' + k] = v
    return m


_CACHE = {}


def kernel(**inp):
    S, CL, DEPTH = 4096, 256, 4
    if 'M' not in _CACHE:
        _CACHE['M'] = build_full(Cfg())
    M = _CACHE['M']
    base = prep_inputs(inp, S, CL, DEPTH, 0)
    per_b = [{'xT': base['xT'], 'cT': base['cT']}]
    b1 = prep_inputs({**inp, 'w_mod': inp['w_mod'][:0]}, S, CL, DEPTH, 1) if False else None
    x = np.asarray(inp['x'], np.float32); ctx = np.asarray(inp['ctx'], np.float32)
    cvec = np.asarray(inp['c'], np.float32); cctx = np.asarray(inp['c_ctx'], np.float32)
    per_b = []
    for b in range(2):
        xT = np.ascontiguousarray(np.concatenate([x[b], ctx[b]], 0).T)
        cv = np.stack([cvec[b], cctx], 0)
        cT = np.ascontiguousarray(cv.T.reshape(16, 128, 2).transpose(1, 0, 2))
        per_b.append({'xT': xT, 'cT': cT})
    maps = []
    for core in range(8):
        m = dict(base)
        m.update(per_b[core // 4])
        maps.append(m)
    res = run_bass_kernel_spmd(M.nc, maps, core_ids=list(range(8)))
    out = np.stack([np.ascontiguousarray(res.results[0]['yT'].T), np.ascontiguousarray(res.results[4]['yT'].T)], 0)
    return out.astype(np.float32)
```

```python
import numpy as np
from contextlib import ExitStack
import concourse.bass as bass
import concourse.mybir as mybir
from concourse.bass_utils import run_bass_kernel_spmd

F32, BF16 = mybir.dt.float32, mybir.dt.bfloat16
AF = mybir.ActivationFunctionType
ALU = mybir.AluOpType
ENG = ('pe', 'act', 'dve', 'pool', 'sp')


def _flat(w):
    out = []
    for x in w:
        if x is None:
            continue
        if isinstance(x, list):
            out.extend(_flat(x))
        else:
            out.append(x)
    return out


class Prog:
    def __init__(s, nc):
        s.nc = nc
        s.q = {e: [] for e in ENG}
        s.cnt = {e: 0 for e in ENG}
        s.sem = {}
        s.waited = {e: {} for e in ENG}
        s.stack = ExitStack()
        for e in ENG:
            s.sem[e] = s.stack.enter_context(nc.semaphore("es_" + e))
        s.nds = 0
        s.ps = s.stack.enter_context(nc.psum_tensor("ps", [128, 8, 512], F32))
        s.bank_free = [[] for _ in range(8)]
        s.bank_i = 0

    def sb(s, name, shape, dt):
        return s.stack.enter_context(s.nc.sbuf_tensor(name, shape, dt))

    def dsem(s, name=None):
        s.nds += 1
        key = "ds%d" % s.nds
        s.sem[key] = s.stack.enter_context(s.nc.semaphore(key))
        s.cnt[key] = 0
        return key

    def _w(s, eng, waits):
        mx = {}
        for (k, v) in _flat(list(waits)):
            if v > mx.get(k, 0):
                mx[k] = v
        res = []
        for k, v in mx.items():
            if s.waited[eng].get(k, 0) >= v:
                continue
            s.waited[eng][k] = v
            res.append((k, v))
        return res

    def op(s, eng, fn, waits=(), sig=True):
        w = s._w(eng, waits)
        ev = None
        if sig:
            s.cnt[eng] += 1
            ev = (eng, s.cnt[eng])
        s.q[eng].append((fn, w, ev, 1))
        return ev

    def dma(s, eng, out, in_, ds, waits=()):
        w = s._w(eng, waits)
        s.cnt[ds] += 16
        ev = (ds, s.cnt[ds])
        s.q[eng].append((lambda e: e.dma_start(out=out, in_=in_), w, ev, 16))
        return ev

    def dmaf(s, eng, fn, ds, waits=()):
        w = s._w(eng, waits)
        s.cnt[ds] += 16
        ev = (ds, s.cnt[ds])
        s.q[eng].append((fn, w, ev, 16))
        return ev

    def bank(s):
        i = s.bank_i
        s.bank_i = (i + 1) % 8
        return i, s.bank_free[i]

    def bank_release(s, i, evs):
        s.bank_free[i] = _flat([evs])

    def run(s):
        nc = s.nc
        with nc.Block() as block:
            def mk(name):
                def f(e):
                    for (fn, w, ev, inc) in s.q[name]:
                        for (k, v) in w:
                            e.wait_ge(s.sem[k], v)
                        ins = fn(e)
                        if ev is not None:
                            ins.then_inc(s.sem[ev[0]], inc)
                return f
            block.tensor(mk('pe'))
            block.scalar(mk('act'))
            block.vector(mk('dve'))
            block.gpsimd(mk('pool'))
            block.sync(mk('sp'))


class Ring:
    def __init__(s, n):
        s.n = n
        s.i = 0
        s.free = [[] for _ in range(n)]

    def get(s):
        i = s.i
        s.i = (i + 1) % s.n
        return i, s.free[i]

    def rel(s, i, evs):
        s.free[i] = _flat([evs])


def subs_of(n, step=512):
    out = []
    t = 0
    while t < n:
        m = min(step, n - t)
        out.append((t, m))
        t += m
    return out


class Cfg:
    def __init__(s, D=2048, FF=5632, S=4096, CL=256, DEPTH=4, NB=4, GW=64, XB=None, CB=None):
        s.D, s.FF, s.S, s.CL, s.DEPTH, s.NB, s.GW = D, FF, S, CL, DEPTH, NB, GW
        s.DC = D // 128
        s.T = S + CL
        s.NM = 9
        s.XB = XB if XB is not None else S // NB
        s.CB = CB if CB is not None else CL // NB
        s.TBT = s.XB + s.CB
        s.FG = 256
        s.WSLOT = 12288 * (16 // s.DC) if s.DC < 16 else 12288
        s.WSLOT = 12288
        s.EPS = 1e-6

    def blk_subs(s):
        out = [(t, n, 0) for (t, n) in subs_of(s.XB)]
        out += [(s.XB + t, n, 1) for (t, n) in subs_of(s.CB)]
        return out


class Model:
    def __init__(M, cfg):
        M.c = c = cfg
        M.nc = nc = bass.Bass("TRN2", target_bir_lowering=False)
        M.P = P = Prog(nc)
        D, DC, T = c.D, c.DC, c.T
        M.din = {}
        M.wbuf = P.sb("wbuf", [128, 2, c.WSLOT], BF16)
        M.wring = Ring(2)
        M.wds = [P.dsem(), P.dsem()]
        M.AW = max(DC * c.TBT, 17408)
        M.arena = P.sb("arena", [128, M.AW], F32)
        M.xres = M.arena[:, 0:DC * c.TBT].rearrange("p (dc t) -> p dc t", dc=DC)
        M.hbuf = P.sb("hbuf", [128, DC, max(c.TBT + 64, 768)], BF16)
        M.actb = P.sb("actb", [128, 2, 2, max(c.TBT, 1024)], BF16)
        M.actring = Ring(2)
        M.tmp = P.sb("tmpf", [128, 6, 512], F32)
        M.tmpring = Ring(6)
        M.sil = M.tmp
        M.silring = M.tmpring
        M.sq = P.sb("sq", [128, 4, 512], BF16)
        M.sqring = Ring(4)
        M.rstd = P.sb("rstd", [128, max(c.TBT, 1024)], F32)
        M.onesD = P.sb("onesD", [128, 128], BF16)
        M.epsb = P.sb("epsb", [128, 1], F32)
        M.MT = P.sb("MT", [128, c.DEPTH, c.NM * DC, 2], F32)
        M.gsb = M.arena[:, 0:c.DEPTH * 3 * DC].rearrange("p (l f) -> p l f", l=c.DEPTH)
        M.bmsb = M.arena[:, 1024:1024 + c.DEPTH * c.NM * DC].rearrange("p (l f) -> p l f", l=c.DEPTH)
        M.csb = P.sb("csb", [128, DC, 2], F32)
        M.scb = P.sb("scb", [128, DC, 2], BF16)
        M.ld = P.dsem()
        M.xld = P.dsem()
        M.xst = P.dsem()
        M.yld = P.dsem()
        M.yt_ld = []
        M.x_ev = {}
        M.h_rd = []
        M.const_ev = []
        M.rstd_rd = {}
        M.ln_rd = []
        M.debug_scratch = False

    def inp(M, name, shape, dt=F32):
        t = M.nc.dram_tensor(name, list(shape), dt, kind="ExternalInput").ap()
        M.din[name] = t
        return t

    def outp(M, name, shape, dt=F32):
        return M.nc.dram_tensor(name, list(shape), dt, kind="ExternalOutput").ap()

    def scratch(M, name, shape, dt):
        if M.debug_scratch:
            return M.nc.dram_tensor(name, list(shape), dt, kind="ExternalOutput").ap()
        return M.nc.dram_tensor(name, list(shape), dt).ap()

    def wslot(M):
        i, fr = M.wring.get()
        return i, fr

    def load_w(M, dst, src, slot, waits):
        return M.P.dma('pool', dst, src.rearrange("(kc p) n -> p kc n", p=128), M.wds[slot], waits)

    def mod_phase(M, cT, wmod, bmodL, gL):
        c, P = M.c, M.P
        DC = c.DC
        e1 = P.dma('sp', M.csb[:], cT, M.ld)
        e2 = P.dma('sp', M.gsb[:], gL.rearrange("l p f -> p l f"), M.ld)
        e3 = P.dma('sp', M.bmsb[:], bmodL.rearrange("l p f -> p l f"), M.ld)
        ld_ev = [e1, e2, e3]
        ev_ones = P.op('pool', lambda e: e.memset(M.onesD[:], 1.0 / c.D))
        M.const_ev.append(ev_ones)
        M.const_ev.append(P.op('pool', lambda e: e.memset(M.epsb[:], c.EPS)))
        ev_sc = P.op('act', lambda e: e.activation(out=M.scb[:], in_=M.csb[:], func=AF.Silu), waits=ld_ev)
        NF = c.NM * DC
        FCS = c.WSLOT // DC // 128
        mt_evs = []
        for l in range(c.DEPTH):
            for f0 in range(0, NF, FCS):
                nf = min(FCS, NF - f0)
                slot, fr = M.wslot()
                wt = M.wbuf[:, slot, 0:DC * nf * 128].rearrange("p (kc n) -> p kc n", kc=DC)
                wev = M.load_w(wt, wmod[l, :, f0 * 128:(f0 + nf) * 128], slot, fr)
                bk, bfree = P.bank()
                last = None
                for fi in range(nf):
                    for kc in range(DC):
                        last = P.op('pe', (lambda e, fi=fi, kc=kc, wt=wt, bk=bk: e.matmul(
                            P.ps[:, bk, 2 * fi:2 * fi + 2], lhsT=wt[:, kc, fi * 128:(fi + 1) * 128],
                            rhs=M.scb[:, kc, :], start=(kc == 0), stop=(kc == DC - 1))),
                            waits=[wev, ev_sc, bfree], sig=(fi == nf - 1 and kc == DC - 1))
                M.wring.rel(slot, last)
                evs = []
                for s_ in range(2):
                    ev = P.op('dve', (lambda e, bk=bk, s_=s_, l=l, f0=f0, nf=nf: e.tensor_tensor(
                        out=M.MT[:, l, f0:f0 + nf, s_],
                        in0=P.ps[:, bk, 0:2 * nf].rearrange("p (f s) -> p f s", s=2)[:, :, s_],
                        in1=M.bmsb[:, l, f0:f0 + nf], op=ALU.add)), waits=[last, ld_ev])
                    evs.append(ev)
                P.bank_release(bk, evs)
                mt_evs += evs
            for j in range(3):
                for s_ in range(2):
                    ev = P.op('dve', (lambda e, l=l, j=j, s_=s_: e.scalar_tensor_tensor(
                        out=M.MT[:, l, (3 * j + 1) * DC:(3 * j + 2) * DC, s_],
                        in0=M.MT[:, l, (3 * j + 1) * DC:(3 * j + 2) * DC, s_], scalar=1.0,
                        in1=M.gsb[:, l, j * DC:(j + 1) * DC], op0=ALU.add, op1=ALU.mult)), waits=mt_evs)
                    mt_evs.append(ev)
                    if j != 1:
                        ev = P.op('dve', (lambda e, l=l, j=j, s_=s_: e.tensor_scalar(
                            out=M.MT[:, l, (3 * j + 2) * DC:(3 * j + 3) * DC, s_],
                            in0=M.MT[:, l, (3 * j + 2) * DC:(3 * j + 3) * DC, s_], scalar1=0.5, scalar2=None,
                            op0=ALU.mult)), waits=mt_evs)
                        mt_evs.append(ev)
        M.mt_ev = mt_evs[-8:] + [mt_evs[-1]]
        M.mt_ev = [mt_evs[-1]]

    def mvec(M, l, m, dc, s_):
        DC = M.c.DC
        return M.MT[:, l, m * DC + dc, s_:s_ + 1]

    def load_x(M, src, b, waits=(), xoff=None, coff=None):
        c, P = M.c, M.P
        fr = _flat([list(M.x_ev.values()), list(waits), M.mt_ev])
        sv = src.rearrange("(dc p) t -> p dc t", p=128)
        coff = c.S if coff is None else coff
        XB, CB, TBT = c.XB, c.CB, c.TBT
        if xoff is None:
            evs = [P.dma('sp', M.xres[:, :, 0:XB], sv[:, :, b * XB:(b + 1) * XB], M.xld, fr)]
        else:
            evs = [P.dmaf('sp', (lambda e: e.dma_start(out=M.xres[:, :, 0:XB], in_=sv[:, :, b * XB:b * XB + xoff.span + XB][:, :, bass.ds(xoff(e), XB)])), M.xld, fr)]
        if CB > 0:
            evs.append(P.dma('sp', M.xres[:, :, XB:TBT], sv[:, :, coff + b * CB:coff + (b + 1) * CB], M.xld, fr))
        for si, _ in enumerate(c.blk_subs()):
            for dc in range(c.DC):
                M.x_ev[(dc, si)] = list(evs)

    def store_x(M, dst, b, with_ctx=True, xs_only_cols=None):
        c, P = M.c, M.P
        evs = _flat([list(M.x_ev.values())])
        dv = dst.rearrange("(dc p) t -> p dc t", p=128)
        out = [P.dma('sp', dv[:, :, b * c.XB:(b + 1) * c.XB], M.xres[:, :, 0:c.XB], M.xst, evs)]
        if with_ctx and c.CB > 0:
            out.append(P.dma('sp', dv[:, :, c.S + b * c.CB:c.S + (b + 1) * c.CB], M.xres[:, :, c.XB:c.TBT], M.xst, evs))
        for k in M.x_ev:
            M.x_ev[k] = _flat([M.x_ev[k], out])
        return out

    def adaln(M, l, j):
        c, P = M.c, M.P
        DC = c.DC
        h_ev = {}
        for si, (t0, n, s_) in enumerate(c.blk_subs()):
            bk, bfree = P.bank()
            last = None
            for dc in range(DC):
                qi, qfree = M.sqring.get()
                ev = P.op('act', (lambda e, qi=qi, dc=dc, t0=t0, n=n: e.activation(
                    out=M.sq[:, qi, 0:n], in_=M.xres[:, dc, t0:t0 + n], func=AF.Square)),
                    waits=[qfree, M.x_ev[(dc, si)]])
                last = P.op('pe', (lambda e, qi=qi, dc=dc, n=n, bk=bk: e.matmul(
                    P.ps[:, bk, 0:n], lhsT=M.onesD[:], rhs=M.sq[:, qi, 0:n], start=(dc == 0), stop=(dc == DC - 1))),
                    waits=[ev, bfree, M.const_ev])
                M.sqring.rel(qi, last)
            ev_q = P.op('act', (lambda e, bk=bk, t0=t0, n=n: e.activation(
                out=M.rstd[:, t0:t0 + n], in_=P.ps[:, bk, 0:n], func=AF.Sqrt, bias=M.epsb[:, 0:1], scale=1.0)),
                waits=[last, M.rstd_rd.get(si), M.const_ev])
            P.bank_release(bk, ev_q)
            ev_r = P.op('dve', (lambda e, t0=t0, n=n: e.reciprocal(
                out=M.rstd[:, t0:t0 + n], in_=M.rstd[:, t0:t0 + n])), waits=[ev_q])
            evs = []
            for dc in range(DC):
                ti, tfree = M.silring.get()
                ev = P.op('dve', (lambda e, ti=ti, dc=dc, t0=t0, n=n, s_=s_: e.scalar_tensor_tensor(
                    out=M.sil[:, ti, 0:n], in0=M.xres[:, dc, t0:t0 + n], scalar=M.mvec(l, 3 * j + 1, dc, s_),
                    in1=M.rstd[:, t0:t0 + n], op0=ALU.mult, op1=ALU.mult)),
                    waits=[tfree, ev_r, M.x_ev[(dc, si)], M.mt_ev])
                ev2 = P.op('act', (lambda e, ti=ti, dc=dc, t0=t0, n=n, s_=s_: e.activation(
                    out=M.hbuf[:, dc, t0:t0 + n], in_=M.sil[:, ti, 0:n], func=AF.Identity,
                    bias=M.mvec(l, 3 * j, dc, s_), scale=1.0)), waits=[ev, M.h_rd])
                M.silring.rel(ti, ev2)
                evs.append(ev2)
            M.rstd_rd[si] = evs[-1:]
            M.rstd_rd[si] = [ev]
            h_ev[si] = evs
        return h_ev

    def ffn(M, l, j, w_in, w_out, h_ev):
        c, P = M.c, M.P
        DC, FF, FG = c.DC, c.FF, c.FG
        NG = FF // FG
        FJ = FG // 128
        subs = c.blk_subs()
        rd = []
        for g in range(NG):
            slot, fr = M.wslot()
            win = M.wbuf[:, slot, 0:DC * 2 * FG].rearrange("p (kc n) -> p kc n", kc=DC)
            wo = M.wbuf[:, slot, DC * 2 * FG:DC * 2 * FG + FJ * c.D].rearrange("p (kc n) -> p kc n", kc=FJ)
            wev = [M.load_w(win[:, :, 0:FG], w_in[:, g * FG:(g + 1) * FG], slot, fr),
                   M.load_w(win[:, :, FG:2 * FG], w_in[:, FF + g * FG:FF + (g + 1) * FG], slot, fr),
                   M.load_w(wo, w_out[g * FG:(g + 1) * FG, :], slot, fr)]
            ai, afree = M.actring.get()
            act_ev = {}
            last_pe = None
            for si, (t0, n, s_) in enumerate(subs):
                for jf in range(FJ):
                    ba, fa = P.bank()
                    for kc in range(DC):
                        la = P.op('pe', (lambda e, ba=ba, kc=kc, jf=jf, t0=t0, n=n, win=win: e.matmul(
                            P.ps[:, ba, 0:n], lhsT=win[:, kc, jf * 128:(jf + 1) * 128], rhs=M.hbuf[:, kc, t0:t0 + n],
                            start=(kc == 0), stop=(kc == DC - 1))), waits=[wev, fa, h_ev[si]], sig=(kc == DC - 1))
                    bg, fg_ = P.bank()
                    for kc in range(DC):
                        lg = P.op('pe', (lambda e, bg=bg, kc=kc, jf=jf, t0=t0, n=n, win=win: e.matmul(
                            P.ps[:, bg, 0:n], lhsT=win[:, kc, FG + jf * 128:FG + (jf + 1) * 128],
                            rhs=M.hbuf[:, kc, t0:t0 + n], start=(kc == 0), stop=(kc == DC - 1))),
                            waits=[fg_], sig=(kc == DC - 1))
                    ti, tfree = M.silring.get()
                    e1 = P.op('act', (lambda e, ba=ba, ti=ti, n=n: e.activation(
                        out=M.sil[:, ti, 0:n], in_=P.ps[:, ba, 0:n], func=AF.Silu)), waits=[la, tfree])
                    P.bank_release(ba, e1)
                    e2 = P.op('dve', (lambda e, bg=bg, ti=ti, n=n, ai=ai, jf=jf, t0=t0: e.tensor_tensor(
                        out=M.actb[:, ai, jf, t0:t0 + n], in0=M.sil[:, ti, 0:n], in1=P.ps[:, bg, 0:n], op=ALU.mult)),
                        waits=[lg, e1, afree])
                    P.bank_release(bg, e2)
                    M.silring.rel(ti, e2)
                    act_ev[(si, jf)] = e2
                    last_pe = lg
            rd.append(last_pe)
            arel = []
            for si, (t0, n, s_) in enumerate(subs):
                for dc in range(DC):
                    bo, fo = P.bank()
                    for jf in range(FJ):
                        lo = P.op('pe', (lambda e, bo=bo, jf=jf, dc=dc, t0=t0, n=n, wo=wo, ai=ai: e.matmul(
                            P.ps[:, bo, 0:n], lhsT=wo[:, jf, dc * 128:(dc + 1) * 128], rhs=M.actb[:, ai, jf, t0:t0 + n],
                            start=(jf == 0), stop=(jf == FJ - 1))), waits=[fo, act_ev[(si, jf)]], sig=(jf == FJ - 1))
                    ex = P.op('dve', (lambda e, bo=bo, dc=dc, t0=t0, n=n, s_=s_: e.scalar_tensor_tensor(
                        out=M.xres[:, dc, t0:t0 + n], in0=P.ps[:, bo, 0:n], scalar=M.mvec(l, 3 * j + 2, dc, s_),
                        in1=M.xres[:, dc, t0:t0 + n], op0=ALU.mult, op1=ALU.add)),
                        waits=[lo, M.x_ev[(dc, si)], M.mt_ev])
                    P.bank_release(bo, ex)
                    M.x_ev[(dc, si)] = [ex]
                    arel = lo
            M.wring.rel(slot, arel)
            M.actring.rel(ai, arel)
        M.h_rd = _flat([rd])

    def outproj(M, l, YT, w_out, b, y_wr, xoff=None, coff=None):
        c, P = M.c, M.P
        DC = c.DC
        subs = c.blk_subs()
        yv = YT.rearrange("(kc p) t -> p kc t", p=128)
        rd = []
        for kh in range(2):
            fr = _flat([M.h_rd, rd])
            XB, CB, TBT = c.XB, c.CB, c.TBT
            cof = c.S if coff is None else coff
            if xoff is None:
                yev = [P.dma('sp', M.hbuf[:, :, 0:XB], yv[:, kh * DC:(kh + 1) * DC, b * XB:(b + 1) * XB], M.yld, [fr, y_wr])]
            else:
                yev = [P.dmaf('sp', (lambda e, kh=kh: e.dma_start(out=M.hbuf[:, :, 0:XB], in_=YT[kh * c.D:(kh + 1) * c.D, b * XB:b * XB + xoff.span + XB].rearrange("(kc p) t -> p kc t", p=128)[:, :, bass.ds(xoff(e), XB)])),
                              M.yld, [fr, y_wr])]
            if CB > 0:
                yev.append(P.dma('sp', M.hbuf[:, :, XB:TBT], yv[:, kh * DC:(kh + 1) * DC, cof + b * CB:cof + (b + 1) * CB], M.yld, [fr, y_wr]))
            M.yt_ld += yev
            NS = c.D // 512 if c.D >= 512 else 1
            CW = min(512, c.D)
            for ds in range(NS):
                slot, wfr = M.wslot()
                wt = M.wbuf[:, slot, 0:DC * CW].rearrange("p (kc n) -> p kc n", kc=DC)
                wev = M.load_w(wt, w_out[kh * c.D:(kh + 1) * c.D, ds * CW:(ds + 1) * CW], slot, wfr)
                lo = None
                for si, (t0, n, s_) in enumerate(subs):
                    for dcl in range(CW // 128):
                        dc = ds * (CW // 128) + dcl
                        bo, fo = P.bank()
                        for kc in range(DC):
                            lo = P.op('pe', (lambda e, bo=bo, kc=kc, dcl=dcl, t0=t0, n=n, wt=wt: e.matmul(
                                P.ps[:, bo, 0:n], lhsT=wt[:, kc, dcl * 128:(dcl + 1) * 128], rhs=M.hbuf[:, kc, t0:t0 + n],
                                start=(kc == 0), stop=(kc == DC - 1))), waits=[fo, wev, yev], sig=(kc == DC - 1))
                        ex = P.op('dve', (lambda e, bo=bo, dc=dc, t0=t0, n=n, s_=s_: e.scalar_tensor_tensor(
                            out=M.xres[:, dc, t0:t0 + n], in0=P.ps[:, bo, 0:n], scalar=M.mvec(l, 5, dc, s_),
                            in1=M.xres[:, dc, t0:t0 + n], op0=ALU.mult, op1=ALU.add)),
                            waits=[lo, M.x_ev[(dc, si)], M.mt_ev])
                        P.bank_release(bo, ex)
                        M.x_ev[(dc, si)] = [ex]
                M.wring.rel(slot, lo)
                rd = [lo]
        M.h_rd = rd

    def store_h(M, HT, b, h_ev, waits):
        c, P = M.c, M.P
        evs = _flat([list(h_ev.values())])
        hv = HT.rearrange("(dc p) t -> p dc t", p=128)
        o1 = P.dma('sp', hv[:, :, b * c.XB:(b + 1) * c.XB], M.hbuf[:, :, 0:c.XB], M.xst, [evs, waits])
        o2 = P.dma('sp', hv[:, :, c.S + b * c.CB:c.S + (b + 1) * c.CB], M.hbuf[:, :, c.XB:c.TBT], M.xst, [evs, waits])
        M.h_rd = _flat([M.h_rd, o1, o2])
        return [o1, o2]

    def finish(M, evs):
        M.P.q['sp'].append((None, M.P._w('sp', evs), None, 0))

    def run(M):
        P = M.P
        nc = M.nc
        with nc.Block() as block:
            def mkf(name):
                def f(e):
                    for (fn, w, ev, inc) in P.q[name]:
                        for (k, v) in w:
                            e.wait_ge(P.sem[k], v)
                        if fn is None:
                            continue
                        ins = fn(e)
                        if ev is not None:
                            ins.then_inc(P.sem[ev[0]], inc)
                return f
            block.tensor(mkf('pe'))
            block.scalar(mkf('act'))
            block.vector(mkf('dve'))
            block.gpsimd(mkf('pool'))
            block.sync(mkf('sp'))

    def mixer_setup(M):
        c, P = M.c, M.P
        M.stg = P.sb("stg", [128, 4, 512], BF16)
        M.stgring = Ring(4)
        M.stgds = [P.dsem() for _ in range(4)]
        M.onesH = P.sb("onesH", [128, 128], BF16)
        M.const_ev.append(P.op('pool', lambda e: e.memset(M.onesH[:], 1.0 / 128)))
        M.ones1 = P.sb("ones1", [128, 128], BF16)
        M.const_ev.append(P.op('pool', lambda e: e.memset(M.ones1[:], 1.0)))
        M.hmds = P.dsem()

    def stage(M):
        i, fr = M.stgring.get()
        return i, fr

    def stage_dma(M, i, dst, src_ap, ev, extra=()):
        o = M.P.dma('sp', dst, src_ap, M.stgds[i], [ev, list(extra)])
        M.stgring.rel(i, o)
        return o

    def load_hm(M, HT, b, HL, waits):
        c, P = M.c, M.P
        fr = _flat([M.h_rd, list(waits)])
        hv = HT.rearrange("(dc p) t -> p dc t", p=128)
        xo = 0
        co = c.XB + 2 * HL
        W = c.TBT + 4 * HL
        ez = P.op('pool', lambda e: e.memset(M.hbuf[:, :, 0:W], 0.0), waits=fr)
        a0 = max(0, b * c.XB - HL)
        a1 = min(c.S, (b + 1) * c.XB + HL)
        e1 = P.dma('sp', M.hbuf[:, :, xo + HL - (b * c.XB - a0): xo + HL - (b * c.XB - a0) + (a1 - a0)], hv[:, :, a0:a1], M.hmds, [ez])
        g0 = max(0, b * c.CB - HL)
        g1 = min(c.CL, (b + 1) * c.CB + HL)
        e2 = P.dma('sp', M.hbuf[:, :, co + HL - (b * c.CB - g0): co + HL - (b * c.CB - g0) + (g1 - g0)], hv[:, :, c.S + g0:c.S + g1], M.hmds, [ez])
        return [e1, e2], xo + HL, co + HL

    def odd_consts(M, qgL, kgL, scwL, n_odd):
        c, P = M.c, M.P
        M.qk_g = P.sb("qk_g", [128, 2, n_odd], F32)
        M.scw = P.sb("scw", [128, n_odd, c.DC * 3], F32)
        ds_ = P.dsem()
        e1 = P.dma('sp', M.qk_g[:, 0, :], qgL, ds_)
        e2 = P.dma('sp', M.qk_g[:, 1, :], kgL, ds_)
        e3 = P.dma('sp', M.scw[:], scwL.rearrange("j p f -> p j f"), ds_)
        ev = P.op('dve', lambda e: e.tensor_scalar(out=M.qk_g[:, 0, :], in0=M.qk_g[:, 0, :], scalar1=128 ** -0.5,
                                                   scalar2=None, op0=ALU.mult), waits=[e1, e2, e3])
        M.odd_ev = [ev, e1, e2, e3]

    def odd_in(M, jl, HT, w_in, QT, KT, VT, YT, h_wr, dst_free):
        c, P = M.c, M.P
        D, DC = c.D, c.DC
        HL = 1
        wr = []
        rd_all = []
        for b in range(c.NB):
            hev, xc0, cc0 = M.load_hm(HT, b, HL, h_wr)
            rd_all += hev
            subsA = [(xc0 + t, n, b * c.XB + t) for (t, n) in subs_of(c.XB, 510)] + \
                    [(cc0 + t, n, c.S + b * c.CB + t) for (t, n) in subs_of(c.CB, 510)]
            last_rd = None
            for fam, dst in enumerate([QT, KT]):
                CW = min(512, D)
                for sl in range(D // CW):
                    slot, wfr = M.wslot()
                    wt = M.wbuf[:, slot, 0:DC * CW].rearrange("p (kc n) -> p kc n", kc=DC)
                    wev = M.load_w(wt, w_in[:, fam * D + sl * CW: fam * D + (sl + 1) * CW], slot, wfr)
                    for hh in range(CW // 128):
                        head = sl * (CW // 128) + hh
                        for (col0, n, tok0) in subsA:
                            bq, fq = P.bank()
                            for kc in range(DC):
                                lq = P.op('pe', (lambda e, bq=bq, kc=kc, hh=hh, col0=col0, n=n, wt=wt: e.matmul(
                                    P.ps[:, bq, 0:n], lhsT=wt[:, kc, hh * 128:(hh + 1) * 128], rhs=M.hbuf[:, kc, col0:col0 + n],
                                    start=(kc == 0), stop=(kc == DC - 1))), waits=[fq, wev, hev], sig=(kc == DC - 1))
                            qi, qfree = M.sqring.get()
                            es = P.op('act', (lambda e, bq=bq, qi=qi, n=n: e.activation(
                                out=M.sq[:, qi, 0:n], in_=P.ps[:, bq, 0:n], func=AF.Square)), waits=[lq, qfree])
                            br, frr = P.bank()
                            lr = P.op('pe', (lambda e, br=br, qi=qi, n=n: e.matmul(
                                P.ps[:, br, 0:n], lhsT=M.onesH[:], rhs=M.sq[:, qi, 0:n], start=True, stop=True)),
                                waits=[es, frr, M.const_ev])
                            M.sqring.rel(qi, lr)
                            ti, tfree = M.tmpring.get()
                            e1 = P.op('act', (lambda e, br=br, ti=ti, n=n: e.activation(
                                out=M.tmp[:, ti, 0:n], in_=P.ps[:, br, 0:n], func=AF.Sqrt, bias=M.epsb[:, 0:1], scale=1.0)),
                                waits=[lr, tfree])
                            P.bank_release(br, e1)
                            e2 = P.op('dve', (lambda e, ti=ti, n=n: e.reciprocal(out=M.tmp[:, ti, 0:n], in_=M.tmp[:, ti, 0:n])),
                                      waits=[e1])
                            gi, gfree = M.stage()
                            e3 = P.op('dve', (lambda e, bq=bq, ti=ti, gi=gi, n=n, fam=fam: e.scalar_tensor_tensor(
                                out=M.stg[:, gi, 0:n], in0=P.ps[:, bq, 0:n], scalar=M.qk_g[:, fam, jl:jl + 1],
                                in1=M.tmp[:, ti, 0:n], op0=ALU.mult, op1=ALU.mult)), waits=[e2, gfree, M.odd_ev])
                            P.bank_release(bq, e3)
                            M.tmpring.rel(ti, e3)
                            wr.append(M.stage_dma(gi, dst[head * 128:(head + 1) * 128, tok0:tok0 + n], M.stg[:, gi, 0:n], e3, dst_free))
                            last_rd = lq
                    M.wring.rel(slot, last_rd)
            tiles = [(xc0 + t, n, b * c.XB + t) for (t, n) in subs_of(c.XB, 128)] + \
                    [(cc0 + t, n, c.S + b * c.CB + t) for (t, n) in subs_of(c.CB, 128)]
            CW = min(512, D)
            for sl in range(D // CW):
                slot, wfr = M.wslot()
                wt = M.wbuf[:, slot, 0:DC * CW].rearrange("p (kc n) -> p kc n", kc=DC)
                wev = M.load_w(wt, w_in[:, 2 * D + sl * CW: 2 * D + (sl + 1) * CW], slot, wfr)
                for (col0, m, tok0) in tiles:
                    bv, fv = P.bank()
                    for kc in range(DC):
                        lv = P.op('pe', (lambda e, bv=bv, kc=kc, col0=col0, m=m, wt=wt, CW=CW: e.matmul(
                            P.ps[0:m, bv, 0:CW], lhsT=M.hbuf[:, kc, col0:col0 + m], rhs=wt[:, kc, 0:CW],
                            start=(kc == 0), stop=(kc == DC - 1))), waits=[fv, wev, hev], sig=(kc == DC - 1))
                    gi, gfree = M.stage()
                    e3 = P.op('act', (lambda e, bv=bv, gi=gi, m=m, CW=CW: e.activation(
                        out=M.stg[0:m, gi, 0:CW], in_=P.ps[0:m, bv, 0:CW], func=AF.Copy)), waits=[lv, gfree])
                    P.bank_release(bv, e3)
                    wr.append(M.stage_dma(gi, VT[tok0:tok0 + m, sl * CW:(sl + 1) * CW], M.stg[0:m, gi, 0:CW], e3, dst_free))
                    last_rd = lv
                M.wring.rel(slot, last_rd)
            GW_ = 256 if D >= 256 else D
            for gq in range(D // GW_):
                slot, wfr = M.wslot()
                wt = M.wbuf[:, slot, 0:DC * 3 * GW_].rearrange("p (kc n) -> p kc n", kc=DC)
                wev = [M.load_w(wt[:, :, 0:GW_], w_in[:, 4 * D + gq * GW_:4 * D + (gq + 1) * GW_], slot, wfr),
                       M.load_w(wt[:, :, GW_:2 * GW_], w_in[:, 5 * D + gq * GW_:5 * D + (gq + 1) * GW_], slot, wfr),
                       M.load_w(wt[:, :, 2 * GW_:3 * GW_], w_in[:, 3 * D + gq * GW_:3 * D + (gq + 1) * GW_], slot, wfr)]
                for cc in range(GW_ // 128):
                    ch = gq * (GW_ // 128) + cc
                    for (col0, n, tok0) in subsA:
                        n2 = n + 2
                        banks = []
                        lasts = []
                        for fi in range(3):
                            bb, fb = P.bank()
                            lo_, nn = (col0 - 1, n2) if fi < 2 else (col0, n)
                            for kc in range(DC):
                                lx = P.op('pe', (lambda e, bb=bb, kc=kc, fi=fi, cc=cc, lo_=lo_, nn=nn, wt=wt: e.matmul(
                                    P.ps[:, bb, 0:nn], lhsT=wt[:, kc, fi * GW_ + cc * 128: fi * GW_ + (cc + 1) * 128],
                                    rhs=M.hbuf[:, kc, lo_:lo_ + nn], start=(kc == 0), stop=(kc == DC - 1))),
                                    waits=[fb, wev, hev], sig=(kc == DC - 1))
                            banks.append(bb)
                            lasts.append(lx)
                        t1, f1 = M.tmpring.get()
                        ea = P.op('act', (lambda e, b0=banks[0], t1=t1, n2=n2: e.activation(
                            out=M.tmp[:, t1, 0:n2], in_=P.ps[:, b0, 0:n2], func=AF.Copy)), waits=[lasts[0], f1])
                        P.bank_release(banks[0], ea)
                        t2, f2 = M.tmpring.get()
                        eb = P.op('dve', (lambda e, b1=banks[1], t1=t1, t2=t2, n2=n2: e.tensor_tensor(
                            out=M.tmp[:, t2, 0:n2], in0=M.tmp[:, t1, 0:n2], in1=P.ps[:, b1, 0:n2], op=ALU.mult)),
                            waits=[ea, lasts[1], f2])
                        P.bank_release(banks[1], eb)
                        M.tmpring.rel(t1, eb)
                        t3, f3 = M.tmpring.get()
                        w_ = lambda k, ch=ch: M.scw[:, jl, ch * 3 + k: ch * 3 + k + 1]
                        ec = P.op('dve', (lambda e, t2=t2, t3=t3, n=n, w_=w_: e.tensor_scalar(
                            out=M.tmp[:, t3, 0:n], in0=M.tmp[:, t2, 0:n], scalar1=w_(0), scalar2=None, op0=ALU.mult)),
                            waits=[eb, f3, M.odd_ev])
                        for k in (1, 2):
                            ec = P.op('dve', (lambda e, t2=t2, t3=t3, n=n, k=k, w_=w_: e.scalar_tensor_tensor(
                                out=M.tmp[:, t3, 0:n], in0=M.tmp[:, t2, k:k + n], scalar=w_(k), in1=M.tmp[:, t3, 0:n],
                                op0=ALU.mult, op1=ALU.add)), waits=[ec])
                        M.tmpring.rel(t2, ec)
                        gi, gfree = M.stage()
                        ed = P.op('dve', (lambda e, b2=banks[2], t3=t3, gi=gi, n=n: e.tensor_tensor(
                            out=M.stg[:, gi, 0:n], in0=M.tmp[:, t3, 0:n], in1=P.ps[:, b2, 0:n], op=ALU.mult)),
                            waits=[ec, lasts[2], gfree])
                        P.bank_release(banks[2], ed)
                        M.tmpring.rel(t3, ed)
                        wr.append(M.stage_dma(gi, YT[D + ch * 128:D + (ch + 1) * 128, tok0:tok0 + n], M.stg[:, gi, 0:n], ed, dst_free))
                        last_rd = lasts[2]
                M.wring.rel(slot, last_rd)
            M.h_rd = [last_rd]
        return wr

    def odd_att(M, jl, QT, KT, VT, YT, nabias, namask, wr_ev, y_free, bias_jl=None):
        c, P = M.c, M.P
        D, S, CL, T = c.D, c.S, c.CL, c.T
        NH = D // 128
        NT, NC = S // 128, CL // 128
        NQ = NT + NC
        BW = 5 * 5 * 128
        W2 = T // 2
        off = [0]

        def carve(words):
            a = off[0]
            off[0] += words
            return M.arena[:, a:a + words]
        ktv = [carve(W2).bitcast(BF16) for _ in range(2)]
        qtv = [carve(W2).bitcast(BF16) for _ in range(2)]
        vv = [carve(NQ * 64).bitcast(BF16).rearrange("p (t c) -> p t c", c=128) for _ in range(2)]
        otv = carve(W2).bitcast(BF16)
        sv = [carve(640).rearrange("p (i q) -> p i q", q=128) for _ in range(2)]
        ptv = [carve(64 * (5 + NC)).bitcast(BF16).rearrange("p (i q) -> p i q", q=128) for _ in range(2)]
        assert off[0] <= M.AW, off[0]
        maskc = M.wbuf[:, 0, 0:2 * BW].bitcast(F32)
        biasb = M.wbuf[:, 1, 0:2 * BW].bitcast(F32).rearrange("p (a i q) -> p a i q", a=5, i=5)
        ads_m, ads_b, ads_o = P.dsem(), P.dsem(), P.dsem()
        ads_h = [P.dsem(), P.dsem()]
        arena_free = _flat([list(M.x_ev.values())])
        wfree = _flat([M.wring.free[0], M.wring.free[1]])
        em = P.dma('sp', maskc, namask, ads_m, wfree)
        slot_rd = [[], []]
        bias_rd = []
        ot_rd = []
        sring, pring = Ring(2), Ring(2)
        last_pe = None
        outs = []
        for h in range(NH):
            hs = h % 2
            fr = _flat([slot_rd[hs], arena_free, wr_ev])
            e1 = P.dma('sp', ktv[hs], KT[h * 128:(h + 1) * 128, :], ads_h[hs], fr)
            e2 = P.dma('sp', qtv[hs], QT[h * 128:(h + 1) * 128, :], ads_h[hs], fr)
            e3 = P.dma('sp', vv[hs], VT[:, h * 128:(h + 1) * 128].rearrange("(t p) c -> p t c", p=128), ads_h[hs], fr)
            e4 = P.dma('sp', biasb.rearrange("p a i q -> p (a i q)"), nabias[jl if bias_jl is None else bias_jl, h], ads_b, [bias_rd, wfree])
            hev = [e1, e2, e3]
            eb = P.op('dve', lambda e: e.tensor_tensor(out=biasb.rearrange("p a i q -> p (a i q)"),
                                                        in0=biasb.rearrange("p a i q -> p (a i q)"), in1=maskc, op=ALU.add),
                      waits=[e4, em])
            kt, qt, v = ktv[hs], qtv[hs], vv[hs]
            o_evs = []
            for j in range(NQ):
                isx = j < NT
                if isx:
                    a0 = min(max(j - 2, 0), NT - 5)
                    pat = 0 if j == 0 else 1 if j == 1 else 3 if j == NT - 2 else 4 if j == NT - 1 else 2
                    ktiles = [a0 + i for i in range(5)] + [NT + i for i in range(NC)]
                else:
                    ktiles = [NT + i for i in range(NC)]
                nk = len(ktiles)
                bA, fA = P.bank()
                bB, fB = P.bank()
                lA = lB = None
                for i, kt_i in enumerate(ktiles):
                    bk_, cc_ = (bA, i) if i < 4 else (bB, i - 4)
                    ev = P.op('pe', (lambda e, bk_=bk_, cc_=cc_, kt_i=kt_i, j=j, kt=kt, qt=qt: e.matmul(
                        P.ps[:, bk_, cc_ * 128:(cc_ + 1) * 128], lhsT=kt[:, kt_i * 128:(kt_i + 1) * 128],
                        rhs=qt[:, j * 128:(j + 1) * 128], start=True, stop=True)), waits=[fA, fB, hev],
                        sig=(i == min(3, nk - 1) or i == nk - 1))
                    if i < 4:
                        lA = ev
                    else:
                        lB = ev
                pi, pfree = pring.get()
                pt = ptv[pi]
                if isx:
                    si_, sfree = sring.get()
                    sb_ = sv[si_]
                    d1 = P.op('dve', (lambda e, bA=bA, sb_=sb_, pat=pat: e.tensor_tensor(
                        out=sb_[:, 0:4, :], in0=P.ps[:, bA, 0:512].rearrange("p (i q) -> p i q", q=128),
                        in1=biasb[:, pat, 0:4, :], op=ALU.add)), waits=[lA, sfree, eb])
                    d2 = P.op('dve', (lambda e, bB=bB, sb_=sb_, pat=pat: e.tensor_tensor(
                        out=sb_[:, 4, :], in0=P.ps[:, bB, 0:128], in1=biasb[:, pat, 4, :], op=ALU.add)), waits=[lB])
                    P.bank_release(bA, d1)
                    a1 = P.op('act', (lambda e, sb_=sb_, pt=pt: e.activation(out=pt[:, 0:5, :], in_=sb_[:, :, :], func=AF.Exp)),
                              waits=[d1, d2, pfree])
                    sring.rel(si_, a1)
                    a2 = P.op('act', (lambda e, bB=bB, pt=pt: e.activation(
                        out=pt[:, 5:5 + NC, :], in_=P.ps[:, bB, 128:128 + NC * 128].rearrange("p (i q) -> p i q", q=128),
                        func=AF.Exp)), waits=[lB])
                    P.bank_release(bB, [d2, a2])
                    pev = [a1, a2]
                else:
                    a1 = P.op('act', (lambda e, bA=bA, pt=pt, nk=nk: e.activation(
                        out=pt[:, 0:nk, :], in_=P.ps[:, bA, 0:nk * 128].rearrange("p (i q) -> p i q", q=128), func=AF.Exp)),
                        waits=[lA, pfree])
                    P.bank_release(bA, a1)
                    P.bank_release(bB, [])
                    pev = [a1]
                bC, fC = P.bank()
                for part in range(2):
                    for i, kt_i in enumerate(ktiles):
                        lc = P.op('pe', (lambda e, bC=bC, i=i, kt_i=kt_i, part=part, pt=pt, v=v, nk=nk: e.matmul(
                            P.ps[:, bC, part * 128:(part + 1) * 128], lhsT=(v[:, kt_i, :] if part == 0 else M.ones1[:]),
                            rhs=pt[:, i, :], start=(i == 0), stop=(i == nk - 1))), waits=[fC, pev, M.const_ev],
                            sig=(part == 1 and i == nk - 1))
                pring.rel(pi, lc)
                ti, tfree = M.tmpring.get()
                r1 = P.op('dve', (lambda e, bC=bC, ti=ti: e.reciprocal(out=M.tmp[:, ti, 0:128], in_=P.ps[:, bC, 128:256])),
                          waits=[lc, tfree])
                r2 = P.op('dve', (lambda e, bC=bC, ti=ti, j=j: e.tensor_tensor(
                    out=otv[:, j * 128:(j + 1) * 128], in0=P.ps[:, bC, 0:128], in1=M.tmp[:, ti, 0:128], op=ALU.mult)),
                    waits=[r1, ot_rd])
                P.bank_release(bC, r2)
                M.tmpring.rel(ti, r2)
                o_evs = [r2]
                last_pe = lc
            slot_rd[hs] = [last_pe]
            bias_rd = [o_evs[-1]]
            od = P.dma('sp', YT[h * 128:(h + 1) * 128, :], otv, ads_o, [o_evs, y_free])
            ot_rd = [od]
            outs.append(od)
        for k in M.x_ev:
            M.x_ev[k] = _flat([M.x_ev[k], outs, last_pe])
        M.wring.rel(0, [o_evs[-1]])
        M.wring.rel(1, [o_evs[-1]])
        return outs

    def even_consts(M, n_even, ins):
        c, P = M.c, M.P
        DC = c.DC
        XC = (c.D + 2 * 1024) // 128
        M.e_cw = P.sb("e_cw", [128, n_even, XC * 3], F32)
        M.e_cb = P.sb("e_cb", [128, n_even, XC], F32)
        M.e_dtb = P.sb("e_dtb", [128, n_even, 64], F32)
        M.e_aneg = P.sb("e_aneg", [128, n_even, 64], F32)
        M.e_dsk = P.sb("e_dsk", [128, n_even, 32], F32)
        M.e_cmw = P.sb("e_cmw", [128, n_even, DC * 31], F32)
        M.e_cmv = P.sb("e_cmv", [128, n_even, 3, DC], F32)
        M.e_tri = P.sb("e_tri", [128, 2, 128], F32)
        M.e_mask = P.sb("e_mask", [128, 2, 128], F32)
        M.e_id = P.sb("e_id", [128, 128], F32)
        M.e_onesf = P.sb("e_onesf", [128, 128], F32)
        M.e_oh = P.sb("e_oh", [4, 4 * 128], F32)
        M.e_one = P.sb("e_one", [128, 1], F32)
        ds_ = P.dsem()
        evs = []
        for dst, src in [(M.e_cw[:], ins['cw'].rearrange("j p f -> p j f")), (M.e_cb[:], ins['cb'].rearrange("j p f -> p j f")),
                         (M.e_dtb[:], ins['dtb'].rearrange("j p f -> p j f")), (M.e_aneg[:], ins['alog'].rearrange("j p f -> p j f")),
                         (M.e_dsk[:], ins['dsk'].rearrange("j p f -> p j f")), (M.e_cmw[:], ins['cmw'].rearrange("j p f -> p j f")),
                         (M.e_cmv[:, :, 0, :], ins['cmb'].rearrange("j p f -> p j f")), (M.e_cmv[:, :, 1, :], ins['cmg'].rearrange("j p f -> p j f")),
                         (M.e_cmv[:, :, 2, :], ins['cmbeta'].rearrange("j p f -> p j f")),
                         (M.e_tri[:], ins['tri'].rearrange("d p f -> p d f")), (M.e_mask[:], ins['maskfb'].rearrange("d p f -> p d f")),
                         (M.e_id[:], ins['ident']), (M.e_oh[:], ins['onehot'])]:
            evs.append(P.dma('sp', dst, src, ds_))
        e1 = P.op('act', lambda e: e.activation(out=M.e_aneg[:], in_=M.e_aneg[:], func=AF.Exp), waits=evs)
        e2 = P.op('dve', lambda e: e.tensor_scalar(out=M.e_aneg[:], in0=M.e_aneg[:], scalar1=-1.0, scalar2=None, op0=ALU.mult), waits=[e1])
        e4 = P.op('pool', lambda e: e.memset(M.e_onesf[:], 1.0))
        e5 = P.op('pool', lambda e: e.memset(M.e_one[:], 1.0))
        M.even_ev = _flat([evs, e2, e4, e5])
        M.ng_dram = ins['ng']

    def even_in(M, jl, HT, w_in, ZS, XS, BM, BT, CT, DT, YT, h_wr, dst_free):
        c, P = M.c, M.P
        D, DC = c.D, c.DC
        HL = 15
        NXC = D // 128
        NBC = 1024 // 128
        o_xbc = D
        o_dt = D + D + 2048
        o_ga = o_dt + 64
        o_gg = o_ga + D
        wr = []
        dtst = M.actb[:, 0, 0, 0:256].bitcast(F32).rearrange("p (a f) -> p a f", a=2)
        dtring = Ring(2)
        dtds = [P.dsem(), P.dsem()]
        a_ev = _flat([list(M.x_ev.values())])
        xcb = M.arena[:, 0:4 * 512].rearrange("p (c t) -> p c t", c=4)
        cv = M.arena[:, 0:DC * c.TBT].rearrange("p (dc t) -> p dc t", dc=DC)
        for b in range(c.NB):
            hev, xc0, cc0 = M.load_hm(HT, b, HL, h_wr)
            last_rd = None
            tiles = [(xc0 + t, n, b * c.XB + t) for (t, n) in subs_of(c.XB, 128)] + \
                    [(cc0 + t, n, c.S + b * c.CB + t) for (t, n) in subs_of(c.CB, 128)]
            CW = min(512, D)
            for sl in range(D // CW):
                slot, wfr = M.wslot()
                wt = M.wbuf[:, slot, 0:DC * CW].rearrange("p (kc n) -> p kc n", kc=DC)
                wev = M.load_w(wt, w_in[:, sl * CW:(sl + 1) * CW], slot, wfr)
                for (col0, m, tok0) in tiles:
                    bv, fv = P.bank()
                    for kc in range(DC):
                        lv = P.op('pe', (lambda e, bv=bv, kc=kc, col0=col0, m=m, wt=wt, CW=CW: e.matmul(
                            P.ps[0:m, bv, 0:CW], lhsT=M.hbuf[:, kc, col0:col0 + m], rhs=wt[:, kc, 0:CW],
                            start=(kc == 0), stop=(kc == DC - 1))), waits=[fv, wev, hev], sig=(kc == DC - 1))
                    gi, gfree = M.stage()
                    e3 = P.op('act', (lambda e, bv=bv, gi=gi, m=m, CW=CW: e.activation(
                        out=M.stg[0:m, gi, 0:CW], in_=P.ps[0:m, bv, 0:CW], func=AF.Silu)), waits=[lv, gfree])
                    P.bank_release(bv, e3)
                    wr.append(M.stage_dma(gi, ZS[tok0:tok0 + m, sl * CW:(sl + 1) * CW], M.stg[0:m, gi, 0:CW], e3, dst_free))
                    last_rd = lv
                M.wring.rel(slot, last_rd)
            slot, wfr = M.wslot()
            wt = M.wbuf[:, slot, 0:DC * 64].rearrange("p (kc n) -> p kc n", kc=DC)
            wev = M.load_w(wt, w_in[:, o_dt:o_dt + 64], slot, wfr)
            for (col0, m, tok0) in tiles:
                bv, fv = P.bank()
                for kc in range(DC):
                    lv = P.op('pe', (lambda e, bv=bv, kc=kc, col0=col0, m=m, wt=wt: e.matmul(
                        P.ps[0:m, bv, 0:64], lhsT=M.hbuf[:, kc, col0:col0 + m], rhs=wt[:, kc, 0:64],
                        start=(kc == 0), stop=(kc == DC - 1))), waits=[fv, wev, hev], sig=(kc == DC - 1))
                ti, tfree = M.tmpring.get()
                d1 = P.op('dve', (lambda e, bv=bv, ti=ti, m=m: e.tensor_tensor(
                    out=M.tmp[0:m, ti, 0:64], in0=P.ps[0:m, bv, 0:64], in1=M.e_dtb[0:m, jl, :], op=ALU.add)),
                    waits=[lv, tfree, M.even_ev])
                P.bank_release(bv, d1)
                d2 = P.op('act', (lambda e, ti=ti, m=m: e.activation(out=M.tmp[0:m, ti, 0:64], in_=M.tmp[0:m, ti, 0:64], func=AF.Exp)),
                          waits=[d1])
                di, dfree = dtring.get()
                d3 = P.op('act', (lambda e, ti=ti, di=di, m=m: e.activation(
                    out=dtst[0:m, di, :], in_=M.tmp[0:m, ti, 0:64], func=AF.Ln, bias=M.e_one[0:m, 0:1], scale=1.0)),
                    waits=[d2, dfree, M.actring.free[0], M.actring.free[1]])
                M.tmpring.rel(ti, d3)
                o = P.dma('sp', DT[tok0:tok0 + m, :], dtst[0:m, di, :], dtds[di], [d3, dst_free])
                dtring.rel(di, o)
                wr.append(o)
                M.actring.free[0] = _flat([M.actring.free[0], o])
                M.actring.free[1] = _flat([M.actring.free[1], o])
                last_rd = lv
            M.wring.rel(slot, last_rd)
            subsA = [(xc0 + t, n, b * c.XB + t) for (t, n) in subs_of(c.XB, 510)] + \
                    [(cc0 + t, n, c.S + b * c.CB + t) for (t, n) in subs_of(c.CB, 510)]
            NG4 = (NXC + 2 * NBC) // 4
            for g4 in range(NG4):
                slot, wfr = M.wslot()
                wt = M.wbuf[:, slot, 0:DC * 512].rearrange("p (kc n) -> p kc n", kc=DC)
                wev = M.load_w(wt, w_in[:, o_xbc + g4 * 512:o_xbc + (g4 + 1) * 512], slot, wfr)
                for (col0, n, tok0) in subsA:
                    xc_ev = []
                    for cc in range(4):
                        ch = g4 * 4 + cc
                        bb, fb = P.bank()
                        for kc in range(DC):
                            lx = P.op('pe', (lambda e, bb=bb, kc=kc, cc=cc, col0=col0, n=n, wt=wt: e.matmul(
                                P.ps[:, bb, 0:n + 2], lhsT=wt[:, kc, cc * 128:(cc + 1) * 128],
                                rhs=M.hbuf[:, kc, col0 - 1:col0 + n + 1], start=(kc == 0), stop=(kc == DC - 1))),
                                waits=[fb, wev, hev], sig=(kc == DC - 1))
                        last_rd = lx
                        t1, f1 = M.tmpring.get()
                        ea = P.op('act', (lambda e, bb=bb, t1=t1, n=n: e.activation(
                            out=M.tmp[:, t1, 0:n + 2], in_=P.ps[:, bb, 0:n + 2], func=AF.Copy)), waits=[lx, f1])
                        P.bank_release(bb, ea)
                        t2, f2 = M.tmpring.get()
                        w_ = lambda k, ch=ch: M.e_cw[:, jl, ch * 3 + k:ch * 3 + k + 1]
                        ec = P.op('dve', (lambda e, t1=t1, t2=t2, n=n, w_=w_, ch=ch: e.tensor_scalar(
                            out=M.tmp[:, t2, 0:n], in0=M.tmp[:, t1, 0:n], scalar1=w_(0), scalar2=M.e_cb[:, jl, ch:ch + 1],
                            op0=ALU.mult, op1=ALU.add)), waits=[ea, f2, M.even_ev])
                        for k in (1, 2):
                            ec = P.op('dve', (lambda e, t1=t1, t2=t2, n=n, k=k, w_=w_: e.scalar_tensor_tensor(
                                out=M.tmp[:, t2, 0:n], in0=M.tmp[:, t1, k:k + n], scalar=w_(k), in1=M.tmp[:, t2, 0:n],
                                op0=ALU.mult, op1=ALU.add)), waits=[ec])
                        M.tmpring.rel(t1, ec)
                        es = P.op('act', (lambda e, t2=t2, cc=cc, n=n: e.activation(
                            out=xcb[:, cc, 0:n], in_=M.tmp[:, t2, 0:n], func=AF.Silu)), waits=[ec, a_ev])
                        M.tmpring.rel(t2, es)
                        xc_ev.append(es)
                        if ch >= NXC:
                            gi, gfree = M.stage()
                            ef = P.op('dve', (lambda e, gi=gi, cc=cc, n=n: e.tensor_copy(out=M.stg[:, gi, 0:n], in_=xcb[:, cc, 0:n])),
                                      waits=[es, gfree])
                            dstT = BT if ch < NXC + NBC else CT
                            r0 = (ch - NXC) * 128 if ch < NXC + NBC else (ch - NXC - NBC) * 128
                            wr.append(M.stage_dma(gi, dstT[r0:r0 + 128, tok0:tok0 + n], M.stg[:, gi, 0:n], ef, dst_free))
                            xc_ev.append(ef)
                    if g4 * 4 < NXC + NBC:
                        last_t = None
                        for (tt, m) in subs_of(n, 128):
                            bt_, ft = P.bank()
                            for cc in range(4):
                                last_t = P.op('pe', (lambda e, bt_=bt_, cc=cc, tt=tt, m=m: e.transpose(
                                    P.ps[0:m, bt_, cc * 128:(cc + 1) * 128], xcb[:, cc, tt:tt + m], M.e_id[:])),
                                    waits=[ft, xc_ev, M.even_ev], sig=(cc == 3))
                            gi, gfree = M.stage()
                            ecp = P.op('act', (lambda e, bt_=bt_, gi=gi, m=m: e.activation(
                                out=M.stg[0:m, gi, 0:512], in_=P.ps[0:m, bt_, 0:512], func=AF.Copy)), waits=[last_t, gfree])
                            P.bank_release(bt_, ecp)
                            if g4 * 4 < NXC:
                                dd = XS[tok0 + tt:tok0 + tt + m, g4 * 512:(g4 + 1) * 512]
                            else:
                                dd = BM[tok0 + tt:tok0 + tt + m, (g4 * 4 - NXC) * 128:(g4 * 4 - NXC) * 128 + 512]
                            wr.append(M.stage_dma(gi, dd, M.stg[0:m, gi, 0:512], ecp, dst_free))
                        a_ev = _flat([xc_ev, last_t])
                    else:
                        a_ev = _flat([xc_ev])
                M.wring.rel(slot, last_rd)
            subsC = [(xc0 + t, n, b * c.XB + t, t) for (t, n) in subs_of(c.XB, 482)] + \
                    [(cc0 + t, n, c.S + b * c.CB + t, c.XB + t) for (t, n) in subs_of(c.CB, 482)]
            GC = 256
            cv_ev = {}
            for gq in range(D // GC):
                slot, wfr = M.wslot()
                wt = M.wbuf[:, slot, 0:DC * 2 * GC].rearrange("p (kc n) -> p kc n", kc=DC)
                wev = [M.load_w(wt[:, :, 0:GC], w_in[:, o_ga + gq * GC:o_ga + (gq + 1) * GC], slot, wfr),
                       M.load_w(wt[:, :, GC:2 * GC], w_in[:, o_gg + gq * GC:o_gg + (gq + 1) * GC], slot, wfr)]
                for cc in range(GC // 128):
                    ch = gq * (GC // 128) + cc
                    for (col0, n, tok0, bo) in subsC:
                        nn = n + 30
                        bks, ls = [], []
                        for fi in range(2):
                            bb, fb = P.bank()
                            for kc in range(DC):
                                lx = P.op('pe', (lambda e, bb=bb, kc=kc, fi=fi, cc=cc, col0=col0, nn=nn, wt=wt: e.matmul(
                                    P.ps[:, bb, 0:nn], lhsT=wt[:, kc, fi * GC + cc * 128:fi * GC + (cc + 1) * 128],
                                    rhs=M.hbuf[:, kc, col0 - 15:col0 - 15 + nn], start=(kc == 0), stop=(kc == DC - 1))),
                                    waits=[fb, wev, hev], sig=(kc == DC - 1))
                            bks.append(bb)
                            ls.append(lx)
                        last_rd = lx
                        t1, f1 = M.tmpring.get()
                        ea = P.op('act', (lambda e, b1=bks[1], t1=t1, nn=nn: e.activation(
                            out=M.tmp[:, t1, 0:nn], in_=P.ps[:, b1, 0:nn], func=AF.Sigmoid)), waits=[ls[1], f1])
                        P.bank_release(bks[1], ea)
                        eb = P.op('dve', (lambda e, b0=bks[0], t1=t1, nn=nn: e.tensor_tensor(
                            out=M.tmp[:, t1, 0:nn], in0=M.tmp[:, t1, 0:nn], in1=P.ps[:, b0, 0:nn], op=ALU.mult)),
                            waits=[ea, ls[0]])
                        P.bank_release(bks[0], eb)
                        w_ = lambda k, ch=ch: M.e_cmw[:, jl, ch * 31 + k:ch * 31 + k + 1]
                        t4, f4 = M.tmpring.get()
                        ec = P.op('dve', (lambda e, t1=t1, n=n, w_=w_, ch=ch, bo=bo: e.tensor_scalar(
                            out=cv[:, ch, bo:bo + n], in0=M.tmp[:, t1, 0:n], scalar1=w_(0), scalar2=M.e_cmv[:, jl, 0, ch:ch + 1],
                            op0=ALU.mult, op1=ALU.add)), waits=[eb, a_ev, M.even_ev])
                        ec = P.op('dve', (lambda e, t1=t1, t4=t4, n=n, w_=w_: e.tensor_scalar(
                            out=M.tmp[:, t4, 0:n], in0=M.tmp[:, t1, 1:1 + n], scalar1=w_(1), scalar2=None, op0=ALU.mult)),
                            waits=[f4])
                        for k in range(2, 31):
                            acc = (lambda ch=ch, bo=bo, n=n: cv[:, ch, bo:bo + n]) if k % 2 == 0 else (lambda t4=t4, n=n: M.tmp[:, t4, 0:n])
                            ec = P.op('dve', (lambda e, t1=t1, n=n, k=k, w_=w_, acc=acc: e.scalar_tensor_tensor(
                                out=acc(), in0=M.tmp[:, t1, k:k + n], scalar=w_(k), in1=acc(),
                                op0=ALU.mult, op1=ALU.add)), sig=(k == 30))
                        ec = P.op('dve', (lambda e, t4=t4, n=n, ch=ch, bo=bo: e.tensor_tensor(
                            out=cv[:, ch, bo:bo + n], in0=cv[:, ch, bo:bo + n], in1=M.tmp[:, t4, 0:n], op=ALU.add)), waits=[ec])
                        M.tmpring.rel(t1, ec)
                        M.tmpring.rel(t4, ec)
                        cv_ev[(ch, bo)] = ec
                M.wring.rel(slot, last_rd)
            M.h_rd = [last_rd]
            fin = []
            for (col0, n, tok0, bo) in subsC:
                b1, f1_ = P.bank()
                b2, f2_ = P.bank()
                l1 = l2 = None
                for ch in range(DC):
                    qi, qfree = M.sqring.get()
                    t1, f1 = M.tmpring.get()
                    e0 = P.op('act', (lambda e, t1=t1, ch=ch, bo=bo, n=n: e.activation(
                        out=M.tmp[:, t1, 0:n], in_=cv[:, ch, bo:bo + n], func=AF.Square)), waits=[cv_ev[(ch, bo)], f1])
                    l1 = P.op('pe', (lambda e, b1=b1, ch=ch, bo=bo, n=n: e.matmul(
                        P.ps[:, b1, 0:n], lhsT=M.e_onesf[:], rhs=cv[:, ch, bo:bo + n], start=(ch == 0), stop=(ch == DC - 1))),
                        waits=[f1_, cv_ev[(ch, bo)], M.even_ev], sig=(ch == DC - 1))
                    l2 = P.op('pe', (lambda e, b2=b2, t1=t1, ch=ch, n=n: e.matmul(
                        P.ps[:, b2, 0:n], lhsT=M.e_onesf[:], rhs=M.tmp[:, t1, 0:n], start=(ch == 0), stop=(ch == DC - 1))),
                        waits=[f2_, e0])
                    M.tmpring.rel(t1, l2)
                    M.sqring.rel(qi, [])
                tm, fm = 'm', M.ln_rd
                tv, fv_ = 'v', M.ln_rd
                m1 = P.op('act', (lambda e, b1=b1, tm=tm, n=n: e.activation(
                    out=M.rstd[:, 0:n], in_=P.ps[:, b1, 0:n], func=AF.Copy, scale=1.0 / D)), waits=[l1, fm])
                P.bank_release(b1, m1)
                m2 = P.op('dve', (lambda e, tm=tm, tv=tv, n=n: e.tensor_tensor(
                    out=M.rstd[:, 512:512 + n], in0=M.rstd[:, 0:n], in1=M.rstd[:, 0:n], op=ALU.mult)), waits=[m1, fv_])
                m3 = P.op('dve', (lambda e, b2=b2, tv=tv, n=n: e.scalar_tensor_tensor(
                    out=M.rstd[:, 512:512 + n], in0=P.ps[:, b2, 0:n], scalar=1.0 / D, in1=M.rstd[:, 512:512 + n],
                    op0=ALU.mult, op1=ALU.subtract)), waits=[m2, l2])
                P.bank_release(b2, m3)
                m4 = P.op('act', (lambda e, tv=tv, n=n: e.activation(
                    out=M.rstd[:, 512:512 + n], in_=M.rstd[:, 512:512 + n], func=AF.Sqrt, bias=M.epsb[:, 0:1], scale=1.0)), waits=[m3])
                m5 = P.op('dve', (lambda e, tv=tv, n=n: e.reciprocal(out=M.rstd[:, 512:512 + n], in_=M.rstd[:, 512:512 + n])), waits=[m4])
                lastv = None
                for ch in range(DC):
                    t3, f3 = M.tmpring.get()
                    v1 = P.op('dve', (lambda e, t3=t3, tm=tm, ch=ch, bo=bo, n=n: e.tensor_tensor(
                        out=M.tmp[:, t3, 0:n], in0=cv[:, ch, bo:bo + n], in1=M.rstd[:, 0:n], op=ALU.subtract)),
                        waits=[m5, f3])
                    v2 = P.op('dve', (lambda e, t3=t3, tv=tv, ch=ch, n=n: e.scalar_tensor_tensor(
                        out=M.tmp[:, t3, 0:n], in0=M.tmp[:, t3, 0:n], scalar=M.e_cmv[:, jl, 1, ch:ch + 1], in1=M.rstd[:, 512:512 + n],
                        op0=ALU.mult, op1=ALU.mult)), waits=[v1])
                    gi, gfree = M.stage()
                    v3 = P.op('act', (lambda e, t3=t3, gi=gi, ch=ch, n=n: e.activation(
                        out=M.stg[:, gi, 0:n], in_=M.tmp[:, t3, 0:n], func=AF.Silu, bias=M.e_cmv[:, jl, 2, ch:ch + 1], scale=1.0)),
                        waits=[v2, gfree])
                    M.tmpring.rel(t3, v3)
                    wr.append(M.stage_dma(gi, YT[D + ch * 128:D + (ch + 1) * 128, tok0:tok0 + n], M.stg[:, gi, 0:n], v3, dst_free))
                    lastv = v1
                M.ln_rd = [lastv, v2]
                fin += [lastv, l1]
            a_ev = _flat([fin])
        for k in M.x_ev:
            M.x_ev[k] = _flat([a_ev])
        return wr

    def even_ssd(M, jl, ZS, XS, BM, BT, CT, DT, YF, YT, in_wr, y_free):
        c, P = M.c, M.P
        D, S, CL = c.D, c.S, c.CL
        op = P.op
        NCX, NCC = S // 128, CL // 128
        NH, HP, NG = 32, 64, 8
        off = [0]

        def carve(words):
            a = off[0]
            off[0] += words
            return M.arena[:, a:a + words]
        xsv = [carve(1024).bitcast(BF16) for _ in range(2)]
        bmv = [carve(512).bitcast(BF16) for _ in range(2)]
        btv = [carve(512).bitcast(BF16).rearrange("p (g t) -> p g t", g=NG) for _ in range(2)]
        ctv = [carve(512).bitcast(BF16).rearrange("p (g t) -> p g t", g=NG) for _ in range(2)]
        dtv = [carve(64) for _ in range(2)]
        hT = carve(2048)
        hTb = carve(1024).bitcast(BF16)
        xdt = carve(1024).bitcast(BF16)
        xw = carve(1024).bitcast(BF16)
        ydir = carve(2048)
        yw = carve(2048)
        small = carve(32 * 8).rearrange("p (a f) -> p a f", a=8)
        acumT = M.rstd[:, 0:1024]
        nacumT = M.sq[:, :, :].rearrange('p a b -> p (a b)').bitcast(F32)
        LTv = [carve(512) for _ in range(2)]
        MTv = [carve(256).bitcast(BF16).rearrange("p (h q) -> p h q", h=4) for _ in range(2)]
        ctm = [carve(256) for _ in range(2)]
        assert off[0] <= M.AW, off[0]
        hflat = M.hbuf[:, :, :].rearrange("p a b -> p (a b)")
        zsv = [hflat[:, i * 2048:(i + 1) * 2048] for i in range(2)]
        yfv = [hflat[:, 4096 + i * 4096:4096 + (i + 1) * 4096].bitcast(F32) for i in range(2)]
        normg = M.actb[:, :, :, :].rearrange("p a b t -> p (a b t)")[:, 0:4096].bitcast(F32)
        lds = [P.dsem(), P.dsem()]
        lds2 = [P.dsem(), P.dsem()]
        ods = P.dsem()
        ngds = P.dsem()
        a_free = _flat([list(M.x_ev.values())])
        e_ng = P.dma('sp', normg, M.ng_dram[jl], ngds, [M.actring.free[0], M.actring.free[1]])
        a_sb, acum_sb, Eq, dec, wst, ss, rstd = [small[:, i, :] for i in range(7)]
        lring, mring, cring = Ring(2), Ring(2), Ring(2)
        slot_rd = [[], []]
        slot2_rd = [_flat([M.h_rd]), _flat([M.h_rd])]
        outs = []
        n = 0
        prev = {'a_rd': [], 'small_rd': [], 'acT_rd': [], 'xdt_rd': [], 'xw_rd': [], 'ydir_rd': [], 'hTb_rd': [], 'yw_rd': []}
        hT_ev = [[] for _ in range(NG)]
        hTb_ev = [[] for _ in range(NG)]
        last_all = []
        for d in range(2):
            order = [('c', i) for i in range(NCC)] + [('x', i) for i in range(NCX)]
            if d == 1:
                order = [('c', i) for i in reversed(range(NCC))] + [('x', i) for i in reversed(range(NCX))]
            for idx, (kind, ci) in enumerate(order):
                if getattr(M, 'dbg_ssd', None) and (d > 0 or idx >= M.dbg_ssd[0]):
                    continue
                first = idx == 0
                t0 = (S if kind == 'c' else 0) + ci * 128
                sl = n % 2
                n += 1
                fr = _flat([slot_rd[sl], a_free, in_wr])
                xs, bm, bt, ct, dt = xsv[sl], bmv[sl], btv[sl], ctv[sl], dtv[sl]
                lev = [P.dma('sp', xs, XS[t0:t0 + 128, :], lds[sl], fr),
                       P.dma('sp', bm, BM[t0:t0 + 128, :], lds[sl], fr),
                       P.dma('sp', bt, BT[:, t0:t0 + 128].rearrange("(g n) t -> n g t", g=NG), lds[sl], fr),
                       P.dma('sp', ct, CT[:, t0:t0 + 128].rearrange("(g n) t -> n g t", g=NG), lds[sl], fr),
                       P.dma('sp', dt, DT[t0:t0 + 128, :], lds[sl], fr)]
                if d == 1:
                    fr2 = _flat([slot2_rd[sl], outs])
                    lev2 = [P.dma('sp', zsv[sl], ZS[t0:t0 + 128, :], lds2[sl], fr2),
                            P.dma('sp', yfv[sl], YF[t0:t0 + 128, :], lds2[sl], fr2)]
                dtd = dt[:, d * 32:(d + 1) * 32]
                rds = []
                eA = op('dve', lambda e, dtd=dtd, d=d: e.tensor_tensor(out=a_sb, in0=dtd, in1=M.e_aneg[:, jl, d * 32:(d + 1) * 32], op=ALU.mult),
                        waits=[lev, prev['a_rd'], M.even_ev])
                b1, f1 = P.bank()
                pB1 = op('pe', lambda e, b1=b1, d=d: e.matmul(P.ps[:, b1, 0:32], lhsT=M.e_tri[:, d, :], rhs=a_sb, start=True, stop=True),
                         waits=[eA, f1, M.even_ev])
                pB2 = op('pe', lambda e, b1=b1: e.matmul(P.ps[:, b1, 32:64], lhsT=M.e_onesf[:], rhs=a_sb, start=True, stop=True))
                prev['a_rd'] = [pB2]
                c1 = op('act', lambda e, b1=b1: e.activation(out=acum_sb, in_=P.ps[:, b1, 0:32], func=AF.Copy),
                        waits=[pB2, prev['small_rd'], prev['acT_rd']])
                c2 = op('act', lambda e, b1=b1: e.activation(out=Eq, in_=P.ps[:, b1, 0:32], func=AF.Exp))
                c3 = op('act', lambda e, b1=b1: e.activation(out=dec, in_=P.ps[:, b1, 32:64], func=AF.Exp))
                c4 = op('dve', lambda e, b1=b1: e.tensor_tensor(out=wst, in0=P.ps[:, b1, 32:64], in1=acum_sb, op=ALU.subtract),
                        waits=[c1, pB2, prev['small_rd']])
                c5 = op('act', lambda e: e.activation(out=wst, in_=wst, func=AF.Exp), waits=[c4])
                c6 = op('dve', lambda e, dtd=dtd: e.tensor_tensor(out=wst, in0=wst, in1=dtd, op=ALU.mult), waits=[c5])
                d1 = []
                for half in range(2):
                    b2, f2 = P.bank()
                    for gg in range(4):
                        g_ = half * 4 + gg
                        pD = op('pe', lambda e, b2=b2, gg=gg, g_=g_: e.transpose(P.ps[0:4, b2, gg * 128:(gg + 1) * 128], acum_sb[:, 4 * g_:4 * g_ + 4], M.e_id[:]),
                                waits=[c1, f2, M.even_ev], sig=(gg == 3))
                    dA = op('act', lambda e, b2=b2, half=half: e.activation(out=acumT[0:4, half * 512:(half + 1) * 512], in_=P.ps[0:4, b2, 0:512], func=AF.Copy),
                            waits=[pD, prev['acT_rd']])
                    dB = op('act', lambda e, b2=b2, half=half: e.activation(out=nacumT[0:4, half * 512:(half + 1) * 512], in_=P.ps[0:4, b2, 0:512], func=AF.Copy, scale=-1.0))
                    P.bank_release(b2, [dB])
                    d1 += [dA, dB]
                P.bank_release(b1, [c2, c3, c4])
                xs3 = xs.rearrange("p (h q) -> p h q", h=NH)
                e1 = op('dve', lambda e, xs3=xs3, dtd=dtd: e.tensor_tensor(
                    out=xdt.rearrange("p (h q) -> p h q", h=NH), in0=xs3, in1=dtd.unsqueeze(2).to_broadcast([128, NH, HP]), op=ALU.mult),
                    waits=[lev, prev['xdt_rd']])
                e2 = op('dve', lambda e, xs3=xs3: e.tensor_tensor(
                    out=xw.rearrange("p (h q) -> p h q", h=NH), in0=xs3, in1=wst.unsqueeze(2).to_broadcast([128, NH, HP]), op=ALU.mult),
                    waits=[c6, prev['xw_rd']])
                y_evs = []
                xdt_rd, xw_rd, small_rd, acT_rd, hTb_rd = [], [], [], [], []
                for g in range(NG if not getattr(M, 'dbg_ssd', None) else M.dbg_ssd[1]):
                    bc, fc = P.bank()
                    pc = op('pe', lambda e, bc=bc, g=g, bt=bt, ct=ct: e.matmul(P.ps[:, bc, 0:128], lhsT=bt[:, g, :], rhs=ct[:, g, :],
                                                                            start=True, stop=True), waits=[fc, lev])
                    bs, fs = P.bank()
                    for hh in range(4):
                        h = 4 * g + hh
                        op('pe', lambda e, bs=bs, hh=hh, g=g: e.matmul(P.ps[:, bs, hh * 128:(hh + 1) * 128], lhsT=M.e_oh[0:4, hh * 128:(hh + 1) * 128],
                                                                      rhs=acumT[0:4, g * 128:(g + 1) * 128], start=True, stop=False), waits=[fs, d1, M.even_ev], sig=False)
                        op('pe', lambda e, bs=bs, hh=hh, g=g: e.matmul(P.ps[:, bs, hh * 128:(hh + 1) * 128], lhsT=nacumT[0:4, g * 128:(g + 1) * 128],
                                                                      rhs=M.e_oh[0:4, hh * 128:(hh + 1) * 128], start=False, stop=False), sig=False)
                        psg = op('pe', lambda e, bs=bs, hh=hh, d=d: e.matmul(P.ps[:, bs, hh * 128:(hh + 1) * 128], lhsT=M.e_id[:],
                                                                       rhs=M.e_mask[:, d, :], start=False, stop=True), sig=(hh == 3))
                    li, lf = lring.get()
                    aL = op('act', lambda e, bs=bs, li=li: e.activation(out=LTv[li], in_=P.ps[:, bs, 0:512], func=AF.Exp), waits=[psg, lf])
                    P.bank_release(bs, aL)
                    mi, mf = mring.get()
                    dM = op('dve', lambda e, bc=bc, li=li, mi=mi: e.tensor_tensor(
                        out=MTv[mi], in0=LTv[li].rearrange("p (h q) -> p h q", h=4),
                        in1=P.ps[:, bc, 0:128].unsqueeze(1).to_broadcast([128, 4, 128]), op=ALU.mult), waits=[aL, pc, mf])
                    P.bank_release(bc, dM)
                    lring.rel(li, dM)
                    by, fy = P.bank()
                    for hh in range(4):
                        h = 4 * g + hh
                        py = op('pe', lambda e, by=by, hh=hh, h=h, mi=mi: e.matmul(
                            P.ps[:, by, hh * HP:(hh + 1) * HP], lhsT=MTv[mi][:, hh, :], rhs=xdt[:, h * HP:(h + 1) * HP],
                            start=True, stop=True), waits=[fy, dM, e1], sig=(hh == 3))
                    mring.rel(mi, py)
                    xdt_rd = [py]
                    if not first:
                        po = op('pe', lambda e, by=by, g=g, ct=ct: e.matmul(P.ps[:, by, 256:512], lhsT=ct[:, g, :], rhs=hTb[:, g * 256:(g + 1) * 256],
                                                                            start=True, stop=True), waits=[hTb_ev[g]])
                        hTb_rd = [po]
                        ki, kf = cring.get()
                        k1 = op('dve', lambda e, by=by, g=g, ki=ki: e.tensor_tensor(
                            out=ctm[ki].rearrange("p (h q) -> p h q", h=4), in0=P.ps[:, by, 256:512].rearrange("p (h q) -> p h q", h=4),
                            in1=Eq[:, 4 * g:4 * g + 4].unsqueeze(2).to_broadcast([128, 4, HP]), op=ALU.mult), waits=[po, c2, kf])
                        k2 = op('dve', lambda e, by=by, g=g, ki=ki: e.tensor_tensor(
                            out=ydir[:, g * 256:(g + 1) * 256], in0=P.ps[:, by, 0:256], in1=ctm[ki], op=ALU.add),
                            waits=[k1, py, prev['ydir_rd']])
                        cring.rel(ki, k2)
                    else:
                        po = py
                        k2 = op('act', lambda e, by=by, g=g: e.activation(out=ydir[:, g * 256:(g + 1) * 256], in_=P.ps[:, by, 0:256], func=AF.Copy),
                                waits=[py, prev['ydir_rd']])
                    P.bank_release(by, k2)
                    y_evs.append(k2)
                    bst, fst = P.bank()
                    pst = op('pe', lambda e, bst=bst, g=g, bm=bm: e.matmul(P.ps[:, bst, 0:256], lhsT=bm[:, g * 128:(g + 1) * 128],
                                                                          rhs=xw[:, g * 256:(g + 1) * 256], start=True, stop=True),
                             waits=[fst, e2, lev])
                    xw_rd = [pst]
                    hg = hT[:, g * 256:(g + 1) * 256]
                    if first:
                        s2 = op('dve', lambda e, bst=bst, hg=hg: e.tensor_copy(out=hg, in_=P.ps[:, bst, 0:256]), waits=[pst, hT_ev[g]])
                    else:
                        s1 = op('dve', lambda e, hg=hg, g=g: e.tensor_tensor(
                            out=hg.rearrange("p (h q) -> p h q", h=4), in0=hg.rearrange("p (h q) -> p h q", h=4),
                            in1=dec[:, 4 * g:4 * g + 4].unsqueeze(2).to_broadcast([128, 4, HP]), op=ALU.mult), waits=[c3, hT_ev[g]])
                        s2 = op('dve', lambda e, bst=bst, hg=hg: e.tensor_tensor(out=hg, in0=hg, in1=P.ps[:, bst, 0:256], op=ALU.add),
                                waits=[s1, pst])
                    P.bank_release(bst, s2)
                    s3 = op('act', lambda e, hg=hg, g=g: e.activation(out=hTb[:, g * 256:(g + 1) * 256], in_=hg, func=AF.Copy),
                            waits=[s2, po, prev['hTb_rd']])
                    hT_ev[g] = [s3]
                    hTb_ev[g] = [s3]
                    small_rd = [s2, k2]
                    acT_rd = [psg]
                prev['xdt_rd'], prev['xw_rd'], prev['small_rd'], prev['acT_rd'], prev['hTb_rd'] = xdt_rd, xw_rd, _flat([small_rd, c6, e1, e2]), acT_rd, hTb_rd
                slot_rd[sl] = _flat([xdt_rd, xw_rd, e1, e2, hTb_rd, pc])
                if d == 0:
                    o = P.dma('sp', YF[t0:t0 + 128, :], ydir, ods, [y_evs])
                    prev['ydir_rd'] = [o]
                    outs.append(o)
                else:
                    zs, yf = zsv[sl], yfv[sl]
                    fin_ev = []
                    sq_ev = []
                    for g in range(NG):
                        gs = slice(g * 256, (g + 1) * 256)
                        ki, kf = cring.get()
                        f1_ = op('dve', lambda e, g=g, gs=gs, ki=ki, xs=xs: e.tensor_tensor(
                            out=ctm[ki].rearrange("p (h q) -> p h q", h=4), in0=xs[:, gs].rearrange("p (h q) -> p h q", h=4),
                            in1=M.e_dsk[:, jl, 4 * g:4 * g + 4].unsqueeze(2).to_broadcast([128, 4, HP]), op=ALU.mult), waits=[kf, lev])
                        f2_ = op('dve', lambda e, gs=gs, yf=yf: e.tensor_tensor(out=yw[:, gs], in0=ydir[:, gs], in1=yf[:, gs], op=ALU.add),
                                 waits=[y_evs[g], lev2, prev['yw_rd']])
                        f3_ = op('dve', lambda e, gs=gs, ki=ki: e.tensor_tensor(out=yw[:, gs], in0=yw[:, gs], in1=ctm[ki], op=ALU.add), waits=[f1_, f2_])
                        cring.rel(ki, f3_)
                        f4_ = op('dve', lambda e, gs=gs, zs=zs: e.tensor_tensor(out=yw[:, gs], in0=yw[:, gs], in1=zs[:, gs], op=ALU.mult), waits=[f3_])
                        ti, tf = M.tmpring.get()
                        f5a = op('act', lambda e, gs=gs, g=g, ti=ti: e.activation(out=M.tmp[:, ti, 0:256], in_=yw[:, gs], func=AF.Square),
                                 waits=[f4_, tf])
                        f5_ = op('dve', lambda e, g=g, ti=ti: e.reduce_sum(out=ss[:, g:g + 1], in_=M.tmp[:, ti, 0:256], axis=mybir.AxisListType.X),
                                 waits=[f5a, prev['small_rd']])
                        M.tmpring.rel(ti, f5_)
                        sq_ev.append(f5_)
                    prev['ydir_rd'] = [f2_]
                    r1 = op('act', lambda e: e.activation(out=rstd[:, 0:8], in_=ss[:, 0:8], func=AF.Sqrt, bias=M.epsb[:, 0:1], scale=1.0 / 256),
                            waits=[sq_ev])
                    r2 = op('dve', lambda e: e.reciprocal(out=rstd[:, 0:8], in_=rstd[:, 0:8]), waits=[r1])
                    for g in range(NG):
                        gs = slice(g * 256, (g + 1) * 256)
                        f6_ = op('dve', lambda e, gs=gs, g=g: e.scalar_tensor_tensor(
                            out=yw[:, gs], in0=yw[:, gs], scalar=rstd[:, g:g + 1], in1=normg[:, gs], op0=ALU.mult, op1=ALU.mult),
                            waits=[r2, e_ng])
                        fin_ev.append(f6_)
                    prev['small_rd'] = _flat([prev['small_rd'], r2, fin_ev])
                    slot2_rd[sl] = [f4_]
                    slot_rd[sl] = _flat([slot_rd[sl], f1_])
                    tl = None
                    for g4 in range(D // 512):
                        bt_, ft = P.bank()
                        for cc in range(4):
                            ch = g4 * 4 + cc
                            tl = op('pe', lambda e, bt_=bt_, cc=cc, ch=ch: e.transpose(
                                P.ps[:, bt_, cc * 128:(cc + 1) * 128], yw[:, ch * 128:(ch + 1) * 128], M.e_id[:]),
                                waits=[ft, fin_ev[ch // 2]], sig=(cc == 3))
                        gi, gfree = M.stage()
                        ecp = op('act', lambda e, bt_=bt_, gi=gi: e.activation(out=M.stg[:, gi, 0:512], in_=P.ps[:, bt_, 0:512], func=AF.Copy),
                                 waits=[tl, gfree])
                        P.bank_release(bt_, ecp)
                        o = M.stage_dma(gi, YT[g4 * 512:(g4 + 1) * 512, t0:t0 + 128].rearrange("(c p) t -> p c t", p=128),
                                        M.stg[:, gi, 0:512].rearrange("p (c t) -> p c t", c=4), ecp, y_free)
                        outs.append(o)
                    prev['yw_rd'] = [tl]
                last_all = _flat([slot_rd[sl], y_evs])
        if getattr(M, 'dbg_ssd', None):
            dd_ = P.dsem()
            allev = [(e_, P.cnt[e_]) for e_ in ENG]
            for nm_, ap_, shp in [('d_small', small.rearrange("p a f -> p (a f)"), [128, 256]), ('d_acumT', acumT[0:4, :], [4, 1024]), ('d_nacumT', nacumT[0:4, :], [4, 1024]),
                                  ('d_LT', LTv[0], [128, 512]), ('d_ydir', ydir, [128, 2048]), ('d_hT', hT, [128, 2048])]:
                t_ = M.nc.dram_tensor(nm_, shp, F32, kind="ExternalOutput").ap()
                outs.append(P.dma('sp', t_, ap_, dd_, allev))
            for nm_, ap_, shp in [('d_MT', MTv[0].rearrange("p h q -> p (h q)"), [128, 512]), ('d_xdt', xdt, [128, 2048]), ('d_xw', xw, [128, 2048])]:
                t_ = M.nc.dram_tensor(nm_, shp, BF16, kind="ExternalOutput").ap()
                outs.append(P.dma('sp', t_, ap_, dd_, allev))
        for k in M.x_ev:
            M.x_ev[k] = _flat([last_all, outs[-8:], prev['yw_rd']])
        M.h_rd = _flat([M.h_rd, slot2_rd[0], slot2_rd[1]])
        M.actring.free[0] = _flat([M.actring.free[0], prev['small_rd']])
        M.actring.free[1] = _flat([M.actring.free[1], prev['small_rd']])
        return outs


EVEN_IN = 2048 + 4096 + 64 + 4096
ODD_IN = 6 * 2048
E_SHAPES = lambda n: {'cw': [n, 128, 96], 'cb': [n, 128, 32], 'dtb': [n, 128, 64], 'alog': [n, 128, 64], 'dsk': [n, 128, 32],
                      'ng': [n, 128, 2048], 'cmw': [n, 128, 16 * 31], 'cmb': [n, 128, 16], 'cmg': [n, 128, 16], 'cmbeta': [n, 128, 16],
                      'tri': [2, 128, 128], 'maskfb': [2, 128, 128], 'ident': [128, 128], 'onehot': [4, 512]}


def build_full(c, layer_types=None, window=True):
    M = Model(c)
    P = M.P
    D, DC, T, S, FF = c.D, c.DC, c.T, c.S, c.FF
    L = c.DEPTH
    lt = layer_types or ['e' if i % 2 == 0 else 'o' for i in range(L)]
    n_even = max(1, sum(1 for t in lt if t == 'e'))
    n_odd = max(1, sum(1 for t in lt if t == 'o'))
    xT = M.inp("xT", [D, T])
    cT = M.inp("cT", [128, DC, 2])
    wmod = M.inp("wmod", [L, D, 9 * D])
    bmodL = M.inp("bmodL", [L, 128, 9 * DC])
    gL = M.inp("gL", [L, 128, 3 * DC])
    fwi = M.inp("ffn_w_in", [L, 2, D, 2 * FF])
    fwo = M.inp("ffn_w_out", [L, 2, FF, D])
    ewi = M.inp("ev_w_in", [n_even, D, EVEN_IN])
    ewo = M.inp("ev_w_out", [n_even, 2 * D, D])
    owi = M.inp("od_w_in", [n_odd, D, ODD_IN])
    owo = M.inp("od_w_out", [n_odd, 2 * D, D])
    qgL = M.inp("qgL", [128, n_odd])
    kgL = M.inp("kgL", [128, n_odd])
    scwL = M.inp("scwL", [n_odd, 128, DC * 3])
    nabias = M.inp("nabias", [n_odd, D // 128, 128, 3200])
    namask = M.inp("namask", [128, 3200])
    eins = {k: M.inp("ei_" + k, v) for k, v in E_SHAPES(n_even).items()}
    yT = M.outp("yT", [D, S // 4 if (window and lt[L - 1] == "o") else S])
    XSc = M.scratch("XSc", [D, T], F32)
    HT = M.scratch("HT", [D, T], BF16)
    YT = M.scratch("YT", [2 * D, T], BF16)
    ZS = M.scratch("ZS", [T, D], BF16)
    XS_ = M.scratch("XS_", [T, D], BF16)
    BM = M.scratch("BM", [T, 1024], BF16)
    BT = M.scratch("BT", [1024, T], BF16)
    CT = M.scratch("CT", [1024, T], BF16)
    DT = M.scratch("DT", [T, 64], F32)
    YF = M.scratch("YF", [T, D], F32)
    QT = M.scratch("QT", [D, T], BF16)
    KT = M.scratch("KT", [D, T], BF16)
    VT = M.scratch("VT", [T, D], BF16)

    M.mod_phase(cT, wmod, bmodL, gL)
    M.mixer_setup()
    M.odd_consts(qgL, kgL, scwL, n_odd)
    M.even_consts(n_even, eins)

    x_wr, ht_wr, ht_rd, yt_wr, yt_rd = [], [], [], [], []
    st2_rd = {'e': [], 'o': []}
    je = jo = 0
    prev_wout = None
    final = []
    win = (lt[L - 1] == 'o') and window
    cMain = c
    if win:
        U = S // 8
        cW = Cfg(D=D, FF=FF, S=S // 2, CL=c.CL, DEPTH=L, NB=4, XB=U, CB=c.CL // 4)
        cF = Cfg(D=D, FF=FF, S=S // 4, CL=0, DEPTH=L, NB=2, XB=U, CB=0)
        TW = cW.T
        XW = M.scratch("XW", [D, TW], F32)
        HTW = M.scratch("HTW", [D, TW], BF16)
        YTW = M.scratch("YTW", [2 * D, TW], BF16)
        QTW = M.scratch("QTW", [D, TW], BF16)
        KTW = M.scratch("KTW", [D, TW], BF16)
        VTW = M.scratch("VTW", [TW, D], BF16)
        nabiasW = M.inp("nabiasW", [1, D // 128, 128, 3200])
        namaskW = M.inp("namaskW", [128, 3200])

        dyn_cache = {}

        def w0f(e):
            if 'w0' not in dyn_cache:
                r = e.partition_id() % 4
                dyn_cache['w0'] = e.snap((r + r // 2) * U, min_val=0, max_val=4 * U)
            return dyn_cache['w0']
        w0f.span = 4 * U

        def ownf(e):
            if 'own' not in dyn_cache:
                r = e.partition_id() % 4
                dyn_cache['own'] = e.snap(((r + 1) // 2) * U, min_val=0, max_val=2 * U)
            return dyn_cache['own']
        ownf.span = 2 * U
    for l in range(L + 1):
        src = xT if l == 0 else XSc
        new_ht, new_xwr = [], []
        M.yt_ld = []
        wpass = win and l == L - 1
        fpass = win and l == L
        if wpass:
            M.c = cW
        if fpass:
            M.c = cF
        cc = M.c
        for b in range(cc.NB):
            if wpass:
                M.load_x(src, b, waits=x_wr, xoff=w0f, coff=S)
            elif fpass:
                M.load_x(XW, b, waits=x_wr, xoff=ownf)
            else:
                M.load_x(src, b, waits=x_wr)
            if l > 0:
                if wpass:
                    M.outproj(l - 1, YT, prev_wout, b, yt_wr, xoff=w0f, coff=S)
                elif fpass:
                    M.outproj(l - 1, YTW, prev_wout, b, yt_wr, xoff=ownf)
                else:
                    M.outproj(l - 1, YT, prev_wout, b, yt_wr)
                h = M.adaln(l - 1, 2)
                M.ffn(l - 1, 2, fwi[l - 1, 1], fwo[l - 1, 1], h)
            if l < L:
                h = M.adaln(l, 0)
                M.ffn(l, 0, fwi[l, 0], fwo[l, 0], h)
                h2 = M.adaln(l, 1)
                new_ht += M.store_h(HTW if wpass else HT, b, h2, ht_rd)
                new_xwr += M.store_x(XW if wpass else XSc, b)
            else:
                final += M.store_x(yT, b, with_ctx=False)
        if l == L:
            break
        x_wr = new_xwr
        ht_wr = new_ht
        yt_rd = list(M.yt_ld)
        if lt[l] == 'e':
            wr = M.even_in(je, HT, ewi[je], ZS, XS_, BM, BT, CT, DT, YT, ht_wr, [st2_rd['e'], yt_rd])
            outs = M.even_ssd(je, ZS, XS_, BM, BT, CT, DT, YF, YT, wr, yt_rd)
            prev_wout = ewo[je]
            je += 1
        elif wpass:
            wr = M.odd_in(jo, HTW, owi[jo], QTW, KTW, VTW, YTW, ht_wr, [])
            outs = M.odd_att(jo, QTW, KTW, VTW, YTW, nabiasW, namaskW, wr, [], bias_jl=0)
            prev_wout = owo[jo]
            jo += 1
        else:
            wr = M.odd_in(jo, HT, owi[jo], QT, KT, VT, YT, ht_wr, [st2_rd['o'], yt_rd])
            outs = M.odd_att(jo, QT, KT, VT, YT, nabias, namask, wr, yt_rd)
            prev_wout = owo[jo]
            jo += 1
        st2_rd[lt[l]] = outs[-4:]
        yt_wr = _flat([wr, outs])
        ht_rd = _flat([M.h_rd])
    M.c = cMain
    M.finish([final])
    M.run()
    return M


def na_tables(rpb, S, GW=64, NR=8, NCOL=16):
    rows = S // GW; NT = S // 128
    col = np.arange(GW)
    c0 = np.clip(col - NCOL // 2, 0, GW - NCOL)
    col_ok = (col[None, :] >= c0[:, None]) & (col[None, :] < c0[:, None] + NCOL)
    col_idx = np.clip(col[None, :] - col[:, None] + NCOL - 1, 0, 2 * NCOL - 2)
    reps = [0, 1, 2, NT - 2, NT - 1]
    n_odd, H = rpb.shape[:2]
    idx_rel = np.zeros((5, 2, 5, 2), np.int64); valid = np.zeros((5, 2, 5, 2), bool)
    for p, j in enumerate(reps):
        a0 = min(max(j - 2, 0), NT - 5)
        for i in range(5):
            for eps in range(2):
                kr = 2 * (a0 + i) + eps
                for dl in range(2):
                    r = 2 * j + dl
                    r0 = min(max(r - NR // 2, 0), rows - NR)
                    ok = (r0 <= kr <= r0 + NR - 1)
                    valid[p, eps, i, dl] = ok
                    idx_rel[p, eps, i, dl] = (kr - r + NR - 1) if ok else 0
    g = rpb[:, :, idx_rel]
    g = g[..., col_idx]
    m = valid[:, :, :, :, None, None] & col_ok[None, None, None, None]
    g = np.where(m[None, None], g, 0.0).astype(np.float32)
    g = g.transpose(0, 1, 3, 7, 2, 4, 5, 6)
    nabias = np.ascontiguousarray(g.reshape(n_odd, H, 128, 5 * 5 * 128))
    mm = np.where(m, 0.0, -30000.0).astype(np.float32).transpose(1, 5, 0, 2, 3, 4)
    namask = np.ascontiguousarray(mm.reshape(128, 5 * 5 * 128))
    return nabias, namask

def fm(v, nch):
    v = np.asarray(v)
    return np.ascontiguousarray(np.moveaxis(v.reshape(v.shape[:-1] + (nch, 128)), -1, -2))

def rep(v):
    v = np.asarray(v, np.float32).reshape(-1)
    return np.ascontiguousarray(np.broadcast_to(v[None, :], (128, v.size)))

def even_tables(ssd_conv_w, ssd_conv_b, dt_bias, a_log, ssd_d, ssd_norm_g, cm_conv_w, cm_conv_b, cm_ln_g, cm_ln_b):
    n = ssd_conv_w.shape[0]
    out = {}
    out['cw'] = np.stack([np.ascontiguousarray(fm(ssd_conv_w[j], 32).transpose(1, 2, 0).reshape(128, 96)) for j in range(n)])
    out['cb'] = np.stack([fm(ssd_conv_b[j], 32) for j in range(n)])
    out['dtb'] = np.stack([rep(dt_bias[j]) for j in range(n)])
    out['alog'] = np.stack([rep(a_log[j]) for j in range(n)])
    out['dsk'] = np.stack([rep(ssd_d[j]) for j in range(n)])
    out['ng'] = np.stack([rep(ssd_norm_g[j]) for j in range(n)])
    out['cmw'] = np.stack([np.ascontiguousarray(fm(cm_conv_w[j], 16).transpose(1, 2, 0).reshape(128, 16 * 31)) for j in range(n)])
    out['cmb'] = np.stack([fm(cm_conv_b[j], 16) for j in range(n)])
    out['cmg'] = np.stack([fm(cm_ln_g[j], 16) for j in range(n)])
    out['cmbeta'] = np.stack([fm(cm_ln_b[j], 16) for j in range(n)])
    i = np.arange(128)
    tri = np.stack([(i[:, None] <= i[None, :]), (i[:, None] >= i[None, :])]).astype(np.float32)
    out['tri'] = tri
    out['maskfb'] = np.stack([np.where(i[None, :] >= i[:, None], 0.0, -30000.0), np.where(i[None, :] <= i[:, None], 0.0, -30000.0)]).astype(np.float32)
    out['ident'] = np.eye(128, dtype=np.float32)
    oh = np.zeros((4, 4, 128), np.float32)
    for h in range(4):
        oh[h, h, :] = 1.0
    out['onehot'] = oh.reshape(4, 4 * 128)
    return {k: np.ascontiguousarray(v, dtype=np.float32) for k, v in out.items()}

def prep_inputs(inp, S, CL, DEPTH, bidx):
    D = 2048; DC = 16
    f32 = np.float32
    x = np.asarray(inp['x'], f32)[bidx]; ctx = np.asarray(inp['ctx'], f32)[bidx]
    m = {}
    m['xT'] = np.ascontiguousarray(np.concatenate([x, ctx], 0).T)
    cv = np.stack([np.asarray(inp['c'], f32)[bidx], np.asarray(inp['c_ctx'], f32)], 0)
    m['cT'] = np.ascontiguousarray(cv.T.reshape(DC, 128, 2).transpose(1, 0, 2))
    m['wmod'] = np.asarray(inp['w_mod'], f32)
    m['bmodL'] = np.ascontiguousarray(np.asarray(inp['b_mod'], f32).reshape(DEPTH, 9 * DC, 128).transpose(0, 2, 1))
    m['gL'] = np.ascontiguousarray(np.asarray(inp['norm_g'], f32).reshape(DEPTH, 3 * DC, 128).transpose(0, 2, 1))
    m['ffn_w_in'] = np.asarray(inp['ffn_w_in'], f32); m['ffn_w_out'] = np.asarray(inp['ffn_w_out'], f32)
    m['ev_w_in'] = np.asarray(inp['ev_w_in'], f32); m['ev_w_out'] = np.asarray(inp['ev_w_out'], f32)
    m['od_w_in'] = np.asarray(inp['od_w_in'], f32); m['od_w_out'] = np.asarray(inp['od_w_out'], f32)
    m['qgL'] = np.ascontiguousarray(np.asarray(inp['na_q_g'], f32).T)
    m['kgL'] = np.ascontiguousarray(np.asarray(inp['na_k_g'], f32).T)
    scw = np.asarray(inp['sc_conv_w'], f32)
    m['scwL'] = np.ascontiguousarray(scw.transpose(0, 2, 1).reshape(-1, DC, 128, 3).transpose(0, 2, 1, 3).reshape(-1, 128, DC * 3))
    nb, nm = na_tables(np.asarray(inp['na_rpb'], f32), S)
    m['nabias'] = nb; m['namask'] = nm
    nbw, nmw = na_tables(np.asarray(inp['na_rpb'], f32)[-1:], S // 2)
    m['nabiasW'] = nbw; m['namaskW'] = nmw
    tabs = even_tables(*[np.asarray(inp[k], f32) for k in ['ssd_conv_w', 'ssd_conv_b', 'ssd_dt_bias', 'ssd_a_log', 'ssd_d', 'ssd_norm_g',
                                                            'cm_conv_w', 'cm_conv_b', 'cm_ln_g', 'cm_ln_b']])
    for k, v in tabs.items():
        m['ei_' + k] = v
    return m


_CACHE = {}


def kernel(**inp):
    S, CL, DEPTH = 4096, 256, 4
    if 'M' not in _CACHE:
        _CACHE['M'] = build_full(Cfg())
    M = _CACHE['M']
    base = prep_inputs(inp, S, CL, DEPTH, 0)
    x = np.asarray(inp['x'], np.float32); ctx = np.asarray(inp['ctx'], np.float32)
    cvec = np.asarray(inp['c'], np.float32); cctx = np.asarray(inp['c_ctx'], np.float32)
    per_b = []
    for b in range(2):
        xT = np.ascontiguousarray(np.concatenate([x[b], ctx[b]], 0).T)
        cv = np.stack([cvec[b], cctx], 0)
        cT = np.ascontiguousarray(cv.T.reshape(16, 128, 2).transpose(1, 0, 2))
        per_b.append({'xT': xT, 'cT': cT})
    maps = []
    for core in range(8):
        m = dict(base)
        m.update(per_b[core // 4])
        maps.append(m)
    res = run_bass_kernel_spmd(M.nc, maps, core_ids=list(range(8)))
    out = np.empty((2, S, 2048), np.float32)
    for core in range(8):
        b, r = core // 4, core % 4
        out[b, r * (S // 4):(r + 1) * (S // 4), :] = res.results[core]['yT'].T
    return out
```
